# Optimizing a Trainium2 kernel written in Bass

```python
import math
import jax, jax.numpy as jnp
from jax import lax
import numpy as np

D_MODEL = 1024
BATCH = 8
SEQ = 2048
DEPTH = 1
DEC_BATCH = 128
DEC_SEQ = 1
PAST_LEN = 16384
PAGE_SIZE = 128

W_LRU = D_MODEL
LRU_BLOCKS = 8
LRU_BLK = W_LRU // LRU_BLOCKS
LRU_C = 8.0
CONV_W = 4
M_HEADS = 4
D_M = D_MODEL
M_HD = D_M // M_HEADS
CHUNK = 64
D_FF = ((8 * D_MODEL + 3 * 256 - 1) // (3 * 256)) * 256
P_DIM = 256
EPS = 1e-6
M_INIT = -1e30
SPLITS = (W_LRU, W_LRU + D_M, W_LRU + 2 * D_M, W_LRU + 2 * D_M + M_HEADS,
          W_LRU + 2 * D_M + 2 * M_HEADS, W_LRU + 2 * D_M + 2 * M_HEADS + D_MODEL)
N_IN = W_LRU + 2 * D_M + 2 * M_HEADS + 2 * D_MODEL

kernel_name = "hawk_mlstm_parallel_gated_decode_step"


def _rmsnorm(x, g):
    xf = x.astype(jnp.float32)
    y = xf * lax.rsqrt(jnp.mean(xf * xf, axis=-1, keepdims=True) + EPS)
    return (y * g.astype(jnp.float32)).astype(x.dtype)


def _causal_conv(x, buf, w, b):
    T = x.shape[1]
    xp = jnp.concatenate([buf.astype(x.dtype), x], axis=1)
    out = b + sum(xp[:, j:j + T] * w[j] for j in range(CONV_W))
    return out, xp[:, T:]


def _rg_lru(x, h0, w_a, b_a, w_x, b_x, lam):
    B, T, W = x.shape
    f32 = jnp.float32
    xb = x.reshape(B, T, LRU_BLOCKS, LRU_BLK)
    r = jax.nn.sigmoid((jnp.einsum('btnc,ncd->btnd', xb, w_a).reshape(B, T, W) + b_a).astype(f32))
    ig = jax.nn.sigmoid((jnp.einsum('btnc,ncd->btnd', xb, w_x).reshape(B, T, W) + b_x).astype(f32))
    log_a = -LRU_C * r * jax.nn.softplus(-lam.astype(f32))
    a = jnp.exp(log_a)
    u = jnp.sqrt(-jnp.expm1(2.0 * log_a)) * ig * x.astype(f32)
    u = u.at[:, 0].add(a[:, 0] * h0.astype(f32))

    def comb(e1, e2):
        a1, b1 = e1
        a2, b2 = e2
        return a1 * a2, a2 * b1 + b2

    _, h = lax.associative_scan(comb, (a, u), axis=1)
    return h.astype(x.dtype), h[:, -1]


def _mlstm_chunk(carry, xs):
    C, n, m = carry
    qc, kc, vc, ic, fc = xs
    L = qc.shape[2]
    b = jnp.cumsum(fc, axis=-1)
    causal = jnp.tril(jnp.ones((L, L), dtype=bool))
    dlog = jnp.where(causal, b[..., :, None] - b[..., None, :] + ic[..., None, :], -jnp.inf)
    m_inter = b + m[..., None]
    m_t = jnp.maximum(m_inter, jnp.max(dlog, axis=-1))
    s = jnp.einsum('bhtd,bhsd->bhts', qc, kc) * jnp.exp(dlog - m_t[..., None])
    sc = jnp.exp(m_inter - m_t)
    num = jnp.einsum('bhts,bhse->bhte', s, vc) + sc[..., None] * jnp.einsum('bhtd,bhde->bhte', qc, C)
    den = jnp.sum(s, axis=-1) + sc * jnp.einsum('bhtd,bhd->bht', qc, n)
    h = num / jnp.maximum(jnp.abs(den), jnp.exp(-m_t))[..., None]
    m_last = m_t[..., -1]
    wk = jnp.exp(b[..., -1:] - b + ic - m_last[..., None])
    dec = jnp.exp(b[..., -1] + m - m_last)
    C_new = dec[..., None, None] * C + jnp.einsum('bhs,bhsd,bhse->bhde', wk, kc, vc)
    n_new = dec[..., None] * n + jnp.einsum('bhs,bhsd->bhd', wk, kc)
    return (C_new, n_new, m_last), h


def _mlstm(xm, o_pre, i_pre, f_pre, conv_buf, C0, n0, m0, cw, cb, w_q, w_k, w_v, g):
    B, T, _ = xm.shape
    f32 = jnp.float32
    xc, new_buf = _causal_conv(xm, conv_buf, cw, cb)
    xc = jax.nn.silu(xc).reshape(B, T, M_HEADS, M_HD)
    xh = xm.reshape(B, T, M_HEADS, M_HD)
    q = jnp.einsum('bthd,hde->bhte', xc, w_q).astype(f32) * (M_HD ** -0.5)
    k = jnp.einsum('bthd,hde->bhte', xc, w_k).astype(f32)
    v = jnp.einsum('bthd,hde->bhte', xh, w_v).astype(f32)
    ig = jnp.swapaxes(i_pre.astype(f32), 1, 2)
    lf = jax.nn.log_sigmoid(jnp.swapaxes(f_pre.astype(f32), 1, 2))
    L = CHUNK if T % CHUNK == 0 else T
    nc = T // L

    def chunks(a):
        return jnp.moveaxis(a.reshape(a.shape[:2] + (nc, L) + a.shape[3:]), 2, 0)

    carry0 = (C0.astype(f32), n0.astype(f32), m0.astype(f32))
    (C, n, m), h = lax.scan(_mlstm_chunk, carry0,
                            (chunks(q), chunks(k), chunks(v), chunks(ig), chunks(lf)))
    h = jnp.moveaxis(h, 0, 2).reshape(B, M_HEADS, T, M_HD)
    h = h * lax.rsqrt(jnp.mean(h * h, axis=-1, keepdims=True) + EPS) * g[:, None, :].astype(f32)
    h = jnp.swapaxes(h, 1, 2).reshape(B, T, D_M)
    out = (jax.nn.sigmoid(o_pre.astype(f32)) * h).astype(xm.dtype)
    return out, new_buf, C.astype(xm.dtype), n.astype(xm.dtype), m.astype(xm.dtype)


def _layer(x, p, st, lw):
    conv_l, h_l, conv_m, C0, n0, m0 = st
    (g_mix, w_in, b_gates, lru_cw, lru_cb, lru_wa, lru_ba, lru_wx, lru_bx, lru_lam,
     m_cw, m_cb, w_q, w_k, w_v, m_g, w_out, g_ffn, w_gate, w_up, w_down,
     g_ple, w_ple_gate, w_ple) = lw
    xn = _rmsnorm(x, g_mix)
    z = xn @ w_in
    x_l, x_m, o_m, i_m, f_m, g_l, g_m = jnp.split(z, SPLITS, axis=-1)
    i_m = i_m + b_gates[:M_HEADS]
    f_m = f_m + b_gates[M_HEADS:]
    xl_c, new_lconv = _causal_conv(x_l, conv_l, lru_cw, lru_cb)
    y_l, new_h = _rg_lru(xl_c, h_l, lru_wa, lru_ba, lru_wx, lru_bx, lru_lam)
    y_m, new_mconv, C1, n1, m1 = _mlstm(x_m, o_m, i_m, f_m, conv_m, C0, n0, m0,
                                        m_cw, m_cb, w_q, w_k, w_v, m_g)
    merged = jax.nn.sigmoid(g_l) * y_l + jax.nn.sigmoid(g_m) * y_m
    x = x + merged @ w_out
    xn = _rmsnorm(x, g_ffn)
    x = x + (jax.nn.silu(xn @ w_gate) * (xn @ w_up)) @ w_down
    gate = jax.nn.sigmoid(_rmsnorm(x, g_ple) @ w_ple_gate)
    x = x + gate * (p @ w_ple)
    return x, (new_lconv, new_h.astype(x.dtype), new_mconv, C1, n1, m1)


def _trunk(x, p, states, weights, final_g):
    new = [[] for _ in range(len(states))]
    for l in range(DEPTH):
        st = tuple(s[l] for s in states)
        lw = tuple(w[l] for w in weights)
        x, ns = _layer(x, p[l], st, lw)
        for lst, a in zip(new, ns):
            lst.append(a)
    return _rmsnorm(x, final_g), tuple(jnp.stack(a) for a in new)


def setup_inputs(seed: int = 0) -> dict:
    key = jax.random.key(seed)
    ks = iter(jax.random.split(key, 48))
    nrm = lambda shape, s: jax.random.normal(next(ks), shape, jnp.float32) * s
    gain = lambda shape: 1.0 + nrm(shape, 0.02)
    u = jax.random.uniform(next(ks), (DEPTH, W_LRU), jnp.float32, 0.9, 0.999)
    s = u ** (1.0 / LRU_C)
    lam = jnp.log(s) - jnp.log1p(-s)
    f_bias = jnp.linspace(3.0, 6.0, M_HEADS, dtype=jnp.float32)[None, :] + nrm((DEPTH, M_HEADS), 0.1)
    b_gates = jnp.concatenate([nrm((DEPTH, M_HEADS), 0.1), f_bias], axis=-1)
    return {
        "x_prompt": nrm((BATCH, SEQ, D_MODEL), 1.0),
        "x_sample": nrm((DEC_BATCH, DEC_SEQ, D_MODEL), 1.0),
        "state_lru_conv": nrm((DEPTH, DEC_BATCH, CONV_W - 1, W_LRU), 1.0),
        "state_lru_h": nrm((DEPTH, DEC_BATCH, W_LRU), 0.5),
        "state_mlstm_conv": nrm((DEPTH, DEC_BATCH, CONV_W - 1, D_M), 1.0),
        "state_mlstm_C": nrm((DEPTH, DEC_BATCH, M_HEADS, M_HD, M_HD), 0.1),
        "state_mlstm_n": nrm((DEPTH, DEC_BATCH, M_HEADS, M_HD), 0.3),
        "state_mlstm_m": jax.random.uniform(next(ks), (DEPTH, DEC_BATCH, M_HEADS), jnp.float32, -1.0, 3.0),
        "p_prompt": nrm((DEPTH, BATCH, SEQ, P_DIM), 1.0),
        "p_sample": nrm((DEPTH, DEC_BATCH, DEC_SEQ, P_DIM), 1.0),
        "norm_mix_g": gain((DEPTH, D_MODEL)),
        "w_in": nrm((DEPTH, D_MODEL, N_IN), D_MODEL ** -0.5),
        "b_gates": b_gates,
        "lru_conv_w": nrm((DEPTH, CONV_W, W_LRU), CONV_W ** -0.5),
        "lru_conv_b": nrm((DEPTH, W_LRU), 0.02),
        "lru_w_a": nrm((DEPTH, LRU_BLOCKS, LRU_BLK, LRU_BLK), LRU_BLK ** -0.5),
        "lru_b_a": nrm((DEPTH, W_LRU), 0.02),
        "lru_w_x": nrm((DEPTH, LRU_BLOCKS, LRU_BLK, LRU_BLK), LRU_BLK ** -0.5),
        "lru_b_x": nrm((DEPTH, W_LRU), 0.02),
        "lru_lambda": lam,
        "mlstm_conv_w": nrm((DEPTH, CONV_W, D_M), CONV_W ** -0.5),
        "mlstm_conv_b": nrm((DEPTH, D_M), 0.02),
        "w_q": nrm((DEPTH, M_HEADS, M_HD, M_HD), M_HD ** -0.5),
        "w_k": nrm((DEPTH, M_HEADS, M_HD, M_HD), M_HD ** -0.5),
        "w_v": nrm((DEPTH, M_HEADS, M_HD, M_HD), M_HD ** -0.5),
        "mlstm_norm_g": gain((DEPTH, M_HEADS, M_HD)),
        "w_out": nrm((DEPTH, D_MODEL, D_MODEL), D_MODEL ** -0.5),
        "norm_ffn_g": gain((DEPTH, D_MODEL)),
        "w_ffn_gate": nrm((DEPTH, D_MODEL, D_FF), D_MODEL ** -0.5),
        "w_ffn_up": nrm((DEPTH, D_MODEL, D_FF), D_MODEL ** -0.5),
        "w_ffn_down": nrm((DEPTH, D_FF, D_MODEL), D_FF ** -0.5),
        "norm_ple_g": gain((DEPTH, D_MODEL)),
        "w_ple_gate": nrm((DEPTH, D_MODEL, D_MODEL), D_MODEL ** -0.5),
        "w_ple": nrm((DEPTH, P_DIM, D_MODEL), P_DIM ** -0.5),
        "final_norm_g": gain((D_MODEL,)),
    }


def reference(x_prompt, x_sample, state_lru_conv, state_lru_h, state_mlstm_conv,
              state_mlstm_C, state_mlstm_n, state_mlstm_m, p_prompt, p_sample,
              norm_mix_g, w_in, b_gates, lru_conv_w, lru_conv_b, lru_w_a, lru_b_a,
              lru_w_x, lru_b_x, lru_lambda, mlstm_conv_w, mlstm_conv_b, w_q, w_k, w_v,
              mlstm_norm_g, w_out, norm_ffn_g, w_ffn_gate, w_ffn_up, w_ffn_down,
              norm_ple_g, w_ple_gate, w_ple, final_norm_g):
    weights = (norm_mix_g, w_in, b_gates, lru_conv_w, lru_conv_b, lru_w_a, lru_b_a,
               lru_w_x, lru_b_x, lru_lambda, mlstm_conv_w, mlstm_conv_b, w_q, w_k, w_v,
               mlstm_norm_g, w_out, norm_ffn_g, w_ffn_gate, w_ffn_up, w_ffn_down,
               norm_ple_g, w_ple_gate, w_ple)
    B = x_prompt.shape[0]
    dt = x_prompt.dtype
    prompt_states = (
        jnp.zeros((DEPTH, B, CONV_W - 1, W_LRU), dt),
        jnp.zeros((DEPTH, B, W_LRU), dt),
        jnp.zeros((DEPTH, B, CONV_W - 1, D_M), dt),
        jnp.zeros((DEPTH, B, M_HEADS, M_HD, M_HD), jnp.float32),
        jnp.zeros((DEPTH, B, M_HEADS, M_HD), jnp.float32),
        jnp.full((DEPTH, B, M_HEADS), M_INIT, jnp.float32),
    )
    sample_states = (state_lru_conv, state_lru_h, state_mlstm_conv,
                     state_mlstm_C, state_mlstm_n, state_mlstm_m)
    y_prompt, ps = _trunk(x_prompt, p_prompt, prompt_states, weights, final_norm_g)
    y_sample, ss = _trunk(x_sample, p_sample, sample_states, weights, final_norm_g)
    p_lru_conv, p_lru_h, p_m_conv, p_C, p_n, p_m = ps
    s_lru_conv, s_lru_h, s_m_conv, s_C, s_n, s_m = ss
    return (y_prompt, y_sample, p_lru_conv, p_lru_h, p_m_conv, p_C, p_n, p_m,
            s_lru_conv, s_lru_h, s_m_conv, s_C, s_n, s_m)
```

```python
from contextlib import ExitStack
import numpy as np
import concourse.bass as bass
import concourse.mybir as mybir
from concourse.bass_utils import run_bass_kernel_spmd

F32 = mybir.dt.float32
BF16 = mybir.dt.bfloat16
AF = mybir.ActivationFunctionType
ALU = mybir.AluOpType
AX = mybir.AxisListType

T = 2048
NS = 16
TT = T + NS
D = 1024
NIN = 5128
DFF = 2816
PD = 256
EPS = 1e-6
NEG = -1.0e30
ENGS = ("pe", "act", "dve", "pool", "sp")
BANK = 3000
SCHEDULE = True
MARKS = []
KNOB = {"pbanks": (0, 8), "prio": "cp", "lat": 400.0, "asq": "act", "ix": True, "ixeng": "dve", "nbev": "dve", "tset_aware": True, "tpen": 2500.0, "sqrt_explog": True}


def _esz(dt):
    return 2 if dt == BF16 else 4


def region(ap):
    steps = ap.ap
    esz = _esz(ap.dtype)
    name = ap.tensor.name
    if str(ap.space) == "DRAM" or "DRAM" in str(ap.space):
        ext = sum((c - 1) * abs(s) for s, c in steps) + 1
        return (name, 0, 1, ap.offset * esz, (ap.offset + ext) * esz)
    pstep, pcount = steps[0]
    if pstep == 0:
        p0, f0 = 0, ap.offset
        pcount = 128
    else:
        p0, f0 = ap.offset // pstep, ap.offset % pstep
    ext = sum((c - 1) * abs(s) for s, c in steps[1:]) + 1
    if name == "PS":
        return (name, 0, 128, (f0 * esz) // 2048 * 2048, ((f0 + ext) * esz + 2047) // 2048 * 2048)
    return (name, p0, p0 + pcount, f0 * esz, (f0 + ext) * esz)


class Op:
    __slots__ = ("eng", "fn", "deps", "seq", "needed", "ms", "stream", "scount", "waits", "idx", "cost", "xfer", "tset")


class Sched:
    def __init__(self, nc):
        self.nc = nc
        self.ops = []
        self.eng_ops = {e: [] for e in ENGS}
        self.acc = {}
        self.streams = {}
        self.out_dmas = []
        self.last_stream = {}

    def add(self, eng, fn, reads, writes, stream=None, is_out=False, cost=300.0, xfer=0.0):
        op = Op()
        op.idx = len(self.ops)
        op.cost = cost
        op.xfer = xfer
        op.tset = None
        op.eng = eng
        op.fn = fn
        op.needed = False
        op.stream = stream
        op.seq = len(self.eng_ops[eng])
        deps = set()
        key = eng if stream is None else ("dma", len(self.ops))
        if stream is not None:
            prev = self.last_stream.get(stream)
            if prev is not None:
                deps.add(prev)
            self.last_stream[stream] = op
        writes = list(writes) + [ap for ap in reads if ap.tensor.name == "PS"]
        reads = [ap for ap in reads if ap.tensor.name != "PS"]
        for ap in reads:
            name, p0, p1, b0, b1 = region(ap)
            recs = self.acc.setdefault(name, [])
            for r in recs:
                if r[4] and r[0] < p1 and p0 < r[1] and r[2] < b1 and b0 < r[3]:
                    deps.add(r[5])
            for r in recs:
                if (not r[4]) and r[6] == key and r[0] == p0 and r[1] == p1 and r[2] == b0 and r[3] == b1:
                    r[5].append(op)
                    break
            else:
                recs.append([p0, p1, b0, b1, False, [op], key])
        for ap in writes:
            name, p0, p1, b0, b1 = region(ap)
            recs = self.acc.setdefault(name, [])
            keep = []
            for r in recs:
                if r[0] < p1 and p0 < r[1] and r[2] < b1 and b0 < r[3]:
                    if r[4]:
                        if r[5] is not op:
                            deps.add(r[5])
                    else:
                        deps.update(r[5])
                    if p0 <= r[0] and r[1] <= p1 and b0 <= r[2] and r[3] <= b1:
                        continue
                keep.append(r)
            keep.append([p0, p1, b0, b1, True, op, key])
            self.acc[name] = keep
        deps.discard(op)
        op.deps = deps
        if stream is not None:
            self.streams[stream] = self.streams.get(stream, 0) + 16
            op.scount = self.streams[stream]
            if is_out:
                self.out_dmas.append(op)
        self.ops.append(op)
        self.eng_ops[eng].append(op)
        return op

    def schedule(self, window=1000):
        ops = self.ops
        n = len(ops)
        succs = [[] for _ in range(n)]
        npred = [0] * n
        for op in ops:
            npred[op.idx] = len(op.deps)
            for d in op.deps:
                succs[d.idx].append(op)
        rt = [0.0] * n
        fin = [0.0] * n
        lat = KNOB.get("lat", 0.0)
        cp = [0.0] * n
        if KNOB.get("prio", "idx") == "cp":
            for op in reversed(ops):
                m_ = 0.0
                for s in succs[op.idx]:
                    if cp[s.idx] > m_:
                        m_ = cp[s.idx]
                cp[op.idx] = m_ + op.cost + (op.xfer if op.stream is not None else 0.0)
        use_cp = KNOB.get("prio", "idx") == "cp"
        self.start_t = [0.0] * n
        self.fin_t = fin
        done = [False] * n
        avail = {e: [] for e in ENGS}
        for op in ops:
            if npred[op.idx] == 0:
                avail[op.eng].append(op)
        efree = {e: 0.0 for e in ENGS}
        dma_free = 0.0
        cur_set = None
        aware = KNOB.get("tset_aware", True)
        nsw = 0
        new_eng = {e: [] for e in ENGS}
        new_ops = []
        oldest = 0
        cnt = 0
        while cnt < n:
            while oldest < n and done[oldest]:
                oldest += 1
            lim = oldest + window
            best = None
            for e in ENGS:
                fe = efree[e]
                for op in avail[e]:
                    if op.idx > lim:
                        continue
                    r = rt[op.idx]
                    pk = -cp[op.idx] if use_cp else op.idx
                    pen = KNOB.get("tpen", ACT_SWITCH_NS) if (aware and op.tset is not None and op.tset != cur_set) else 0.0
                    key = (fe + pen, 0, pk) if r <= fe else (r + pen, 1, pk)
                    if best is None or key < best[0]:
                        best = (key, op)
            if best is None:
                cands = [op for e in ENGS for op in avail[e]]
                op = min(cands, key=lambda o: o.idx)
                best = ((max(rt[op.idx], efree[op.eng]), 0, op.idx), op)
            op = best[1]
            st = max(rt[op.idx], efree[op.eng])
            avail[op.eng].remove(op)
            if op.stream is not None:
                efree[op.eng] = st + op.cost
                ts = max(st + op.cost, dma_free)
                dma_free = ts + op.xfer * 0.6
                f = ts + 2000.0 + op.xfer
            else:
                f = st + op.cost
                if op.tset is not None and op.tset != cur_set:
                    f += ACT_SWITCH_NS
                    cur_set = op.tset
                    nsw += 1
                efree[op.eng] = f
            fin[op.idx] = f
            self.start_t[op.idx] = st
            done[op.idx] = True
            cnt += 1
            op.seq = len(new_eng[op.eng])
            new_eng[op.eng].append(op)
            new_ops.append(op)
            for s in succs[op.idx]:
                fl = f + (lat if s.eng != op.eng else 0.0)
                if fl > rt[s.idx]:
                    rt[s.idx] = fl
                npred[s.idx] -= 1
                if npred[s.idx] == 0:
                    avail[s.eng].append(s)
        self.ops = new_ops
        self.eng_ops = new_eng
        self.est_ns = max(fin) if n else 0.0
        self.n_switch = nsw

    def emit(self):
        nc = self.nc
        if SCHEDULE:
            self.schedule()
        fin = Op()
        fin.eng = "sp"
        fin.fn = None
        fin.needed = False
        fin.stream = None
        fin.seq = len(self.eng_ops["sp"])
        fin.deps = set(self.out_dmas)
        self.ops.append(fin)
        self.eng_ops["sp"].append(fin)
        wm = {e: {} for e in ENGS}
        for op in self.ops:
            e = op.eng
            best = {}
            waits = []
            for d in op.deps:
                if d.stream is not None:
                    k = ("s", d.stream)
                    if wm[e].get(k, 0) >= d.scount:
                        continue
                    if k not in best or best[k].scount < d.scount:
                        best[k] = d
                else:
                    if d.eng == e and e == "pe":
                        continue
                    if wm[e].get(d.eng, -1) >= d.seq:
                        continue
                    if d.eng not in best or best[d.eng].seq < d.seq:
                        best[d.eng] = d
            for k, d in best.items():
                if d.stream is not None:
                    wm[e][k] = d.scount
                else:
                    wm[e][k] = d.seq
                    d.needed = True
                waits.append(d)
            op.waits = waits
        nbanks = {}
        for e in ENGS:
            c = 0
            for op in self.eng_ops[e]:
                if op.needed:
                    op.ms = c
                    c += 1
            nbanks[e] = c // BANK + 1
        with ExitStack() as es:
            esem = {e: [es.enter_context(nc.semaphore(f"s_{e}{i}")) for i in range(nbanks[e])] for e in ENGS}
            ssem = {s: es.enter_context(nc.semaphore(f"d_{s}")) for s in self.streams}
            block = es.enter_context(nc.Block())

            def make(engname):
                def body(eng):
                    for op in self.eng_ops[engname]:
                        for d in op.waits:
                            if d.stream is not None:
                                eng.wait_ge(ssem[d.stream], d.scount)
                            else:
                                eng.wait_ge(esem[d.eng][d.ms // BANK], d.ms % BANK + 1)
                        if op.fn is None:
                            continue
                        ins = op.fn(eng)
                        if op.stream is not None:
                            ins.then_inc(ssem[op.stream], 16)
                        elif op.needed:
                            ins.then_inc(esem[engname][op.ms // BANK], 1)
                return body

            block.tensor(make("pe"))
            block.scalar(make("act"))
            block.vector(make("dve"))
            block.gpsimd(make("pool"))
            block.sync(make("sp"))


class Arena:
    def __init__(self, t):
        self.t = t
        self.top = 0

    def alloc(self, shape, dtype=F32, parts=128, at=None):
        n = int(np.prod(shape))
        nb = (n * _esz(dtype) + 63) // 64 * 64
        if at is None:
            at = self.top
            self.top += nb
            assert self.top <= self.t.shape[1] * 4, ("arena overflow", self.top)
        v = self.t[0:parts, at // 4:(at + nb) // 4]
        if dtype != F32:
            v = v.bitcast(dtype)
        v = v[:, 0:n]
        if len(shape) == 2:
            v = v.rearrange("p (a b) -> p a b", b=shape[1])
        elif len(shape) == 3:
            v = v.rearrange("p (a b c) -> p a b c", b=shape[1], c=shape[2])
        return v, at


class _Stop(Exception):
    pass


LEVEL = 99


ACT_TSET = {AF.Exp: "E", AF.Ln: "E", AF.Square: "E", AF.Abs: "E", AF.Sigmoid: "S", AF.Silu: "U", AF.Sqrt: "Q"}
ACT_SWITCH_NS = 1300.0


def build():
    nc = bass.Bass("TRN2", target_bir_lowering=False)

    def din(name, shape):
        return nc.dram_tensor(name, list(shape), F32, kind="ExternalInput").ap()

    def dout(name, shape):
        return nc.dram_tensor(name, list(shape), F32, kind="ExternalOutput").ap()

    xp = din("xp", (T, D)); xs = din("xs", (NS, D))
    st_lconv = din("st_lconv", (NS, 3, D)); st_lh = din("st_lh", (NS, D))
    st_mconv = din("st_mconv", (NS, 3, D)); st_C = din("st_C", (NS, 4, 256, 256))
    st_n = din("st_n", (NS, 4, 256)); st_m = din("st_m", (NS, 4))
    pp = din("pp", (T, PD)); psm = din("psm", (NS, PD))
    g_mix = din("norm_mix_g", (D,)); w_in = din("w_in", (D, NIN)); b_gates = din("b_gates", (8,))
    lru_cw = din("lru_conv_w", (4, D)); lru_cb = din("lru_conv_b", (D,))
    lru_wa = din("lru_w_a", (8, 128, 128)); lru_ba = din("lru_b_a", (D,))
    lru_wx = din("lru_w_x", (8, 128, 128)); lru_bx = din("lru_b_x", (D,))
    lru_lam = din("lru_lambda", (D,))
    m_cw = din("mlstm_conv_w", (4, D)); m_cb = din("mlstm_conv_b", (D,))
    w_q = din("w_q", (4, 256, 256)); w_k = din("w_k", (4, 256, 256)); w_v = din("w_v", (4, 256, 256))
    m_g = din("mlstm_norm_g", (4, 256))
    w_out = din("w_out", (D, D)); g_ffn = din("norm_ffn_g", (D,))
    w_gate = din("w_ffn_gate", (D, DFF)); w_up = din("w_ffn_up", (D, DFF)); w_down = din("w_ffn_down", (DFF, D))
    g_ple = din("norm_ple_g", (D,)); w_pleg = din("w_ple_gate", (D, D)); w_ple = din("w_ple", (PD, D))
    g_fin = din("final_norm_g", (D,))

    y_p = dout("y_p", (T, D)); y_s = dout("y_s", (NS, D))
    o_plconv = dout("o_plconv", (3, D)); o_plh = dout("o_plh", (1, D)); o_pmconv = dout("o_pmconv", (3, D))
    o_pC = dout("o_pC", (4, 256, 256)); o_pn = dout("o_pn", (4, 256)); o_pm = dout("o_pm", (1, 4))
    o_slconv = dout("o_slconv", (NS, 3, D)); o_slh = dout("o_slh", (NS, D)); o_smconv = dout("o_smconv", (NS, 3, D))
    o_sC = dout("o_sC", (NS, 4, 256, 256)); o_sn = dout("o_sn", (NS, 4, 256)); o_sm = dout("o_sm", (NS, 4))

    es = ExitStack()
    ARENA_BYTES = 207 * 1024
    At = es.enter_context(nc.sbuf_tensor("A", [128, ARENA_BYTES // 4], F32))
    PS = es.enter_context(nc.psum_tensor("PS", [128, 8, 512], F32))
    S = Sched(nc)
    AR = Arena(At)

    def aps(*xs_):
        return [x for x in xs_ if x is not None and not isinstance(x, (int, float))]

    def nfree(ap):
        return int(np.prod(ap.shape[1:]))

    def MM(out, lhsT, rhs, start, stop):
        rd = [lhsT, rhs] + ([] if start else [out])
        nf_ = nfree(rhs)
        c = (60.0 + 0.30 * nf_ * 4) if rhs.dtype == F32 else max(30.0, 30.0 + 0.42 * nf_, 110.0 if nf_ >= 64 else 0.0)
        return S.add("pe", lambda e: e.matmul(out, lhsT=lhsT, rhs=rhs, start=start, stop=stop), rd, [out], cost=c)

    def TR(out, in_, ident):
        return S.add("pe", lambda e: e.transpose(out, in_, ident), [in_, ident], [out], cost=200.0)

    def ACT(out, in_, func, bias=None, scale=None, accum_out=None):
        kw = {}
        if bias is not None:
            kw["bias"] = bias
        if scale is not None:
            kw["scale"] = scale
        if accum_out is not None:
            kw["accum_out"] = accum_out
        op_ = S.add("act", lambda e: e.activation(out, in_, func, **kw),
                    aps(in_, bias, scale), aps(out, accum_out), cost=220.0 + 0.75 * nfree(out))
        op_.tset = ACT_TSET.get(func)
        return op_

    def ENG(name):
        return {"dve": "dve", "pool": "pool"}[name]

    def TTo(eng, out, in0, in1, op):
        return S.add(eng, lambda e: e.tensor_tensor(out, in0, in1, op), [in0, in1], [out], cost=((100.0 + 1.0 * nfree(out)) if eng != "pool" else (500.0 + 2.0 * nfree(out))))

    def STT(eng, out, in0, scalar, in1, op0, op1):
        return S.add(eng, lambda e: e.scalar_tensor_tensor(out, in0, scalar, in1, op0, op1),
                     aps(in0, scalar, in1), [out], cost=((100.0 + 1.0 * nfree(out)) if eng != "pool" else (500.0 + 2.0 * nfree(out))))

    def TS(eng, out, in0, s1, s2, op0, op1=None):
        if op1 is None:
            return S.add(eng, lambda e: e.tensor_scalar(out, in0, s1, None, op0), aps(in0, s1), [out], cost=((100.0 + 1.0 * nfree(out)) if eng != "pool" else (500.0 + 2.0 * nfree(out))))
        return S.add(eng, lambda e: e.tensor_scalar(out, in0, s1, s2, op0, op1), aps(in0, s1, s2), [out], cost=((100.0 + 1.0 * nfree(out)) if eng != "pool" else (500.0 + 2.0 * nfree(out))))

    def CP(eng, out, in_):
        if eng == "act":
            return S.add("act", lambda e: e.copy(out, in_), [in_], [out], cost=220.0 + 0.75 * nfree(out))
        return S.add(eng, lambda e: e.tensor_copy(out, in_), [in_], [out], cost=((100.0 + 1.0 * nfree(out)) if eng != "pool" else (500.0 + 2.0 * nfree(out))))

    def MSET(eng, ap, val):
        return S.add(eng, lambda e: e.memset(ap, val), [], [ap], cost=((100.0 + 1.0 * nfree(ap)) if eng != "pool" else (500.0 + 2.0 * nfree(ap))))

    def SCAN(out, d0, d1, init, op0, op1):
        return S.add("dve", lambda e: e.tensor_tensor_scan(out, d0, d1, init, op0, op1), aps(d0, d1, init), [out],
                     cost=100.0 + 2.1 * nfree(out))

    def RECIP(out, in_):
        return S.add("dve", lambda e: e.reciprocal(out, in_), [in_], [out], cost=100.0 + 8.0 * nfree(out))

    def DMA(q, out, in_, stream, is_out=False, slow=False, cast=False):
        kw = {}
        if slow:
            kw["allow_slow_non_contiguous"] = True
        if cast:
            kw["max_dma_last_dim"] = 4096
        nbytes = int(np.prod(in_.shape)) * 4
        per_b = 0.008 if cast else (0.05 if slow else 0.004)
        return S.add(q, lambda e: e.dma_start(out=out, in_=in_, **kw), [in_], [out], stream=stream, is_out=is_out,
                     cost=(1000.0 if q == "pool" else 100.0), xfer=nbytes * per_b)

    def psb(bank, parts=128, n=512):
        return PS[0:parts, bank, 0:n]

    ident, _ = AR.alloc((128,), F32)
    maskneg, _ = AR.alloc((128,), F32)
    ones_bf, _ = AR.alloc((128,), BF16)
    esel, _ = AR.alloc((4, 128), F32, parts=4)
    cols, _ = AR.alloc((16, 8), F32)
    G1C, G2C, G3C, GFC, LCB, LBA, LBX, CCOL, MCB, MGC = [cols[:, i, :] for i in range(10)]
    LAMC = cols[:, 10, :]
    C2COL = cols[:, 11, :]
    lcw, _ = AR.alloc((4, 8), F32)
    mcw, _ = AR.alloc((4, 8), F32)
    bgc, _ = AR.alloc((2,), F32, parts=4)
    wa_bf, _ = AR.alloc((8, 128), BF16)
    wx_bf, _ = AR.alloc((8, 128), BF16)
    wq_bf, _ = AR.alloc((8, 256), BF16)
    wk_bf, _ = AR.alloc((8, 256), BF16)
    wv_bf, _ = AR.alloc((8, 256), BF16)
    xnS, _ = AR.alloc((8, NS), BF16)
    mrgS, _ = AR.alloc((8, NS), BF16)
    zT, zT_off = AR.alloc((41, NS), F32)
    zmap = []

    MSET("pool", ident, 1.0)
    S.add("pool", lambda e: e.affine_select(ident, ident, [[1, 128]], ALU.is_equal, 0.0, base=0, channel_multiplier=-1),
          [ident], [ident])
    idrep, _ = AR.alloc((NS, NS), F32)
    MSET("pool", idrep, 1.0)
    S.add("pool", lambda e: e.affine_select(idrep.rearrange("p a b -> p (a b)"), idrep.rearrange("p a b -> p (a b)"),
                                            [[1, NS], [-1, NS]], ALU.is_equal, 0.0, base=0, channel_multiplier=0),
          [idrep], [idrep])
    MSET("pool", maskneg, 0.0)
    S.add("pool", lambda e: e.affine_select(maskneg, maskneg, [[1, 128]], ALU.is_ge, -1.0e4, base=0, channel_multiplier=-1),
          [maskneg], [maskneg])
    MSET("dve", ones_bf, 1.0)
    for h in range(4):
        CP("dve", esel[:, h, :], ident[0:4, h:h + 1].broadcast_to([4, 128]))

    stg, _ = AR.alloc((128,), F32, parts=88, at=zT_off)
    stg2, _ = AR.alloc((128,), F32, parts=64, at=zT_off + 512)
    MSET("dve", stg, 0.0)
    for slot_, src_ in ((0, g_mix), (1, g_ffn), (2, g_ple), (3, g_fin), (4, lru_cb), (5, lru_ba), (6, lru_bx), (10, lru_lam), (8, m_cb)):
        DMA("sp", stg[8 * slot_:8 * slot_ + 8, :], src_.rearrange("(c p) -> c p", p=128), f"c{slot_}")
    DMA("sp", stg[72:80, :], m_g.rearrange("h (j p) -> (h j) p", p=128), "c9")
    DMA("sp", stg2[0:32, :], lru_cw.rearrange("j (c p) -> (j c) p", p=128), "c11")
    DMA("sp", stg2[32:64, :], m_cw.rearrange("j (c p) -> (j c) p", p=128), "c13")
    TR(PS[:, 0, 0:88], stg, ident[0:88, 0:88])
    CP("dve", cols[:, 0:11, :], PS[:, 0, 0:88].rearrange("p (s c) -> p s c", c=8))
    TR(PS[:, 1, 0:64], stg2, ident[0:64, 0:64])
    CP("dve", lcw, PS[:, 1, 0:32].rearrange("p (j c) -> p j c", c=8))
    CP("dve", mcw, PS[:, 1, 32:64].rearrange("p (j c) -> p j c", c=8))
    DMA("sp", bgc, b_gates.rearrange("(g h) -> h g", h=4), "c12", slow=True)
    DMA("pool", wa_bf, lru_wa.rearrange("n c d -> c n d"), "w_a", cast=True)
    DMA("pool", wx_bf, lru_wx.rearrange("n c d -> c n d"), "w_x", cast=True)
    DMA("pool", wq_bf, w_q.rearrange("h (j p) e -> p (h j) e", p=128), "w_q", cast=True)
    DMA("pool", wk_bf, w_k.rearrange("h (j p) e -> p (h j) e", p=128), "w_k", cast=True)
    DMA("pool", wv_bf, w_v.rearrange("h (j p) e -> p (h j) e", p=128), "w_v", cast=True)
    ACT(CCOL, LAMC, AF.Exp, scale=-1.0)
    ACT(CCOL, CCOL, AF.Ln, bias=1.0)
    TS("dve", CCOL, CCOL, -8.0, None, ALU.mult)
    TS("dve", C2COL, CCOL, 2.0, None, ALU.mult)

    WS_BYTES = 3 * 5632
    wreg, wreg_off = AR.alloc((WS_BYTES // 2,), BF16)
    wctr = [0]
    wcfg = {"n": 3}

    def wload(src2d, c0, w, kparts=8):
        n = wcfg["n"]
        sz = WS_BYTES // 2 // n
        assert kparts * w <= sz, (kparts, w, sz)
        s = wctr[0] % n
        wctr[0] += 1
        dst = wreg[:, s * sz:s * sz + kparts * w].rearrange("p (a b) -> p a b", b=w)
        DMA("pool", dst, src2d[:, c0:c0 + w].rearrange("(k p) n -> p k n", p=128), f"ws{n}_{s}", cast=True)
        return dst

    bctr = [0]

    def nbank(lo=0, hi=8):
        b = lo + bctr[0] % (hi - lo)
        bctr[0] += 1
        return b

    st1 = {}

    def alloc_stage1(n=2):
        st1["n"] = n
        st1["xin"], _ = AR.alloc((n, D), F32)
        st1["xsc"], _ = AR.alloc((n, D), F32)
        st1["nstat"], _ = AR.alloc((n, 4), F32)
        st1["sq"], _ = AR.alloc((D,), BF16)

    mk0 = AR.top
    alloc_stage1()

    def load_tiles(mode, dst, dstS, tiles):
        for tt in tiles:
            npart = 128 if tt < 16 else NS
            src_ = xp[tt * 128:(tt + 1) * 128, :] if tt < 16 else xs
            sl = tt % st1["n"]
            xin, xsc, nstat, sq_scr = st1["xin"], st1["xsc"], st1["nstat"], st1["sq"]
            xi = xin[0:npart, sl, :]
            DMA("sp", xi, src_, f"xin{sl}")
            tin = xi
            if mode == "norm":
                ss = nstat[0:npart, sl, 0:1]
                rs = nstat[0:npart, sl, 1:2]
                ACT(sq_scr[0:npart, :], xi, AF.Square, accum_out=ss)
                ACT(rs, ss, AF.Ln, bias=EPS, scale=1.0 / D)
                ACT(rs, rs, AF.Exp, scale=-0.5)
                tin = xsc[0:npart, sl, :]
                TS("dve", tin, xi, rs, None, ALU.mult)
            for half in range(2):
                b = nbank()
                for c in range(4):
                    kc = half * 4 + c
                    TR(PS[:, b, c * 128:c * 128 + npart], tin[:, kc * 128:(kc + 1) * 128], ident[0:npart, 0:npart])
                pv = PS[:, b, :].rearrange("p (c t) -> p c t", t=128)[:, :, 0:npart]
                if tt < 16:
                    dv = dst[:, half * 4:half * 4 + 4, tt * 128:tt * 128 + npart]
                else:
                    dv = dstS[:, half * 4:half * 4 + 4, :]
                if mode == "norm":
                    TTo("dve", dv, pv, G1C[:, half * 4:half * 4 + 4].unsqueeze(2).broadcast_to([128, 4, npart]), ALU.mult)
                else:
                    CP("act", dv, pv)

    load_tiles("norm", None, xnS, [16])

    def phase_S():
        AR.top = PB
        MARKS.append(("S", len(S.ops)))
        zs, _ = AR.alloc((NIN,), F32, parts=NS)
        for zi_, (zc_, c0_, w_) in enumerate(zmap):
            b = nbank()
            if w_ == 128:
                TR(PS[0:NS, b, 0:128], zT[:, zc_, :], ident)
            else:
                TR(PS[0:NS, b, 0:w_], zT[0:w_, zc_, :], ident[0:w_, 0:w_])
            CP("act" if zi_ % 2 == 0 else "dve", zs[:, c0_:c0_ + w_], PS[0:NS, b, 0:w_])

        bcn = [0]

        MARKS.append(("S_chain", len(S.ops)))
        def bcload(src_flat, n, t=None):
            if t is None:
                t, _ = AR.alloc((n,), F32, parts=NS)
            bcn[0] += 1
            DMA("sp", t, src_flat.partition_broadcast(NS), f"b{bcn[0]}")
            return t

        lam_b = bcload(lru_lam, D)
        bg_b = bcload(b_gates, 8)
        cwt, _ = AR.alloc((3, D), F32, parts=NS)
        rot = [0]

        def prow(src_flat):
            rot[0] += 1
            return bcload(src_flat, D, t=cwt[:, rot[0] % 3, :])

        cs_l, _ = AR.alloc((3, D), F32, parts=NS)
        cs_m = cs_l
        h0, _ = AR.alloc((D,), F32, parts=NS)
        n0, _ = AR.alloc((D,), F32, parts=NS)
        m0, _ = AR.alloc((4,), F32, parts=NS)
        DMA("sp", cs_l, st_lconv, "b20")
        DMA("sp", h0, st_lh, "b22")
        DMA("sp", n0, st_n.rearrange("b h e -> b (h e)"), "b23")
        DMA("sp", m0, st_m, "b24")
        ACT(lam_b, lam_b, AF.Exp, scale=-1.0)
        ACT(lam_b, lam_b, AF.Ln, bias=1.0)
        TS("dve", lam_b, lam_b, -8.0, None, ALU.mult)
        cc_b = lam_b

        ta, _ = AR.alloc((D,), F32, parts=NS)
        tb, _ = AR.alloc((D,), F32, parts=NS)
        tc_, _ = AR.alloc((D,), F32, parts=NS)
        td, _ = AR.alloc((D,), F32, parts=NS)
        te, _ = AR.alloc((D,), F32, parts=NS)
        tT, _ = AR.alloc((8, NS), BF16)
        tT2, _ = AR.alloc((8, NS), BF16)
        qTf, _ = AR.alloc((8, NS), F32)
        mS, _ = AR.alloc((D,), F32, parts=NS)
        sm, _ = AR.alloc((16, 4), F32, parts=NS)

        xl_s = zs[:, 0:D]
        xm_s = zs[:, D:2 * D]
        o_s = zs[:, 2 * D:3 * D]
        ig_s = zs[:, 3 * D:3 * D + 4]
        fg_s = zs[:, 3 * D + 4:3 * D + 8]
        gl_s = zs[:, 3 * D + 8:4 * D + 8]
        gm_s = zs[:, 4 * D + 8:5 * D + 8]

        def convS(out, xnew, cs, wsrc, bsrc):
            b_b = prow(bsrc)
            for j in range(4):
                wt = prow(wsrc[j, :])
                xj = cs[:, j, :] if j < 3 else xnew
                if j == 0:
                    TTo("dve", out, xj, wt, ALU.mult)
                    TTo("dve", out, out, b_b, ALU.add)
                else:
                    TTo("dve", ta, xj, wt, ALU.mult)
                    TTo("dve", out, out, ta, ALU.add)

        def transS(dstT, src_):
            b = nbank()
            for kc in range(8):
                TR(PS[:, b, kc * NS:(kc + 1) * NS], src_[:, kc * 128:(kc + 1) * 128], ident[0:NS, 0:NS])
            CP("dve", dstT, PS[:, b, 0:8 * NS].rearrange("p (c t) -> p c t", t=NS))

        def wideS(fn_mm):
            b0_, b1_ = nbank(), nbank()
            fn_mm(lambda col, w: PS[0:NS, b0_ if col < 512 else b1_, (col % 512):(col % 512) + w])
            return [PS[0:NS, b0_, :], PS[0:NS, b1_, :]]

        DMA("sp", o_slconv[:, 0:2, :], cs_l[:, 1:3, :], "o0", is_out=True)
        DMA("sp", o_slconv[:, 2, :], xl_s, "o1", is_out=True)
        convS(tb, xl_s, cs_l, lru_cw, lru_cb)
        transS(tT, tb)

        def mm_gate(wbf):
            def f(dst):
                for n in range(8):
                    MM(dst(n * 128, 128), tT[:, n, :], wbf[:, n, :], True, True)
            return f

        pr = wideS(mm_gate(wa_bf))
        pi = wideS(mm_gate(wx_bf))
        lba_b = prow(lru_ba)
        lbx_b = prow(lru_bx)
        for hh in range(2):
            sl_ = slice(hh * 512, (hh + 1) * 512)
            TTo("dve", tc_[:, sl_], pr[hh], lba_b[:, sl_], ALU.add)
            TTo("dve", td[:, sl_], pi[hh], lbx_b[:, sl_], ALU.add)
        ACT(tc_, tc_, AF.Sigmoid)
        ACT(td, td, AF.Sigmoid)
        TTo("dve", tc_, tc_, cc_b, ALU.mult)
        ACT(tc_, tc_, AF.Exp)
        TTo("dve", te, tc_, tc_, ALU.mult)
        ACT(te, te, AF.Sqrt, bias=1.0, scale=-1.0)
        TTo("dve", te, te, td, ALU.mult)
        TTo("dve", te, te, tb, ALU.mult)
        TTo("dve", tc_, tc_, h0, ALU.mult)
        TTo("dve", tc_, tc_, te, ALU.add)
        DMA("sp", o_slh, tc_, "o2", is_out=True)
        ACT(td, gl_s, AF.Sigmoid)
        TTo("dve", mS, td, tc_, ALU.mult)

        DMA("sp", cs_m, st_mconv, "b21")
        DMA("sp", o_smconv[:, 0:2, :], cs_m[:, 1:3, :], "o3", is_out=True)
        DMA("sp", o_smconv[:, 2, :], xm_s, "o4", is_out=True)
        convS(tb, xm_s, cs_m, m_cw, m_cb)
        ACT(tb, tb, AF.Silu)
        transS(tT, tb)
        transS(tT2, xm_s)
        qS, _ = AR.alloc((D,), F32, parts=NS)
        kS, _ = AR.alloc((D,), F32, parts=NS)
        vS, _ = AR.alloc((D,), F32, parts=NS)

        def mm_qkv(wbf, xT):
            def f(dst):
                for h in range(4):
                    for j in range(2):
                        MM(dst(h * 256, 256), xT[:, 2 * h + j, :], wbf[:, 2 * h + j, :], j == 0, j == 1)
            return f

        pq = wideS(mm_qkv(wq_bf, tT))
        for hh in range(2):
            ACT(qS[:, hh * 512:(hh + 1) * 512], pq[hh], AF.Copy, scale=1.0 / 16.0)
        pk = wideS(mm_qkv(wk_bf, tT))
        for hh in range(2):
            CP("dve", kS[:, hh * 512:(hh + 1) * 512], pk[hh])
        pv_ = wideS(mm_qkv(wv_bf, tT2))
        for hh in range(2):
            CP("act", vS[:, hh * 512:(hh + 1) * 512], pv_[hh])
        SM = lambda k: sm[:, k, :]
        TTo("dve", SM(0), ig_s, bg_b[:, 0:4], ALU.add)
        TTo("dve", SM(1), fg_s, bg_b[:, 4:8], ALU.add)
        ACT(SM(1), SM(1), AF.Exp, scale=-1.0)
        ACT(SM(1), SM(1), AF.Ln, bias=1.0)
        TTo("dve", SM(2), m0, SM(1), ALU.subtract)
        TTo("dve", SM(3), SM(2), SM(0), ALU.max)
        DMA("sp", o_sm, SM(3), "o5", is_out=True)
        TTo("dve", SM(4), SM(0), SM(3), ALU.subtract)
        ACT(SM(4), SM(4), AF.Exp)
        TTo("dve", SM(5), SM(2), SM(3), ALU.subtract)
        ACT(SM(5), SM(5), AF.Exp)
        ACT(SM(6), SM(3), AF.Exp, scale=-1.0)
        TTo("dve", ta, qS, kS, ALU.mult)
        S.add("dve", lambda e: e.tensor_reduce(SM(7), ta.rearrange("p (h e) -> p h e", e=256), AX.X, ALU.add),
              [ta], [SM(7)])
        TTo("dve", ta, qS, n0, ALU.mult)
        S.add("dve", lambda e: e.tensor_reduce(SM(8), ta.rearrange("p (h e) -> p h e", e=256), AX.X, ALU.add),
              [ta], [SM(8)])
        TTo("dve", SM(9), SM(7), SM(4), ALU.mult)
        TTo("dve", SM(10), SM(5), SM(8), ALU.mult)
        TTo("dve", SM(10), SM(10), SM(9), ALU.add)
        STT("dve", SM(14), SM(10), -1.0, SM(10), ALU.mult, ALU.max)
        TTo("dve", SM(10), SM(14), SM(6), ALU.max)
        RECIP(SM(11), SM(10))
        for h in range(4):
            hs = slice(h * 256, (h + 1) * 256)
            TS("dve", ta[:, hs], kS[:, hs], sm[:, 4, h:h + 1], None, ALU.mult)
            STT("dve", tb[:, hs], n0[:, hs], sm[:, 5, h:h + 1], ta[:, hs], ALU.mult, ALU.add)
        DMA("sp", o_sn.rearrange("b h e -> b (h e)"), tb, "o6", is_out=True)
        kwS = ta
        b = nbank()
        for kc in range(8):
            TR(PS[:, b, kc * NS:(kc + 1) * NS], qS[:, (kc // 2) * 256 + kc % 2:(kc // 2 + 1) * 256:2], ident[0:NS, 0:NS])
        CP("dve", qTf, PS[:, b, 0:8 * NS].rearrange("p (c t) -> p c t", t=NS))
        scd, _ = AR.alloc((NS, 4), F32, parts=NS)
        scbc, _ = AR.alloc((NS, 4), F32)
        ones16, _ = AR.alloc((128,), F32, parts=NS)
        MSET("dve", ones16, 1.0)
        TTo("dve", scd, sm[:, 5, :].unsqueeze(1).broadcast_to([NS, NS, 4]),
            ident[0:NS, 0:NS].unsqueeze(2).broadcast_to([NS, NS, 4]), ALU.mult)
        b = nbank()
        MM(PS[:, b, 0:64], ones16, scd.rearrange("p a b -> p (a b)"), True, True)
        CP("dve", scbc.rearrange("p a b -> p (a b)"), PS[:, b, 0:64])
        vbf, _ = AR.alloc((D,), BF16, parts=NS)
        CP("act", vbf, vS)
        qmk, _ = AR.alloc((2, 8, NS), F32)
        kmk, _ = AR.alloc((2, D), BF16, parts=NS)
        C0t, _ = AR.alloc((2, 8, 256), F32, at=mk0)
        Cnt, _ = AR.alloc((2, 8, 256), F32, at=mk0 + 16384)
        numS, _ = AR.alloc((D,), F32, parts=NS, at=wreg_off + 12288)
        for h in range(4):
            pass
        def c0_load(bsm):
            DMA("sp", C0t[:, bsm % 2, :, :].rearrange("p (h r) e -> p h (r e)", r=2),
                st_C[bsm].rearrange("h (p r) e -> p h (r e)", r=2), f"c0_{bsm % 2}")

        MARKS.append(("S_loop", len(S.ops)))
        c0_load(0)
        for bsm in range(NS):
            s3 = bsm % 2
            s2 = bsm % 2
            if bsm + 1 < NS:
                c0_load(bsm + 1)
            TTo("dve", qmk[:, s2, :, :], qTf, idrep[:, bsm:bsm + 1, :].broadcast_to([128, 8, NS]), ALU.mult)
            TS("dve", kmk[:, s2, :], kwS, ident[0:NS, bsm:bsm + 1], None, ALU.mult)
            for h in range(4):
                for j in range(2):
                    MM(PS[0:NS, 4 + h, 0:256], qmk[:, s2, 2 * h + j, :], C0t[:, s3, 2 * h + j, :],
                       bsm == 0 and j == 0, bsm == NS - 1 and j == 1)
            for h in range(4):
                bb = nbank(0, 4)
                for j in range(2):
                    MM(PS[:, bb, j * 256:(j + 1) * 256], kmk[:, s2, h * 256 + j:(h + 1) * 256:2], vbf[:, h * 256:(h + 1) * 256],
                       True, True)
                STT("dve", Cnt[:, s2, 2 * h:2 * h + 2, :], C0t[:, s3, 2 * h:2 * h + 2, :], scbc[:, bsm, h:h + 1],
                    PS[:, bb, :].rearrange("p (j e) -> p j e", e=256), ALU.mult, ALU.add)
            DMA(KNOB.get("cst_q", "act"), o_sC[bsm].rearrange("h (p r) e -> p h (r e)", r=2),
                Cnt[:, s2, :, :].rearrange("p (h r) e -> p h (r e)", r=2), f"oc{s2}", is_out=True)
        for h in range(4):
            CP("act", numS[:, h * 256:(h + 1) * 256], PS[0:NS, 4 + h, 0:256])
        for h in range(4):
            hs = slice(h * 256, (h + 1) * 256)
            TS("dve", numS[:, hs], numS[:, hs], sm[:, 5, h:h + 1], None, ALU.mult)
            STT("dve", numS[:, hs], vS[:, hs], sm[:, 9, h:h + 1], numS[:, hs], ALU.mult, ALU.add)
            TS("dve", numS[:, hs], numS[:, hs], sm[:, 11, h:h + 1], None, ALU.mult)
            ACT(tb[:, hs], numS[:, hs], AF.Square, accum_out=sm[:, 12, h:h + 1])
        ACT(SM(13), SM(12), AF.Ln, bias=EPS, scale=1.0 / 256.0)
        ACT(SM(13), SM(13), AF.Exp, scale=-0.5)
        mg_b = prow(m_g.rearrange("h e -> (h e)"))
        for h in range(4):
            hs = slice(h * 256, (h + 1) * 256)
            STT("dve", numS[:, hs], numS[:, hs], sm[:, 13, h:h + 1], mg_b[:, hs], ALU.mult, ALU.mult)
        ACT(td, o_s, AF.Sigmoid)
        TTo("dve", numS, numS, td, ALU.mult)
        ACT(td, gm_s, AF.Sigmoid)
        TTo("dve", numS, numS, td, ALU.mult)
        TTo("dve", mS, mS, numS, ALU.add)
        transS(mrgS, mS)

    try:
        MARKS.append(("P", len(S.ops)))
        AR.top = mk0
        xn, _ = AR.alloc((8, T), BF16)
        mrg, _ = AR.alloc((8, T), BF16)
        PB = AR.top
        alloc_stage1(KNOB.get("st1_slots", 4))
        load_tiles("norm", xn, None, list(range(16)))
        AR.top = PB
        gX2, _ = AR.alloc((T,), F32, parts=4)
        gX3, _ = AR.alloc((T,), F32, parts=4)
        gcol, _ = AR.alloc((16, 4), F32)
        nbf, _ = AR.alloc((2,), F32, parts=4)
        bufA, _ = AR.alloc((2, T + 4), F32)
        xc, _ = AR.alloc((2, T), BF16)
        qTb, _ = AR.alloc((2, T + 4), BF16)
        qT = qTb[:, :, 0:T]
        qsc, _ = AR.alloc((2, T), BF16)
        kT, _ = AR.alloc((2, T), BF16)
        kw, kwoff = AR.alloc((16, 256), BF16)
        gX1, _ = AR.alloc((T,), F32, parts=4, at=kwoff)
        vt, _ = AR.alloc((16, 258), BF16)
        DT, _ = AR.alloc((16, 128), BF16)
        LB0 = bufA
        Caug, _ = AR.alloc((2, 258), F32)
        Cb2, _ = AR.alloc((2, 2, 258), BF16)
        nb2, _ = AR.alloc((2, 2, 128), BF16)
        Gl, _ = AR.alloc((17,), F32)
        decb, _ = AR.alloc((16,), F32)
        wkc, _ = AR.alloc((16,), F32)
        dd2, _ = AR.alloc((2, 128), F32)
        hT2, _ = AR.alloc((2, 2, 128), F32)
        sqh2, _ = AR.alloc((2, 2, 128), BF16)
        rsh, _ = AR.alloc((128,), F32)
        sd, _ = AR.alloc((128,), BF16)
        dtmp, _ = AR.alloc((4, 128), F32)
        sg = dtmp.rearrange("p a b -> p (a b)")
        gm1, _ = AR.alloc((1,), F32, parts=4)
        xmp = bufA
        hn = bufA[:, :, 0:T]
        scb = bufA[:, 0, 0:T]
        xmb = qTb
        xmlast, _ = AR.alloc((2, 4), F32)
        xllast, _ = AR.alloc((4,), F32)
        dgm, _ = AR.alloc((2, 4, 128), BF16)
        dgl, _ = AR.alloc((4, 128), BF16)
        xcoff = region(xc)[3]
        emb = At[:, xcoff // 4:xcoff // 4 + T]
        ktoff = region(kT)[3]
        ctmp = At[:, ktoff // 4:ktoff // 4 + T]
        def _f32(off, n_):
            return At[:, off // 4:off // 4 + n_]
        oA, oQ, oK, oW, oV = (region(b_)[3] for b_ in (bufA, qsc, kT, kw, vt))
        xlc = _f32(oA, T)
        rr = _f32(oA + T * 4, T)
        ii = _f32(oQ, T)
        uu = _f32(oK, T)
        hh_ = _f32(oW, T)
        xlb = At[:, oV // 4:oV // 4 + (T + 4) // 2].bitcast(BF16)
        xlcb = At[:, (oV + (T + 4) * 2) // 4:(oV + (T + 4) * 2) // 4 + T // 2].bitcast(BF16)
        assert (T + 4) * 2 + T * 2 <= 16 * 258 * 2 and 2 * T * 4 <= 2 * (T + 4) * 4

        def sweepP(wb, w, evac, rhs, c0, nk=8):
            for mi in range(w // 128):
                b = nbank(*KNOB["pbanks"])
                for kc in range(nk):
                    MM(PS[:, b, 0:NS], wb[:, kc, mi * 128:(mi + 1) * 128], xnS[:, kc, :], kc == 0, kc == nk - 1)
                CP("act", zT[:, len(zmap), :], PS[:, b, 0:NS])
                zmap.append((len(zmap), c0 + mi * 128, 128))
                for nt in range(4):
                    b = nbank(*KNOB["pbanks"])
                    for kc in range(nk):
                        MM(PS[:, b, :], wb[:, kc, mi * 128:(mi + 1) * 128], rhs[:, kc, nt * 512:(nt + 1) * 512], kc == 0, kc == nk - 1)
                    evac(mi, nt, PS[:, b, :])

        TS("dve", nbf, bgc, -1.0, None, ALU.mult)
        wb = wload(w_in, 3 * D, 8)
        b = nbank(*KNOB["pbanks"])
        for kc in range(8):
            MM(PS[0:8, b, 0:NS], wb[:, kc, 0:8], xnS[:, kc, :], kc == 0, kc == 7)
        CP("act", zT[0:8, 40, :], PS[0:8, b, 0:NS])
        zmap_gates = (40, 3 * D, 8)
        for nt in range(4):
            ns = slice(nt * 512, (nt + 1) * 512)
            b = nbank(*KNOB["pbanks"])
            for kc in range(8):
                MM(PS[0:4, b, :], wb[:, kc, 0:4], xn[:, kc, ns], kc == 0, kc == 7)
            ACT(gX1[:, ns], PS[0:4, b, :], AF.Identity, bias=bgc[:, 0:1])
            b = nbank(*KNOB["pbanks"])
            for kc in range(8):
                MM(PS[0:4, b, :], wb[:, kc, 4:8], xn[:, kc, ns], kc == 0, kc == 7)
            ACT(gX2[:, ns], PS[0:4, b, :], AF.Exp, bias=nbf[:, 1:2], scale=-1.0)
        ACT(gX2, gX2, AF.Ln, bias=1.0)
        SCAN(gX3, gX2, gX2, 0.0, ALU.add, ALU.max)
        TTo("dve", gX1, gX1, gX3, ALU.add)
        SCAN(gX2, gX1, gX1, NEG, ALU.max, ALU.max)
        TTo("dve", gX3, gX3, gX2, ALU.subtract)
        TS("dve", gm1, gX3[:, T - 1:T], -1.0, None, ALU.mult)
        DMA("sp", o_pm.rearrange("o h -> h o"), gm1, "o7", is_out=True, slow=True)
        ACT(gX3, gX3, AF.Exp)
        b = nbank(*KNOB["pbanks"])
        for c in range(16):
            TR(PS[:, b, c * 4:(c + 1) * 4], gX1[:, c * 128:(c + 1) * 128], ident[0:4, 0:4])
        CP("dve", gcol, PS[:, b, 0:64].rearrange("p (c h) -> p c h", h=4))
        if LEVEL == 1:
            raise _Stop()
        gG, gEm = gX2, gX3

        for h in range(4):
            MARKS.append((f"P_h{h}_prep", len(S.ops)))
            MSET("dve", xmb[:, :, 0:4], 0.0)
            for mi in range(2):
                for j in range(4):
                    TS("dve", dgm[:, mi, j, :], ident, mcw[:, j, 2 * h + mi:2 * h + mi + 1], None, ALU.mult)
            wb = wload(w_in, D + h * 256, 256)

            def ev_xm(mi, nt, ps):
                CP("act", xmb[:, mi, 4 + nt * 512:4 + (nt + 1) * 512], ps)
                if nt == 3:
                    CP("dve", xmlast[:, mi, 0:3], ps[:, 509:512])
            sweepP(wb, 256, ev_xm, xn, D + h * 256)
            for mi in range(2):
                DMA("sp", o_pmconv[:, h * 256 + mi * 128:h * 256 + (mi + 1) * 128].rearrange("r p -> p r"), xmlast[:, mi, 0:3],
                    "o8", is_out=True, slow=True)
            MSET("dve", vt[:, :, 256:258], 1.0)
            for c2 in range(8):
                b = nbank(*KNOB["pbanks"])
                for cc in range(2):
                    c = 2 * c2 + cc
                    for dc in range(2):
                        MM(PS[:, b, cc * 256:(cc + 1) * 256], xmb[:, dc, 4 + c * 128:4 + (c + 1) * 128], wv_bf[:, 2 * h + dc, :], dc == 0, dc == 1)
                CP("act", vt[:, 2 * c2:2 * c2 + 2, 0:256], PS[:, b, :].rearrange("p (c e) -> p c e", e=256))
            for mi in range(2):
                kcg = 2 * h + mi
                for nt in range(4):
                    b = nbank(*KNOB["pbanks"])
                    for j in range(4):
                        MM(PS[:, b, :], dgm[:, mi, j, :], xmb[:, mi, nt * 512 + j + 1:nt * 512 + j + 513], j == 0, j == 3)
                    ACT(xc[:, mi, nt * 512:(nt + 1) * 512], PS[:, b, :], AF.Silu, bias=MCB[:, kcg:kcg + 1])
            if LEVEL == 2:
                raise _Stop()
            MSET("dve", Gl[:, 0:1], NEG)
            for nt in range(4):
                b = nbank(*KNOB["pbanks"])
                MM(PS[:, b, :], esel[:, h, :], gG[:, nt * 512:(nt + 1) * 512], True, True)
                CP("act", Gl[:, 1 + 4 * nt:5 + 4 * nt], PS[:, b, 127::128])
            TTo("dve", decb, Gl[:, 0:16], Gl[:, 1:17], ALU.subtract)
            ACT(decb, decb, AF.Exp)
            TTo("dve", wkc, gcol[:, :, h], Gl[:, 1:17], ALU.subtract)
            ACT(wkc, wkc, AF.Exp)
            for nt in range(4):
                b = nbank(*KNOB["pbanks"])
                MM(PS[:, b, :], esel[:, h, :], gG[:, nt * 512:(nt + 1) * 512], True, True)
                for cc in range(4):
                    c = 4 * nt + cc
                    ACT(scb[:, c * 128:(c + 1) * 128], PS[:, b, cc * 128:(cc + 1) * 128], AF.Exp, bias=Gl[:, c:c + 1], scale=-1.0)
                STT("dve", dtmp, PS[:, b, :].rearrange("p (c t) -> p c t", t=128), -1.0,
                    maskneg.unsqueeze(1).broadcast_to([128, 4, 128]), ALU.mult, ALU.add)
                for cc in range(4):
                    c = 4 * nt + cc
                    ACT(DT[:, c, :], dtmp[:, cc, :], AF.Exp, bias=gcol[:, c, h:h + 1])
            for j in range(2):
                for nt in range(4):
                    ns = slice(nt * 512, (nt + 1) * 512)
                    b = nbank(*KNOB["pbanks"])
                    for dc in range(2):
                        MM(PS[:, b, :], wk_bf[:, 2 * h + dc, j * 128:(j + 1) * 128], xc[:, dc, ns], dc == 0, dc == 1)
                    CP(KNOB.get("kev", "dve"), kT[:, j, ns], PS[:, b, :])
            for c2 in range(8):
                b = nbank(*KNOB["pbanks"])
                for cc in range(2):
                    c = 2 * c2 + cc
                    for dc in range(2):
                        MM(PS[:, b, cc * 256:(cc + 1) * 256], xc[:, dc, c * 128:(c + 1) * 128], wk_bf[:, 2 * h + dc, :], dc == 0, dc == 1)
                for cc in range(2):
                    c = 2 * c2 + cc
                    TS("dve", kw[:, c, :], PS[:, b, cc * 256:(cc + 1) * 256], wkc[:, c:c + 1], None, ALU.mult)
            for j in range(2):
                for nt in range(4):
                    ns = slice(nt * 512, (nt + 1) * 512)
                    b = nbank(*KNOB["pbanks"])
                    for dc in range(2):
                        MM(PS[:, b, :], wq_bf[:, 2 * h + dc, j * 128:(j + 1) * 128], xc[:, dc, ns], dc == 0, dc == 1)
                    ACT(qT[:, j, ns], PS[:, b, :], AF.Copy, scale=1.0 / 16.0)
                    STT("dve", qsc[:, j, ns], PS[:, b, :], 1.0 / 16.0, scb[:, ns], ALU.mult, ALU.mult)
            for nt in range(4):
                b = nbank(*KNOB["pbanks"])
                MM(PS[:, b, :], esel[:, h, :], gEm[:, nt * 512:(nt + 1) * 512], True, True)
                CP("act", emb[:, nt * 512:(nt + 1) * 512], PS[:, b, :])
            if LEVEL == 3:
                raise _Stop()
            MARKS.append((f"P_h{h}_loop", len(S.ops)))
            MSET("dve", Caug, 0.0)
            MSET("dve", Cb2, 0.0)
            MSET("dve", nb2, 0.0)

            def stageA(c):
                cs_ = slice(c * 128, (c + 1) * 128)
                nbk = 4 + c % 2
                k_ = c % 2
                for dc in range(2):
                    MM(PS[:, 6, dc * 256:(dc + 1) * 256], kw[:, c, dc * 128:(dc + 1) * 128], vt[:, c, 0:256], True, True)
                for dc in range(2):
                    MM(PS[:, 7, 256 + 2 * dc:258 + 2 * dc], kw[:, c, dc * 128:(dc + 1) * 128], vt[:, c, 256:258], True, True)
                for dc in range(2):
                    MM(PS[:, 3, 0:128], kT[:, dc, cs_], qT[:, dc, cs_], dc == 0, dc == 1)
                TTo("dve", sd, PS[:, 3, 0:128], DT[:, c, :], ALU.mult)
                Cbp = Cb2[:, 1 - k_, :, :]
                nbp = nb2[:, 1 - k_, :, :]
                for j in range(2):
                    MM(PS[:, nbk, j * 128:(j + 1) * 128], vt[:, c, j * 128:(j + 1) * 128], sd, True, False)
                    for dc in range(2):
                        MM(PS[:, nbk, j * 128:(j + 1) * 128], Cbp[:, dc, j * 128:(j + 1) * 128], qsc[:, dc, cs_], False, dc == 1)
                MM(PS[:, nbk, 256:384], ones_bf, sd, True, False)
                for dc in range(2):
                    MM(PS[:, nbk, 256:384], nbp[:, dc, :], qsc[:, dc, cs_], False, dc == 1)
                STT("dve", Caug[:, :, 0:256], Caug[:, :, 0:256], decb[:, c:c + 1],
                    PS[:, 6, :].rearrange("p (j e) -> p j e", e=256), ALU.mult, ALU.add)
                STT("dve", Caug[:, :, 256], Caug[:, :, 256], decb[:, c:c + 1], PS[:, 7, 256:260:2], ALU.mult, ALU.add)
                CP("act", Cb2[:, k_, :, :], Caug)
                CP(KNOB.get("nbev", "pool"), nb2[:, k_, :, :], Caug[:, :, 256:257].broadcast_to([128, 2, 128]))

            def stageB(c):
                cs_ = slice(c * 128, (c + 1) * 128)
                nbk = 4 + c % 2
                k_ = c % 2
                ACT(dd2[:, k_, :], PS[:, nbk, 256:384], AF.Abs)
                TTo("dve", dd2[:, k_, :], dd2[:, k_, :], emb[:, cs_], ALU.max)
                ACT(dd2[:, k_, :], dd2[:, k_, :], AF.Ln)
                ACT(dd2[:, k_, :], dd2[:, k_, :], AF.Exp, scale=-1.0)
                TTo("dve", hT2[:, k_, :, :], PS[:, nbk, 0:256].rearrange("p (j t) -> p j t", t=128),
                    dd2[:, k_, :].unsqueeze(1).broadcast_to([128, 2, 128]), ALU.mult)
                ACT(sqh2[:, k_, :, :], hT2[:, k_, :, :], AF.Square)

            def stageC(c):
                cs_ = slice(c * 128, (c + 1) * 128)
                k_ = c % 2
                for j in range(2):
                    MM(PS[:, 7, 0:128], ones_bf, sqh2[:, k_, j, :], j == 0, j == 1)
                ACT(rsh, PS[:, 7, 0:128], AF.Ln, bias=EPS, scale=1.0 / 256.0)
                ACT(rsh, rsh, AF.Exp, scale=-0.5)
                for j in range(2):
                    STT("dve", hn[:, j, cs_], hT2[:, k_, j, :], MGC[:, 2 * h + j:2 * h + j + 1], rsh, ALU.mult, ALU.mult)

            for i_ in range(16 + 2):
                if i_ < 16:
                    stageA(i_)
                if 0 <= i_ - 1 < 16:
                    stageB(i_ - 1)
                if 0 <= i_ - 2 < 16:
                    stageC(i_ - 2)
            DMA("sp", o_pC[h].rearrange("(j p) e -> p j e", p=128), Caug[:, :, 0:256], "o9", is_out=True)
            DMA("sp", o_pn[h].rearrange("(j p) -> p j", p=128), Caug[:, :, 256], "o10", is_out=True, slow=True)
            if LEVEL == 4:
                raise _Stop()
            MARKS.append((f"P_h{h}_gate", len(S.ops)))
            for gi_, c0 in enumerate((2 * D + h * 256, 4 * D + 8 + h * 256)):
                wb = wload(w_in, c0, 256)

                def ev_gate(mi, nt, ps, gi_=gi_):
                    ns = slice(nt * 512, (nt + 1) * 512)
                    sg_ = emb[:, ns]
                    ACT(sg_, ps, AF.Sigmoid)
                    if gi_ == 0:
                        TTo("dve", hn[:, mi, ns], hn[:, mi, ns], sg_, ALU.mult)
                    else:
                        TTo("dve", mrg[:, 2 * h + mi, ns], hn[:, mi, ns], sg_, ALU.mult)
                sweepP(wb, 256, ev_gate, xn, c0)
            if LEVEL == 5:
                raise _Stop()
            MARKS.append((f"P_h{h}_lru", len(S.ops)))
            for mi in range(2):
                n = 2 * h + mi
                MSET("dve", xlb[:, 0:4], 0.0)
                for j in range(4):
                    TS("dve", dgl[:, j, :], ident, lcw[:, j, n:n + 1], None, ALU.mult)
                wb = wload(w_in, n * 128, 128)

                def ev_xl(mi_, nt, ps):
                    CP("act", xlb[:, 4 + nt * 512:4 + (nt + 1) * 512], ps)
                    if nt == 3:
                        CP("dve", xllast[:, 0:3], ps[:, 509:512])
                sweepP(wb, 128, ev_xl, xn, n * 128)
                DMA("sp", o_plconv[:, n * 128:(n + 1) * 128].rearrange("r p -> p r"), xllast[:, 0:3], "o11", is_out=True, slow=True)
                for nt in range(4):
                    ns = slice(nt * 512, (nt + 1) * 512)
                    b = nbank(*KNOB["pbanks"])
                    for j in range(4):
                        MM(PS[:, b, :], dgl[:, j, :], xlb[:, nt * 512 + j + 1:nt * 512 + j + 513], j == 0, j == 3)
                    ACT(xlc[:, ns], PS[:, b, :], AF.Identity, bias=LCB[:, n:n + 1])
                    TS("dve", xlcb[:, ns], PS[:, b, :], LCB[:, n:n + 1], None, ALU.add)
                for nt in range(4):
                    ns = slice(nt * 512, (nt + 1) * 512)
                    b = nbank(*KNOB["pbanks"])
                    MM(PS[:, b, :], wa_bf[:, n, :], xlcb[:, ns], True, True)
                    ACT(rr[:, ns], PS[:, b, :], AF.Sigmoid, bias=LBA[:, n:n + 1])
                    b = nbank(*KNOB["pbanks"])
                    MM(PS[:, b, :], wx_bf[:, n, :], xlcb[:, ns], True, True)
                    ACT(ii[:, ns], PS[:, b, :], AF.Sigmoid, bias=LBX[:, n:n + 1])
                ACT(uu, rr, AF.Exp, scale=C2COL[:, n:n + 1])
                ACT(rr, rr, AF.Exp, scale=CCOL[:, n:n + 1])
                if KNOB.get("sqrt_explog", False):
                    ACT(uu, uu, AF.Ln, bias=1.0, scale=-1.0)
                    ACT(uu, uu, AF.Exp, scale=0.5)
                else:
                    ACT(uu, uu, AF.Sqrt, bias=1.0, scale=-1.0)
                if KNOB.get("ix", False):
                    TTo(KNOB.get("ixeng", "pool"), ii, ii, xlc, ALU.mult)
                    TTo("dve", uu, uu, ii, ALU.mult)
                else:
                    TTo("dve", uu, uu, ii, ALU.mult)
                    TTo("dve", uu, uu, xlc, ALU.mult)
                SCAN(hh_, rr, uu, 0.0, ALU.mult, ALU.add)
                DMA("sp", o_plh[0:1, n * 128:(n + 1) * 128].rearrange("o p -> p o"), hh_[:, T - 1:T], "o12", is_out=True, slow=True)
                wb = wload(w_in, 3 * D + 8 + n * 128, 128)

                def ev_gl(mi_, nt, ps, n=n, mi=mi):
                    ns = slice(nt * 512, (nt + 1) * 512)
                    sg_ = ii[:, ns]
                    ACT(sg_, ps, AF.Sigmoid)
                    TTo("dve", sg_, sg_, hh_[:, ns], ALU.mult)
                    TTo("dve", mrg[:, n, ns], mrg[:, n, ns], sg_, ALU.add)
                sweepP(wb, 128, ev_gl, xn, 3 * D + 8 + n * 128)

        if LEVEL == 6:
            raise _Stop()
        zmap.append(zmap_gates)
        phase_S()
        MARKS.append(("D", len(S.ops)))
        AR.top = PB
        alloc_stage1()
        xres, _ = AR.alloc((8, T), F32)
        xresS, _ = AR.alloc((8, NS), F32)
        sqn, sqoff = AR.alloc((8, 512), BF16)
        rst, _ = AR.alloc((512,), F32)
        sg2, _ = AR.alloc((512,), F32)
        pT, _ = AR.alloc((2, T), BF16, at=sqoff)
        pTS, _ = AR.alloc((2, NS), BF16)
        wple_bf, _ = AR.alloc((2, D), BF16)
        aTS, _ = AR.alloc((12, NS), BF16)
        moff = region(mrg)[3]
        aT = At[:, moff // 4:moff // 4 + 12 * T // 2].bitcast(BF16).rearrange("p (a b) -> p a b", b=T)
        assert moff + 12 * T * 2 <= region(xres)[3], (moff, region(xres))
        load_tiles("raw", xres, xresS, list(range(17)))

        def xsl(bP, bS, m, nt):
            return bP[:, m, nt * 512:(nt + 1) * 512] if nt < 4 else bS[:, m, :]

        def wd_(nt):
            return 512 if nt < 4 else NS

        for blk in range(4):
            wb = wload(w_out, blk * 256, 256)
            for mi in range(2):
                m = blk * 2 + mi
                for nt in range(5):
                    b = nbank()
                    for kc in range(8):
                        MM(PS[:, b, 0:wd_(nt)], wb[:, kc, mi * 128:(mi + 1) * 128], xsl(mrg, mrgS, kc, nt), kc == 0, kc == 7)
                    xr = xsl(xres, xresS, m, nt)
                    TTo("dve", xr, PS[:, b, 0:wd_(nt)], xr, ALU.add)

        def normD(gcols, dstP, dstS):
            for nt in range(5):
                w_ = wd_(nt)
                for kc in range(8):
                    ACT(sqn[:, kc, 0:w_], xsl(xres, xresS, kc, nt), AF.Square)
                b = nbank()
                for kc in range(8):
                    MM(PS[:, b, 0:w_], ones_bf, sqn[:, kc, 0:w_], kc == 0, kc == 7)
                ACT(rst[:, 0:w_], PS[:, b, 0:w_], AF.Ln, bias=EPS, scale=1.0 / D)
                ACT(rst[:, 0:w_], rst[:, 0:w_], AF.Exp, scale=-0.5)
                for kc in range(8):
                    STT("dve", xsl(dstP, dstS, kc, nt), xsl(xres, xresS, kc, nt), gcols[:, kc:kc + 1], rst[:, 0:w_], ALU.mult, ALU.mult)

        MARKS.append(("D_norm2", len(S.ops)))
        normD(G2C, xn, xnS)
        MARKS.append(("D_ffn", len(S.ops)))
        for f0, nf in ((0, 12), (12, 10)):
            for fp in range(nf // 2):
                wg = wload(w_gate, (f0 + 2 * fp) * 128, 256)
                wu = wload(w_up, (f0 + 2 * fp) * 128, 256)
                for fj in range(2):
                    fi = 2 * fp + fj
                    for nt in range(5):
                        w_ = wd_(nt)
                        bg_ = nbank()
                        for kc in range(8):
                            MM(PS[:, bg_, 0:w_], wg[:, kc, fj * 128:(fj + 1) * 128], xsl(xn, xnS, kc, nt), kc == 0, kc == 7)
                        bu_ = nbank()
                        for kc in range(8):
                            MM(PS[:, bu_, 0:w_], wu[:, kc, fj * 128:(fj + 1) * 128], xsl(xn, xnS, kc, nt), kc == 0, kc == 7)
                        ACT(sg2[:, 0:w_], PS[:, bg_, 0:w_], AF.Silu)
                        TTo("dve", xsl(aT, aTS, fi, nt), sg2[:, 0:w_], PS[:, bu_, 0:w_], ALU.mult)
            for m in range(8):
                wdn = wload(w_down[f0 * 128:(f0 + nf) * 128, :], m * 128, 128, kparts=nf)
                for nt in range(5):
                    w_ = wd_(nt)
                    b = nbank()
                    for fi in range(nf):
                        MM(PS[:, b, 0:w_], wdn[:, fi, :], xsl(aT, aTS, fi, nt), fi == 0, fi == nf - 1)
                    xr = xsl(xres, xresS, m, nt)
                    TTo("dve", xr, PS[:, b, 0:w_], xr, ALU.add)
        MARKS.append(("D_norm3", len(S.ops)))
        normD(G3C, xn, xnS)
        DMA("pool", wple_bf, w_ple.rearrange("(k p) n -> p k n", p=128), "w_ple", cast=True)
        for tt in range(17):
            npart = 128 if tt < 16 else NS
            sl = tt % 2
            pin = st1["xin"][0:npart, sl, 0:PD]
            DMA("sp", pin, pp[tt * 128:(tt + 1) * 128, :] if tt < 16 else psm, f"xin{sl}")
            b = nbank()
            for c in range(2):
                TR(PS[:, b, c * 128:c * 128 + npart], pin[:, c * 128:(c + 1) * 128], ident[0:npart, 0:npart])
            dv = pT[:, :, tt * 128:(tt + 1) * 128] if tt < 16 else pTS
            CP("act", dv, PS[:, b, 0:256].rearrange("p (c t) -> p c t", t=128)[:, :, 0:npart])
        for blk in range(4):
            wb = wload(w_pleg, blk * 256, 256)
            for mi in range(2):
                m = blk * 2 + mi
                for nt in range(5):
                    w_ = wd_(nt)
                    bg_ = nbank()
                    for kc in range(8):
                        MM(PS[:, bg_, 0:w_], wb[:, kc, mi * 128:(mi + 1) * 128], xsl(xn, xnS, kc, nt), kc == 0, kc == 7)
                    bp_ = nbank()
                    for kc in range(2):
                        MM(PS[:, bp_, 0:w_], wple_bf[:, kc, m * 128:(m + 1) * 128], xsl(pT, pTS, kc, nt), kc == 0, kc == 1)
                    ACT(sg2[:, 0:w_], PS[:, bg_, 0:w_], AF.Sigmoid)
                    TTo("dve", sg2[:, 0:w_], sg2[:, 0:w_], PS[:, bp_, 0:w_], ALU.mult)
                    xr = xsl(xres, xresS, m, nt)
                    TTo("dve", xr, xr, sg2[:, 0:w_], ALU.add)
        MARKS.append(("D_final", len(S.ops)))
        normD(GFC, xres, xresS)
        yo = st1["xsc"]
        for tt in range(17):
            npart = 128 if tt < 16 else NS
            sl = tt % 2
            for half in range(2):
                b = nbank()
                for c in range(4):
                    kc = half * 4 + c
                    src_ = xres[:, kc, tt * 128:(tt + 1) * 128] if tt < 16 else xresS[:, kc, :]
                    TR(PS[0:npart, b, c * 128:(c + 1) * 128], src_, ident)
                CP("act" if half == 0 else "dve", yo[0:npart, sl, half * 512:(half + 1) * 512], PS[0:npart, b, :])
            DMA("sp", y_p[tt * 128:(tt + 1) * 128, :] if tt < 16 else y_s, yo[0:npart, sl, :], f"yo{sl}", is_out=True)
    except _Stop:
        pass
    S.emit()
    es.close()
    return nc


OUT_SPECS = [
    ("y_p", (T, D)), ("y_s", (NS, D)), ("o_plconv", (3, D)), ("o_plh", (1, D)), ("o_pmconv", (3, D)),
    ("o_pC", (4, 256, 256)), ("o_pn", (4, 256)), ("o_pm", (1, 4)),
    ("o_slconv", (NS, 3, D)), ("o_slh", (NS, D)), ("o_smconv", (NS, 3, D)),
    ("o_sC", (NS, 4, 256, 256)), ("o_sn", (NS, 4, 256)), ("o_sm", (NS, 4)),
]

_NC_CACHE = []


def kernel(**inputs):
    f = lambda a: np.ascontiguousarray(np.asarray(a, dtype=np.float32))
    I = {k: f(v) for k, v in inputs.items()}
    if not _NC_CACHE:
        _NC_CACHE.append(build())
    nc = _NC_CACHE[0]
    wnames = ["norm_mix_g", "w_in", "b_gates", "lru_conv_w", "lru_conv_b", "lru_w_a", "lru_b_a", "lru_w_x", "lru_b_x",
              "lru_lambda", "mlstm_conv_w", "mlstm_conv_b", "w_q", "w_k", "w_v", "mlstm_norm_g", "w_out", "norm_ffn_g",
              "w_ffn_gate", "w_ffn_up", "w_ffn_down", "norm_ple_g", "w_ple_gate", "w_ple"]
    shared = {k: f(I[k][0]) for k in wnames}
    shared["final_norm_g"] = I["final_norm_g"]
    in_maps = []
    for c in range(8):
        sl = slice(c * NS, (c + 1) * NS)
        m = dict(shared)
        m["xp"] = f(I["x_prompt"][c])
        m["xs"] = f(I["x_sample"][sl, 0])
        m["st_lconv"] = f(I["state_lru_conv"][0, sl])
        m["st_lh"] = f(I["state_lru_h"][0, sl])
        m["st_mconv"] = f(I["state_mlstm_conv"][0, sl])
        m["st_C"] = f(I["state_mlstm_C"][0, sl])
        m["st_n"] = f(I["state_mlstm_n"][0, sl])
        m["st_m"] = f(I["state_mlstm_m"][0, sl])
        m["pp"] = f(I["p_prompt"][0, c])
        m["psm"] = f(I["p_sample"][0, sl, 0])
        in_maps.append(m)
    res = run_bass_kernel_spmd(nc, in_maps, core_ids=list(range(8)))
    R = res.results
    g = lambda name: [np.asarray(R[c][name], dtype=np.float32) for c in range(8)]
    y_prompt = np.stack(g("y_p"), 0)
    y_sample = np.concatenate(g("y_s"), 0)[:, None, :]
    p_lconv = np.stack(g("o_plconv"), 0)[None]
    p_lh = np.concatenate(g("o_plh"), 0)[None]
    p_mconv = np.stack(g("o_pmconv"), 0)[None]
    p_C = np.stack(g("o_pC"), 0)[None]
    p_n = np.stack(g("o_pn"), 0)[None]
    p_m = np.concatenate(g("o_pm"), 0)[None]
    s_lconv = np.concatenate(g("o_slconv"), 0)[None]
    s_lh = np.concatenate(g("o_slh"), 0)[None]
    s_mconv = np.concatenate(g("o_smconv"), 0)[None]
    s_C = np.concatenate(g("o_sC"), 0)[None]
    s_n = np.concatenate(g("o_sn"), 0)[None]
    s_m = np.concatenate(g("o_sm"), 0)[None]
    return (y_prompt, y_sample, p_lconv, p_lh, p_mconv, p_C, p_n, p_m, s_lconv, s_lh, s_mconv, s_C, s_n, s_m)
```

```python
from contextlib import ExitStack
import numpy as np
import concourse.bass as bass
import concourse.mybir as mybir
from concourse.bass_utils import run_bass_kernel_spmd

F32 = mybir.dt.float32
BF16 = mybir.dt.bfloat16
AF = mybir.ActivationFunctionType
ALU = mybir.AluOpType
AX = mybir.AxisListType

T = 2048
NS = 16
TT = T + NS
D = 1024
NIN = 5128
DFF = 2816
PD = 256
EPS = 1e-6
NEG = -1.0e30
ENGS = ("pe", "act", "dve", "pool", "sp")
BANK = 3000
SCHEDULE = True
MARKS = []
KNOB = {"pbanks": (0, 8), "prio": "cp", "lat": 400.0, "asq": "act", "ix": True, "ixeng": "dve", "nbev": "dve", "tset_aware": True, "tpen": 2500.0, "sqrt_explog": True}


def _esz(dt):
    return 2 if dt == BF16 else 4


def region(ap):
    steps = ap.ap
    esz = _esz(ap.dtype)
    name = ap.tensor.name
    if str(ap.space) == "DRAM" or "DRAM" in str(ap.space):
        ext = sum((c - 1) * abs(s) for s, c in steps) + 1
        return (name, 0, 1, ap.offset * esz, (ap.offset + ext) * esz)
    pstep, pcount = steps[0]
    if pstep == 0:
        p0, f0 = 0, ap.offset
        pcount = 128
    else:
        p0, f0 = ap.offset // pstep, ap.offset % pstep
    ext = sum((c - 1) * abs(s) for s, c in steps[1:]) + 1
    if name == "PS":
        return (name, 0, 128, (f0 * esz) // 2048 * 2048, ((f0 + ext) * esz + 2047) // 2048 * 2048)
    return (name, p0, p0 + pcount, f0 * esz, (f0 + ext) * esz)


class Op:
    __slots__ = ("eng", "fn", "deps", "seq", "needed", "ms", "stream", "scount", "waits", "idx", "cost", "xfer", "tset")


class Sched:
    def __init__(self, nc):
        self.nc = nc
        self.ops = []
        self.eng_ops = {e: [] for e in ENGS}
        self.acc = {}
        self.streams = {}
        self.out_dmas = []
        self.last_stream = {}

    def add(self, eng, fn, reads, writes, stream=None, is_out=False, cost=300.0, xfer=0.0):
        op = Op()
        op.idx = len(self.ops)
        op.cost = cost
        op.xfer = xfer
        op.tset = None
        op.eng = eng
        op.fn = fn
        op.needed = False
        op.stream = stream
        op.seq = len(self.eng_ops[eng])
        deps = set()
        key = eng if stream is None else ("dma", len(self.ops))
        if stream is not None:
            prev = self.last_stream.get(stream)
            if prev is not None:
                deps.add(prev)
            self.last_stream[stream] = op
        writes = list(writes) + [ap for ap in reads if ap.tensor.name == "PS"]
        reads = [ap for ap in reads if ap.tensor.name != "PS"]
        for ap in reads:
            name, p0, p1, b0, b1 = region(ap)
            recs = self.acc.setdefault(name, [])
            for r in recs:
                if r[4] and r[0] < p1 and p0 < r[1] and r[2] < b1 and b0 < r[3]:
                    deps.add(r[5])
            for r in recs:
                if (not r[4]) and r[6] == key and r[0] == p0 and r[1] == p1 and r[2] == b0 and r[3] == b1:
                    r[5].append(op)
                    break
            else:
                recs.append([p0, p1, b0, b1, False, [op], key])
        for ap in writes:
            name, p0, p1, b0, b1 = region(ap)
            recs = self.acc.setdefault(name, [])
            keep = []
            for r in recs:
                if r[0] < p1 and p0 < r[1] and r[2] < b1 and b0 < r[3]:
                    if r[4]:
                        if r[5] is not op:
                            deps.add(r[5])
                    else:
                        deps.update(r[5])
                    if p0 <= r[0] and r[1] <= p1 and b0 <= r[2] and r[3] <= b1:
                        continue
                keep.append(r)
            keep.append([p0, p1, b0, b1, True, op, key])
            self.acc[name] = keep
        deps.discard(op)
        op.deps = deps
        if stream is not None:
            self.streams[stream] = self.streams.get(stream, 0) + 16
            op.scount = self.streams[stream]
            if is_out:
                self.out_dmas.append(op)
        self.ops.append(op)
        self.eng_ops[eng].append(op)
        return op

    def schedule(self, window=1000):
        ops = self.ops
        n = len(ops)
        succs = [[] for _ in range(n)]
        npred = [0] * n
        for op in ops:
            npred[op.idx] = len(op.deps)
            for d in op.deps:
                succs[d.idx].append(op)
        rt = [0.0] * n
        fin = [0.0] * n
        lat = KNOB.get("lat", 0.0)
        cp = [0.0] * n
        if KNOB.get("prio", "idx") == "cp":
            for op in reversed(ops):
                m_ = 0.0
                for s in succs[op.idx]:
                    if cp[s.idx] > m_:
                        m_ = cp[s.idx]
                cp[op.idx] = m_ + op.cost + (op.xfer if op.stream is not None else 0.0)
        use_cp = KNOB.get("prio", "idx") == "cp"
        self.start_t = [0.0] * n
        self.fin_t = fin
        done = [False] * n
        avail = {e: [] for e in ENGS}
        for op in ops:
            if npred[op.idx] == 0:
                avail[op.eng].append(op)
        efree = {e: 0.0 for e in ENGS}
        dma_free = 0.0
        cur_set = None
        aware = KNOB.get("tset_aware", True)
        nsw = 0
        new_eng = {e: [] for e in ENGS}
        new_ops = []
        oldest = 0
        cnt = 0
        while cnt < n:
            while oldest < n and done[oldest]:
                oldest += 1
            lim = oldest + window
            best = None
            for e in ENGS:
                fe = efree[e]
                for op in avail[e]:
                    if op.idx > lim:
                        continue
                    r = rt[op.idx]
                    pk = -cp[op.idx] if use_cp else op.idx
                    pen = KNOB.get("tpen", ACT_SWITCH_NS) if (aware and op.tset is not None and op.tset != cur_set) else 0.0
                    key = (fe + pen, 0, pk) if r <= fe else (r + pen, 1, pk)
                    if best is None or key < best[0]:
                        best = (key, op)
            if best is None:
                cands = [op for e in ENGS for op in avail[e]]
                op = min(cands, key=lambda o: o.idx)
                best = ((max(rt[op.idx], efree[op.eng]), 0, op.idx), op)
            op = best[1]
            st = max(rt[op.idx], efree[op.eng])
            avail[op.eng].remove(op)
            if op.stream is not None:
                efree[op.eng] = st + op.cost
                ts = max(st + op.cost, dma_free)
                dma_free = ts + op.xfer * 0.6
                f = ts + 2000.0 + op.xfer
            else:
                f = st + op.cost
                if op.tset is not None and op.tset != cur_set:
                    f += ACT_SWITCH_NS
                    cur_set = op.tset
                    nsw += 1
                efree[op.eng] = f
            fin[op.idx] = f
            self.start_t[op.idx] = st
            done[op.idx] = True
            cnt += 1
            op.seq = len(new_eng[op.eng])
            new_eng[op.eng].append(op)
            new_ops.append(op)
            for s in succs[op.idx]:
                fl = f + (lat if s.eng != op.eng else 0.0)
                if fl > rt[s.idx]:
                    rt[s.idx] = fl
                npred[s.idx] -= 1
                if npred[s.idx] == 0:
                    avail[s.eng].append(s)
        self.ops = new_ops
        self.eng_ops = new_eng
        self.est_ns = max(fin) if n else 0.0
        self.n_switch = nsw

    def emit(self):
        nc = self.nc
        if SCHEDULE:
            self.schedule()
        fin = Op()
        fin.eng = "sp"
        fin.fn = None
        fin.needed = False
        fin.stream = None
        fin.seq = len(self.eng_ops["sp"])
        fin.deps = set(self.out_dmas)
        self.ops.append(fin)
        self.eng_ops["sp"].append(fin)
        wm = {e: {} for e in ENGS}
        for op in self.ops:
            e = op.eng
            best = {}
            waits = []
            for d in op.deps:
                if d.stream is not None:
                    k = ("s", d.stream)
                    if wm[e].get(k, 0) >= d.scount:
                        continue
                    if k not in best or best[k].scount < d.scount:
                        best[k] = d
                else:
                    if d.eng == e and e == "pe":
                        continue
                    if wm[e].get(d.eng, -1) >= d.seq:
                        continue
                    if d.eng not in best or best[d.eng].seq < d.seq:
                        best[d.eng] = d
            for k, d in best.items():
                if d.stream is not None:
                    wm[e][k] = d.scount
                else:
                    wm[e][k] = d.seq
                    d.needed = True
                waits.append(d)
            op.waits = waits
        nbanks = {}
        for e in ENGS:
            c = 0
            for op in self.eng_ops[e]:
                if op.needed:
                    op.ms = c
                    c += 1
            nbanks[e] = c // BANK + 1
        with ExitStack() as es:
            esem = {e: [es.enter_context(nc.semaphore(f"s_{e}{i}")) for i in range(nbanks[e])] for e in ENGS}
            ssem = {s: es.enter_context(nc.semaphore(f"d_{s}")) for s in self.streams}
            block = es.enter_context(nc.Block())

            def make(engname):
                def body(eng):
                    for op in self.eng_ops[engname]:
                        for d in op.waits:
                            if d.stream is not None:
                                eng.wait_ge(ssem[d.stream], d.scount)
                            else:
                                eng.wait_ge(esem[d.eng][d.ms // BANK], d.ms % BANK + 1)
                        if op.fn is None:
                            continue
                        ins = op.fn(eng)
                        if op.stream is not None:
                            ins.then_inc(ssem[op.stream], 16)
                        elif op.needed:
                            ins.then_inc(esem[engname][op.ms // BANK], 1)
                return body

            block.tensor(make("pe"))
            block.scalar(make("act"))
            block.vector(make("dve"))
            block.gpsimd(make("pool"))
            block.sync(make("sp"))


class Arena:
    def __init__(self, t):
        self.t = t
        self.top = 0

    def alloc(self, shape, dtype=F32, parts=128, at=None):
        n = int(np.prod(shape))
        nb = (n * _esz(dtype) + 63) // 64 * 64
        if at is None:
            at = self.top
            self.top += nb
            assert self.top <= self.t.shape[1] * 4, ("arena overflow", self.top)
        v = self.t[0:parts, at // 4:(at + nb) // 4]
        if dtype != F32:
            v = v.bitcast(dtype)
        v = v[:, 0:n]
        if len(shape) == 2:
            v = v.rearrange("p (a b) -> p a b", b=shape[1])
        elif len(shape) == 3:
            v = v.rearrange("p (a b c) -> p a b c", b=shape[1], c=shape[2])
        return v, at


class _Stop(Exception):
    pass


LEVEL = 99


ACT_TSET = {AF.Exp: "E", AF.Ln: "E", AF.Square: "E", AF.Abs: "E", AF.Sigmoid: "S", AF.Silu: "U", AF.Sqrt: "Q"}
ACT_SWITCH_NS = 1300.0


def build():
    nc = bass.Bass("TRN2", target_bir_lowering=False)

    def din(name, shape):
        return nc.dram_tensor(name, list(shape), F32, kind="ExternalInput").ap()

    def dout(name, shape):
        return nc.dram_tensor(name, list(shape), F32, kind="ExternalOutput").ap()

    xp = din("xp", (T, D)); xs = din("xs", (NS, D))
    st_lconv = din("st_lconv", (NS, 3, D)); st_lh = din("st_lh", (NS, D))
    st_mconv = din("st_mconv", (NS, 3, D)); st_C = din("st_C", (NS, 4, 256, 256))
    st_n = din("st_n", (NS, 4, 256)); st_m = din("st_m", (NS, 4))
    pp = din("pp", (T, PD)); psm = din("psm", (NS, PD))
    g_mix = din("norm_mix_g", (D,)); w_in = din("w_in", (D, NIN)); b_gates = din("b_gates", (8,))
    lru_cw = din("lru_conv_w", (4, D)); lru_cb = din("lru_conv_b", (D,))
    lru_wa = din("lru_w_a", (8, 128, 128)); lru_ba = din("lru_b_a", (D,))
    lru_wx = din("lru_w_x", (8, 128, 128)); lru_bx = din("lru_b_x", (D,))
    lru_lam = din("lru_lambda", (D,))
    m_cw = din("mlstm_conv_w", (4, D)); m_cb = din("mlstm_conv_b", (D,))
    w_q = din("w_q", (4, 256, 256)); w_k = din("w_k", (4, 256, 256)); w_v = din("w_v", (4, 256, 256))
    m_g = din("mlstm_norm_g", (4, 256))
    w_out = din("w_out", (D, D)); g_ffn = din("norm_ffn_g", (D,))
    w_gate = din("w_ffn_gate", (D, DFF)); w_up = din("w_ffn_up", (D, DFF)); w_down = din("w_ffn_down", (DFF, D))
    g_ple = din("norm_ple_g", (D,)); w_pleg = din("w_ple_gate", (D, D)); w_ple = din("w_ple", (PD, D))
    g_fin = din("final_norm_g", (D,))

    y_p = dout("y_p", (T, D)); y_s = dout("y_s", (NS, D))
    o_plconv = dout("o_plconv", (3, D)); o_plh = dout("o_plh", (1, D)); o_pmconv = dout("o_pmconv", (3, D))
    o_pC = dout("o_pC", (4, 256, 256)); o_pn = dout("o_pn", (4, 256)); o_pm = dout("o_pm", (1, 4))
    o_slconv = dout("o_slconv", (NS, 3, D)); o_slh = dout("o_slh", (NS, D)); o_smconv = dout("o_smconv", (NS, 3, D))
    o_sC = dout("o_sC", (NS, 4, 256, 256)); o_sn = dout("o_sn", (NS, 4, 256)); o_sm = dout("o_sm", (NS, 4))

    es = ExitStack()
    ARENA_BYTES = 207 * 1024
    At = es.enter_context(nc.sbuf_tensor("A", [128, ARENA_BYTES // 4], F32))
    PS = es.enter_context(nc.psum_tensor("PS", [128, 8, 512], F32))
    S = Sched(nc)
    AR = Arena(At)

    def aps(*xs_):
        return [x for x in xs_ if x is not None and not isinstance(x, (int, float))]

    def nfree(ap):
        return int(np.prod(ap.shape[1:]))

    def MM(out, lhsT, rhs, start, stop):
        rd = [lhsT, rhs] + ([] if start else [out])
        nf_ = nfree(rhs)
        c = (60.0 + 0.30 * nf_ * 4) if rhs.dtype == F32 else max(30.0, 30.0 + 0.42 * nf_, 110.0 if nf_ >= 64 else 0.0)
        return S.add("pe", lambda e: e.matmul(out, lhsT=lhsT, rhs=rhs, start=start, stop=stop), rd, [out], cost=c)

    def TR(out, in_, ident):
        return S.add("pe", lambda e: e.transpose(out, in_, ident), [in_, ident], [out], cost=200.0)

    def ACT(out, in_, func, bias=None, scale=None, accum_out=None):
        kw = {}
        if bias is not None:
            kw["bias"] = bias
        if scale is not None:
            kw["scale"] = scale
        if accum_out is not None:
            kw["accum_out"] = accum_out
        op_ = S.add("act", lambda e: e.activation(out, in_, func, **kw),
                    aps(in_, bias, scale), aps(out, accum_out), cost=220.0 + 0.75 * nfree(out))
        op_.tset = ACT_TSET.get(func)
        return op_

    def ENG(name):
        return {"dve": "dve", "pool": "pool"}[name]

    def TTo(eng, out, in0, in1, op):
        return S.add(eng, lambda e: e.tensor_tensor(out, in0, in1, op), [in0, in1], [out], cost=((100.0 + 1.0 * nfree(out)) if eng != "pool" else (500.0 + 2.0 * nfree(out))))

    def STT(eng, out, in0, scalar, in1, op0, op1):
        return S.add(eng, lambda e: e.scalar_tensor_tensor(out, in0, scalar, in1, op0, op1),
                     aps(in0, scalar, in1), [out], cost=((100.0 + 1.0 * nfree(out)) if eng != "pool" else (500.0 + 2.0 * nfree(out))))

    def TS(eng, out, in0, s1, s2, op0, op1=None):
        if op1 is None:
            return S.add(eng, lambda e: e.tensor_scalar(out, in0, s1, None, op0), aps(in0, s1), [out], cost=((100.0 + 1.0 * nfree(out)) if eng != "pool" else (500.0 + 2.0 * nfree(out))))
        return S.add(eng, lambda e: e.tensor_scalar(out, in0, s1, s2, op0, op1), aps(in0, s1, s2), [out], cost=((100.0 + 1.0 * nfree(out)) if eng != "pool" else (500.0 + 2.0 * nfree(out))))

    def CP(eng, out, in_):
        if eng == "act":
            return S.add("act", lambda e: e.copy(out, in_), [in_], [out], cost=220.0 + 0.75 * nfree(out))
        return S.add(eng, lambda e: e.tensor_copy(out, in_), [in_], [out], cost=((100.0 + 1.0 * nfree(out)) if eng != "pool" else (500.0 + 2.0 * nfree(out))))

    def MSET(eng, ap, val):
        return S.add(eng, lambda e: e.memset(ap, val), [], [ap], cost=((100.0 + 1.0 * nfree(ap)) if eng != "pool" else (500.0 + 2.0 * nfree(ap))))

    def SCAN(out, d0, d1, init, op0, op1):
        return S.add("dve", lambda e: e.tensor_tensor_scan(out, d0, d1, init, op0, op1), aps(d0, d1, init), [out],
                     cost=100.0 + 2.1 * nfree(out))

    def RECIP(out, in_):
        return S.add("dve", lambda e: e.reciprocal(out, in_), [in_], [out], cost=100.0 + 8.0 * nfree(out))

    def DMA(q, out, in_, stream, is_out=False, slow=False, cast=False):
        kw = {}
        if slow:
            kw["allow_slow_non_contiguous"] = True
        if cast:
            kw["max_dma_last_dim"] = 4096
        nbytes = int(np.prod(in_.shape)) * 4
        per_b = 0.008 if cast else (0.05 if slow else 0.004)
        return S.add(q, lambda e: e.dma_start(out=out, in_=in_, **kw), [in_], [out], stream=stream, is_out=is_out,
                     cost=(1000.0 if q == "pool" else 100.0), xfer=nbytes * per_b)

    def psb(bank, parts=128, n=512):
        return PS[0:parts, bank, 0:n]

    ident, _ = AR.alloc((128,), F32)
    maskneg, _ = AR.alloc((128,), F32)
    ones_bf, _ = AR.alloc((128,), BF16)
    esel, _ = AR.alloc((4, 128), F32, parts=4)
    cols, _ = AR.alloc((16, 8), F32)
    G1C, G2C, G3C, GFC, LCB, LBA, LBX, CCOL, MCB, MGC = [cols[:, i, :] for i in range(10)]
    LAMC = cols[:, 10, :]
    C2COL = cols[:, 11, :]
    lcw, _ = AR.alloc((4, 8), F32)
    mcw, _ = AR.alloc((4, 8), F32)
    bgc, _ = AR.alloc((2,), F32, parts=4)
    wa_bf, _ = AR.alloc((8, 128), BF16)
    wx_bf, _ = AR.alloc((8, 128), BF16)
    wq_bf, _ = AR.alloc((8, 256), BF16)
    wk_bf, _ = AR.alloc((8, 256), BF16)
    wv_bf, _ = AR.alloc((8, 256), BF16)
    xnS, _ = AR.alloc((8, NS), BF16)
    mrgS, _ = AR.alloc((8, NS), BF16)
    zT, zT_off = AR.alloc((41, NS), F32)
    zmap = []

    MSET("pool", ident, 1.0)
    S.add("pool", lambda e: e.affine_select(ident, ident, [[1, 128]], ALU.is_equal, 0.0, base=0, channel_multiplier=-1),
          [ident], [ident])
    idrep, _ = AR.alloc((NS, NS), F32)
    MSET("pool", idrep, 1.0)
    S.add("pool", lambda e: e.affine_select(idrep.rearrange("p a b -> p (a b)"), idrep.rearrange("p a b -> p (a b)"),
                                            [[1, NS], [-1, NS]], ALU.is_equal, 0.0, base=0, channel_multiplier=0),
          [idrep], [idrep])
    MSET("pool", maskneg, 0.0)
    S.add("pool", lambda e: e.affine_select(maskneg, maskneg, [[1, 128]], ALU.is_ge, -1.0e4, base=0, channel_multiplier=-1),
          [maskneg], [maskneg])
    MSET("dve", ones_bf, 1.0)
    for h in range(4):
        CP("dve", esel[:, h, :], ident[0:4, h:h + 1].broadcast_to([4, 128]))

    stg, _ = AR.alloc((128,), F32, parts=88, at=zT_off)
    stg2, _ = AR.alloc((128,), F32, parts=64, at=zT_off + 512)
    MSET("dve", stg, 0.0)
    for slot_, src_ in ((0, g_mix), (1, g_ffn), (2, g_ple), (3, g_fin), (4, lru_cb), (5, lru_ba), (6, lru_bx), (10, lru_lam), (8, m_cb)):
        DMA("sp", stg[8 * slot_:8 * slot_ + 8, :], src_.rearrange("(c p) -> c p", p=128), f"c{slot_}")
    DMA("sp", stg[72:80, :], m_g.rearrange("h (j p) -> (h j) p", p=128), "c9")
    DMA("sp", stg2[0:32, :], lru_cw.rearrange("j (c p) -> (j c) p", p=128), "c11")
    DMA("sp", stg2[32:64, :], m_cw.rearrange("j (c p) -> (j c) p", p=128), "c13")
    TR(PS[:, 0, 0:88], stg, ident[0:88, 0:88])
    CP("dve", cols[:, 0:11, :], PS[:, 0, 0:88].rearrange("p (s c) -> p s c", c=8))
    TR(PS[:, 1, 0:64], stg2, ident[0:64, 0:64])
    CP("dve", lcw, PS[:, 1, 0:32].rearrange("p (j c) -> p j c", c=8))
    CP("dve", mcw, PS[:, 1, 32:64].rearrange("p (j c) -> p j c", c=8))
    DMA("sp", bgc, b_gates.rearrange("(g h) -> h g", h=4), "c12", slow=True)
    DMA("pool", wa_bf, lru_wa.rearrange("n c d -> c n d"), "w_a", cast=True)
    DMA("pool", wx_bf, lru_wx.rearrange("n c d -> c n d"), "w_x", cast=True)
    DMA("pool", wq_bf, w_q.rearrange("h (j p) e -> p (h j) e", p=128), "w_q", cast=True)
    DMA("pool", wk_bf, w_k.rearrange("h (j p) e -> p (h j) e", p=128), "w_k", cast=True)
    DMA("pool", wv_bf, w_v.rearrange("h (j p) e -> p (h j) e", p=128), "w_v", cast=True)
    ACT(CCOL, LAMC, AF.Exp, scale=-1.0)
    ACT(CCOL, CCOL, AF.Ln, bias=1.0)
    TS("dve", CCOL, CCOL, -8.0, None, ALU.mult)
    TS("dve", C2COL, CCOL, 2.0, None, ALU.mult)

    WS_BYTES = 3 * 5632
    wreg, wreg_off = AR.alloc((WS_BYTES // 2,), BF16)
    wctr = [0]
    wcfg = {"n": 3}

    def wload(src2d, c0, w, kparts=8):
        n = wcfg["n"]
        sz = WS_BYTES // 2 // n
        assert kparts * w <= sz, (kparts, w, sz)
        s = wctr[0] % n
        wctr[0] += 1
        dst = wreg[:, s * sz:s * sz + kparts * w].rearrange("p (a b) -> p a b", b=w)
        DMA("pool", dst, src2d[:, c0:c0 + w].rearrange("(k p) n -> p k n", p=128), f"ws{n}_{s}", cast=True)
        return dst

    bctr = [0]

    def nbank(lo=0, hi=8):
        b = lo + bctr[0] % (hi - lo)
        bctr[0] += 1
        return b

    st1 = {}

    def alloc_stage1(n=2):
        st1["n"] = n
        st1["xin"], _ = AR.alloc((n, D), F32)
        st1["xsc"], _ = AR.alloc((n, D), F32)
        st1["nstat"], _ = AR.alloc((n, 4), F32)
        st1["sq"], _ = AR.alloc((D,), BF16)

    mk0 = AR.top
    alloc_stage1()

    def load_tiles(mode, dst, dstS, tiles):
        for tt in tiles:
            npart = 128 if tt < 16 else NS
            src_ = xp[tt * 128:(tt + 1) * 128, :] if tt < 16 else xs
            sl = tt % st1["n"]
            xin, xsc, nstat, sq_scr = st1["xin"], st1["xsc"], st1["nstat"], st1["sq"]
            xi = xin[0:npart, sl, :]
            DMA("sp", xi, src_, f"xin{sl}")
            tin = xi
            if mode == "norm":
                ss = nstat[0:npart, sl, 0:1]
                rs = nstat[0:npart, sl, 1:2]
                ACT(sq_scr[0:npart, :], xi, AF.Square, accum_out=ss)
                ACT(rs, ss, AF.Ln, bias=EPS, scale=1.0 / D)
                ACT(rs, rs, AF.Exp, scale=-0.5)
                tin = xsc[0:npart, sl, :]
                TS("dve", tin, xi, rs, None, ALU.mult)
            for half in range(2):
                b = nbank()
                for c in range(4):
                    kc = half * 4 + c
                    TR(PS[:, b, c * 128:c * 128 + npart], tin[:, kc * 128:(kc + 1) * 128], ident[0:npart, 0:npart])
                pv = PS[:, b, :].rearrange("p (c t) -> p c t", t=128)[:, :, 0:npart]
                if tt < 16:
                    dv = dst[:, half * 4:half * 4 + 4, tt * 128:tt * 128 + npart]
                else:
                    dv = dstS[:, half * 4:half * 4 + 4, :]
                if mode == "norm":
                    TTo("dve", dv, pv, G1C[:, half * 4:half * 4 + 4].unsqueeze(2).broadcast_to([128, 4, npart]), ALU.mult)
                else:
                    CP("act", dv, pv)

    load_tiles("norm", None, xnS, [16])

    def phase_S():
        AR.top = PB
        MARKS.append(("S", len(S.ops)))
        zs, _ = AR.alloc((NIN,), F32, parts=NS)
        for zi_, (zc_, c0_, w_) in enumerate(zmap):
            b = nbank()
            if w_ == 128:
                TR(PS[0:NS, b, 0:128], zT[:, zc_, :], ident)
            else:
                TR(PS[0:NS, b, 0:w_], zT[0:w_, zc_, :], ident[0:w_, 0:w_])
            CP("act" if zi_ % 2 == 0 else "dve", zs[:, c0_:c0_ + w_], PS[0:NS, b, 0:w_])

        bcn = [0]

        MARKS.append(("S_chain", len(S.ops)))
        def bcload(src_flat, n, t=None):
            if t is None:
                t, _ = AR.alloc((n,), F32, parts=NS)
            bcn[0] += 1
            DMA("sp", t, src_flat.partition_broadcast(NS), f"b{bcn[0]}")
            return t

        lam_b = bcload(lru_lam, D)
        bg_b = bcload(b_gates, 8)
        cwt, _ = AR.alloc((3, D), F32, parts=NS)
        rot = [0]

        def prow(src_flat):
            rot[0] += 1
            return bcload(src_flat, D, t=cwt[:, rot[0] % 3, :])

        cs_l, _ = AR.alloc((3, D), F32, parts=NS)
        cs_m = cs_l
        h0, _ = AR.alloc((D,), F32, parts=NS)
        n0, _ = AR.alloc((D,), F32, parts=NS)
        m0, _ = AR.alloc((4,), F32, parts=NS)
        DMA("sp", cs_l, st_lconv, "b20")
        DMA("sp", h0, st_lh, "b22")
        DMA("sp", n0, st_n.rearrange("b h e -> b (h e)"), "b23")
        DMA("sp", m0, st_m, "b24")
        ACT(lam_b, lam_b, AF.Exp, scale=-1.0)
        ACT(lam_b, lam_b, AF.Ln, bias=1.0)
        TS("dve", lam_b, lam_b, -8.0, None, ALU.mult)
        cc_b = lam_b

        ta, _ = AR.alloc((D,), F32, parts=NS)
        tb, _ = AR.alloc((D,), F32, parts=NS)
        tc_, _ = AR.alloc((D,), F32, parts=NS)
        td, _ = AR.alloc((D,), F32, parts=NS)
        te, _ = AR.alloc((D,), F32, parts=NS)
        tT, _ = AR.alloc((8, NS), BF16)
        tT2, _ = AR.alloc((8, NS), BF16)
        qTf, _ = AR.alloc((8, NS), F32)
        mS, _ = AR.alloc((D,), F32, parts=NS)
        sm, _ = AR.alloc((16, 4), F32, parts=NS)

        xl_s = zs[:, 0:D]
        xm_s = zs[:, D:2 * D]
        o_s = zs[:, 2 * D:3 * D]
        ig_s = zs[:, 3 * D:3 * D + 4]
        fg_s = zs[:, 3 * D + 4:3 * D + 8]
        gl_s = zs[:, 3 * D + 8:4 * D + 8]
        gm_s = zs[:, 4 * D + 8:5 * D + 8]

        def convS(out, xnew, cs, wsrc, bsrc):
            b_b = prow(bsrc)
            for j in range(4):
                wt = prow(wsrc[j, :])
                xj = cs[:, j, :] if j < 3 else xnew
                if j == 0:
                    TTo("dve", out, xj, wt, ALU.mult)
                    TTo("dve", out, out, b_b, ALU.add)
                else:
                    TTo("dve", ta, xj, wt, ALU.mult)
                    TTo("dve", out, out, ta, ALU.add)

        def transS(dstT, src_):
            b = nbank()
            for kc in range(8):
                TR(PS[:, b, kc * NS:(kc + 1) * NS], src_[:, kc * 128:(kc + 1) * 128], ident[0:NS, 0:NS])
            CP("dve", dstT, PS[:, b, 0:8 * NS].rearrange("p (c t) -> p c t", t=NS))

        def wideS(fn_mm):
            b0_, b1_ = nbank(), nbank()
            fn_mm(lambda col, w: PS[0:NS, b0_ if col < 512 else b1_, (col % 512):(col % 512) + w])
            return [PS[0:NS, b0_, :], PS[0:NS, b1_, :]]

        DMA("sp", o_slconv[:, 0:2, :], cs_l[:, 1:3, :], "o0", is_out=True)
        DMA("sp", o_slconv[:, 2, :], xl_s, "o1", is_out=True)
        convS(tb, xl_s, cs_l, lru_cw, lru_cb)
        transS(tT, tb)

        def mm_gate(wbf):
            def f(dst):
                for n in range(8):
                    MM(dst(n * 128, 128), tT[:, n, :], wbf[:, n, :], True, True)
            return f

        pr = wideS(mm_gate(wa_bf))
        pi = wideS(mm_gate(wx_bf))
        lba_b = prow(lru_ba)
        lbx_b = prow(lru_bx)
        for hh in range(2):
            sl_ = slice(hh * 512, (hh + 1) * 512)
            TTo("dve", tc_[:, sl_], pr[hh], lba_b[:, sl_], ALU.add)
            TTo("dve", td[:, sl_], pi[hh], lbx_b[:, sl_], ALU.add)
        ACT(tc_, tc_, AF.Sigmoid)
        ACT(td, td, AF.Sigmoid)
        TTo("dve", tc_, tc_, cc_b, ALU.mult)
        ACT(tc_, tc_, AF.Exp)
        TTo("dve", te, tc_, tc_, ALU.mult)
        ACT(te, te, AF.Sqrt, bias=1.0, scale=-1.0)
        TTo("dve", te, te, td, ALU.mult)
        TTo("dve", te, te, tb, ALU.mult)
        TTo("dve", tc_, tc_, h0, ALU.mult)
        TTo("dve", tc_, tc_, te, ALU.add)
        DMA("sp", o_slh, tc_, "o2", is_out=True)
        ACT(td, gl_s, AF.Sigmoid)
        TTo("dve", mS, td, tc_, ALU.mult)

        DMA("sp", cs_m, st_mconv, "b21")
        DMA("sp", o_smconv[:, 0:2, :], cs_m[:, 1:3, :], "o3", is_out=True)
        DMA("sp", o_smconv[:, 2, :], xm_s, "o4", is_out=True)
        convS(tb, xm_s, cs_m, m_cw, m_cb)
        ACT(tb, tb, AF.Silu)
        transS(tT, tb)
        transS(tT2, xm_s)
        qS, _ = AR.alloc((D,), F32, parts=NS)
        kS, _ = AR.alloc((D,), F32, parts=NS)
        vS, _ = AR.alloc((D,), F32, parts=NS)

        def mm_qkv(wbf, xT):
            def f(dst):
                for h in range(4):
                    for j in range(2):
                        MM(dst(h * 256, 256), xT[:, 2 * h + j, :], wbf[:, 2 * h + j, :], j == 0, j == 1)
            return f

        pq = wideS(mm_qkv(wq_bf, tT))
        for hh in range(2):
            ACT(qS[:, hh * 512:(hh + 1) * 512], pq[hh], AF.Copy, scale=1.0 / 16.0)
        pk = wideS(mm_qkv(wk_bf, tT))
        for hh in range(2):
            CP("dve", kS[:, hh * 512:(hh + 1) * 512], pk[hh])
        pv_ = wideS(mm_qkv(wv_bf, tT2))
        for hh in range(2):
            CP("act", vS[:, hh * 512:(hh + 1) * 512], pv_[hh])
        SM = lambda k: sm[:, k, :]
        TTo("dve", SM(0), ig_s, bg_b[:, 0:4], ALU.add)
        TTo("dve", SM(1), fg_s, bg_b[:, 4:8], ALU.add)
        ACT(SM(1), SM(1), AF.Exp, scale=-1.0)
        ACT(SM(1), SM(1), AF.Ln, bias=1.0)
        TTo("dve", SM(2), m0, SM(1), ALU.subtract)
        TTo("dve", SM(3), SM(2), SM(0), ALU.max)
        DMA("sp", o_sm, SM(3), "o5", is_out=True)
        TTo("dve", SM(4), SM(0), SM(3), ALU.subtract)
        ACT(SM(4), SM(4), AF.Exp)
        TTo("dve", SM(5), SM(2), SM(3), ALU.subtract)
        ACT(SM(5), SM(5), AF.Exp)
        ACT(SM(6), SM(3), AF.Exp, scale=-1.0)
        TTo("dve", ta, qS, kS, ALU.mult)
        S.add("dve", lambda e: e.tensor_reduce(SM(7), ta.rearrange("p (h e) -> p h e", e=256), AX.X, ALU.add),
              [ta], [SM(7)])
        TTo("dve", ta, qS, n0, ALU.mult)
        S.add("dve", lambda e: e.tensor_reduce(SM(8), ta.rearrange("p (h e) -> p h e", e=256), AX.X, ALU.add),
              [ta], [SM(8)])
        TTo("dve", SM(9), SM(7), SM(4), ALU.mult)
        TTo("dve", SM(10), SM(5), SM(8), ALU.mult)
        TTo("dve", SM(10), SM(10), SM(9), ALU.add)
        STT("dve", SM(14), SM(10), -1.0, SM(10), ALU.mult, ALU.max)
        TTo("dve", SM(10), SM(14), SM(6), ALU.max)
        RECIP(SM(11), SM(10))
        for h in range(4):
            hs = slice(h * 256, (h + 1) * 256)
            TS("dve", ta[:, hs], kS[:, hs], sm[:, 4, h:h + 1], None, ALU.mult)
            STT("dve", tb[:, hs], n0[:, hs], sm[:, 5, h:h + 1], ta[:, hs], ALU.mult, ALU.add)
        DMA("sp", o_sn.rearrange("b h e -> b (h e)"), tb, "o6", is_out=True)
        kwS = ta
        b = nbank()
        for kc in range(8):
            TR(PS[:, b, kc * NS:(kc + 1) * NS], qS[:, (kc // 2) * 256 + kc % 2:(kc // 2 + 1) * 256:2], ident[0:NS, 0:NS])
        CP("dve", qTf, PS[:, b, 0:8 * NS].rearrange("p (c t) -> p c t", t=NS))
        scd, _ = AR.alloc((NS, 4), F32, parts=NS)
        scbc, _ = AR.alloc((NS, 4), F32)
        ones16, _ = AR.alloc((128,), F32, parts=NS)
        MSET("dve", ones16, 1.0)
        TTo("dve", scd, sm[:, 5, :].unsqueeze(1).broadcast_to([NS, NS, 4]),
            ident[0:NS, 0:NS].unsqueeze(2).broadcast_to([NS, NS, 4]), ALU.mult)
        b = nbank()
        MM(PS[:, b, 0:64], ones16, scd.rearrange("p a b -> p (a b)"), True, True)
        CP("dve", scbc.rearrange("p a b -> p (a b)"), PS[:, b, 0:64])
        vbf, _ = AR.alloc((D,), BF16, parts=NS)
        CP("act", vbf, vS)
        qmk, _ = AR.alloc((2, 8, NS), F32)
        kmk, _ = AR.alloc((2, D), BF16, parts=NS)
        C0t, _ = AR.alloc((2, 8, 256), F32, at=mk0)
        Cnt, _ = AR.alloc((2, 8, 256), F32, at=mk0 + 16384)
        numS, _ = AR.alloc((D,), F32, parts=NS, at=wreg_off + 12288)
        for h in range(4):
            pass
        def c0_load(bsm):
            DMA("sp", C0t[:, bsm % 2, :, :].rearrange("p (h r) e -> p h (r e)", r=2),
                st_C[bsm].rearrange("h (p r) e -> p h (r e)", r=2), f"c0_{bsm % 2}")

        MARKS.append(("S_loop", len(S.ops)))
        c0_load(0)
        for bsm in range(NS):
            s3 = bsm % 2
            s2 = bsm % 2
            if bsm + 1 < NS:
                c0_load(bsm + 1)
            TTo("dve", qmk[:, s2, :, :], qTf, idrep[:, bsm:bsm + 1, :].broadcast_to([128, 8, NS]), ALU.mult)
            TS("dve", kmk[:, s2, :], kwS, ident[0:NS, bsm:bsm + 1], None, ALU.mult)
            for h in range(4):
                for j in range(2):
                    MM(PS[0:NS, 4 + h, 0:256], qmk[:, s2, 2 * h + j, :], C0t[:, s3, 2 * h + j, :],
                       bsm == 0 and j == 0, bsm == NS - 1 and j == 1)
            for h in range(4):
                bb = nbank(0, 4)
                for j in range(2):
                    MM(PS[:, bb, j * 256:(j + 1) * 256], kmk[:, s2, h * 256 + j:(h + 1) * 256:2], vbf[:, h * 256:(h + 1) * 256],
                       True, True)
                STT("dve", Cnt[:, s2, 2 * h:2 * h + 2, :], C0t[:, s3, 2 * h:2 * h + 2, :], scbc[:, bsm, h:h + 1],
                    PS[:, bb, :].rearrange("p (j e) -> p j e", e=256), ALU.mult, ALU.add)
            DMA(KNOB.get("cst_q", "act"), o_sC[bsm].rearrange("h (p r) e -> p h (r e)", r=2),
                Cnt[:, s2, :, :].rearrange("p (h r) e -> p h (r e)", r=2), f"oc{s2}", is_out=True)
        for h in range(4):
            CP("act", numS[:, h * 256:(h + 1) * 256], PS[0:NS, 4 + h, 0:256])
        for h in range(4):
            hs = slice(h * 256, (h + 1) * 256)
            TS("dve", numS[:, hs], numS[:, hs], sm[:, 5, h:h + 1], None, ALU.mult)
            STT("dve", numS[:, hs], vS[:, hs], sm[:, 9, h:h + 1], numS[:, hs], ALU.mult, ALU.add)
            TS("dve", numS[:, hs], numS[:, hs], sm[:, 11, h:h + 1], None, ALU.mult)
            ACT(tb[:, hs], numS[:, hs], AF.Square, accum_out=sm[:, 12, h:h + 1])
        ACT(SM(13), SM(12), AF.Ln, bias=EPS, scale=1.0 / 256.0)
        ACT(SM(13), SM(13), AF.Exp, scale=-0.5)
        mg_b = prow(m_g.rearrange("h e -> (h e)"))
        for h in range(4):
            hs = slice(h * 256, (h + 1) * 256)
            STT("dve", numS[:, hs], numS[:, hs], sm[:, 13, h:h + 1], mg_b[:, hs], ALU.mult, ALU.mult)
        ACT(td, o_s, AF.Sigmoid)
        TTo("dve", numS, numS, td, ALU.mult)
        ACT(td, gm_s, AF.Sigmoid)
        TTo("dve", numS, numS, td, ALU.mult)
        TTo("dve", mS, mS, numS, ALU.add)
        transS(mrgS, mS)

    try:
        MARKS.append(("P", len(S.ops)))
        AR.top = mk0
        xn, _ = AR.alloc((8, T), BF16)
        mrg, _ = AR.alloc((8, T), BF16)
        PB = AR.top
        alloc_stage1(KNOB.get("st1_slots", 4))
        load_tiles("norm", xn, None, list(range(16)))
        AR.top = PB
        gX2, _ = AR.alloc((T,), F32, parts=4)
        gX3, _ = AR.alloc((T,), F32, parts=4)
        gcol, _ = AR.alloc((16, 4), F32)
        nbf, _ = AR.alloc((2,), F32, parts=4)
        bufA, _ = AR.alloc((2, T + 4), F32)
        xc, _ = AR.alloc((2, T), BF16)
        qTb, _ = AR.alloc((2, T + 4), BF16)
        qT = qTb[:, :, 0:T]
        qsc, _ = AR.alloc((2, T), BF16)
        kT, _ = AR.alloc((2, T), BF16)
        kw, kwoff = AR.alloc((16, 256), BF16)
        gX1, _ = AR.alloc((T,), F32, parts=4, at=kwoff)
        vt, _ = AR.alloc((16, 258), BF16)
        DT, _ = AR.alloc((16, 128), BF16)
        LB0 = bufA
        Caug, _ = AR.alloc((2, 258), F32)
        Cb2, _ = AR.alloc((2, 2, 258), BF16)
        nb2, _ = AR.alloc((2, 2, 128), BF16)
        Gl, _ = AR.alloc((17,), F32)
        decb, _ = AR.alloc((16,), F32)
        wkc, _ = AR.alloc((16,), F32)
        dd2, _ = AR.alloc((2, 128), F32)
        hT2, _ = AR.alloc((2, 2, 128), F32)
        sqh2, _ = AR.alloc((2, 2, 128), BF16)
        rsh, _ = AR.alloc((128,), F32)
        sd, _ = AR.alloc((128,), BF16)
        dtmp, _ = AR.alloc((4, 128), F32)
        sg = dtmp.rearrange("p a b -> p (a b)")
        gm1, _ = AR.alloc((1,), F32, parts=4)
        xmp = bufA
        hn = bufA[:, :, 0:T]
        scb = bufA[:, 0, 0:T]
        xmb = qTb
        xmlast, _ = AR.alloc((2, 4), F32)
        xllast, _ = AR.alloc((4,), F32)
        dgm, _ = AR.alloc((2, 4, 128), BF16)
        dgl, _ = AR.alloc((4, 128), BF16)
        xcoff = region(xc)[3]
        emb = At[:, xcoff // 4:xcoff // 4 + T]
        ktoff = region(kT)[3]
        ctmp = At[:, ktoff // 4:ktoff // 4 + T]
        def _f32(off, n_):
            return At[:, off // 4:off // 4 + n_]
        oA, oQ, oK, oW, oV = (region(b_)[3] for b_ in (bufA, qsc, kT, kw, vt))
        xlc = _f32(oA, T)
        rr = _f32(oA + T * 4, T)
        ii = _f32(oQ, T)
        uu = _f32(oK, T)
        hh_ = _f32(oW, T)
        xlb = At[:, oV // 4:oV // 4 + (T + 4) // 2].bitcast(BF16)
        xlcb = At[:, (oV + (T + 4) * 2) // 4:(oV + (T + 4) * 2) // 4 + T // 2].bitcast(BF16)
        assert (T + 4) * 2 + T * 2 <= 16 * 258 * 2 and 2 * T * 4 <= 2 * (T + 4) * 4

        def sweepP(wb, w, evac, rhs, c0, nk=8):
            for mi in range(w // 128):
                b = nbank(*KNOB["pbanks"])
                for kc in range(nk):
                    MM(PS[:, b, 0:NS], wb[:, kc, mi * 128:(mi + 1) * 128], xnS[:, kc, :], kc == 0, kc == nk - 1)
                CP("act", zT[:, len(zmap), :], PS[:, b, 0:NS])
                zmap.append((len(zmap), c0 + mi * 128, 128))
                for nt in range(4):
                    b = nbank(*KNOB["pbanks"])
                    for kc in range(nk):
                        MM(PS[:, b, :], wb[:, kc, mi * 128:(mi + 1) * 128], rhs[:, kc, nt * 512:(nt + 1) * 512], kc == 0, kc == nk - 1)
                    evac(mi, nt, PS[:, b, :])

        TS("dve", nbf, bgc, -1.0, None, ALU.mult)
        wb = wload(w_in, 3 * D, 8)
        b = nbank(*KNOB["pbanks"])
        for kc in range(8):
            MM(PS[0:8, b, 0:NS], wb[:, kc, 0:8], xnS[:, kc, :], kc == 0, kc == 7)
        CP("act", zT[0:8, 40, :], PS[0:8, b, 0:NS])
        zmap_gates = (40, 3 * D, 8)
        for nt in range(4):
            ns = slice(nt * 512, (nt + 1) * 512)
            b = nbank(*KNOB["pbanks"])
            for kc in range(8):
                MM(PS[0:4, b, :], wb[:, kc, 0:4], xn[:, kc, ns], kc == 0, kc == 7)
            ACT(gX1[:, ns], PS[0:4, b, :], AF.Identity, bias=bgc[:, 0:1])
            b = nbank(*KNOB["pbanks"])
            for kc in range(8):
                MM(PS[0:4, b, :], wb[:, kc, 4:8], xn[:, kc, ns], kc == 0, kc == 7)
            ACT(gX2[:, ns], PS[0:4, b, :], AF.Exp, bias=nbf[:, 1:2], scale=-1.0)
        ACT(gX2, gX2, AF.Ln, bias=1.0)
        SCAN(gX3, gX2, gX2, 0.0, ALU.add, ALU.max)
        TTo("dve", gX1, gX1, gX3, ALU.add)
        SCAN(gX2, gX1, gX1, NEG, ALU.max, ALU.max)
        TTo("dve", gX3, gX3, gX2, ALU.subtract)
        TS("dve", gm1, gX3[:, T - 1:T], -1.0, None, ALU.mult)
        DMA("sp", o_pm.rearrange("o h -> h o"), gm1, "o7", is_out=True, slow=True)
        ACT(gX3, gX3, AF.Exp)
        b = nbank(*KNOB["pbanks"])
        for c in range(16):
            TR(PS[:, b, c * 4:(c + 1) * 4], gX1[:, c * 128:(c + 1) * 128], ident[0:4, 0:4])
        CP("dve", gcol, PS[:, b, 0:64].rearrange("p (c h) -> p c h", h=4))
        if LEVEL == 1:
            raise _Stop()
        gG, gEm = gX2, gX3

        for h in range(4):
            MARKS.append((f"P_h{h}_prep", len(S.ops)))
            MSET("dve", xmb[:, :, 0:4], 0.0)
            for mi in range(2):
                for j in range(4):
                    TS("dve", dgm[:, mi, j, :], ident, mcw[:, j, 2 * h + mi:2 * h + mi + 1], None, ALU.mult)
            wb = wload(w_in, D + h * 256, 256)

            def ev_xm(mi, nt, ps):
                CP("act", xmb[:, mi, 4 + nt * 512:4 + (nt + 1) * 512], ps)
                if nt == 3:
                    CP("dve", xmlast[:, mi, 0:3], ps[:, 509:512])
            sweepP(wb, 256, ev_xm, xn, D + h * 256)
            for mi in range(2):
                DMA("sp", o_pmconv[:, h * 256 + mi * 128:h * 256 + (mi + 1) * 128].rearrange("r p -> p r"), xmlast[:, mi, 0:3],
                    "o8", is_out=True, slow=True)
            MSET("dve", vt[:, :, 256:258], 1.0)
            for c2 in range(8):
                b = nbank(*KNOB["pbanks"])
                for cc in range(2):
                    c = 2 * c2 + cc
                    for dc in range(2):
                        MM(PS[:, b, cc * 256:(cc + 1) * 256], xmb[:, dc, 4 + c * 128:4 + (c + 1) * 128], wv_bf[:, 2 * h + dc, :], dc == 0, dc == 1)
                CP("act", vt[:, 2 * c2:2 * c2 + 2, 0:256], PS[:, b, :].rearrange("p (c e) -> p c e", e=256))
            for mi in range(2):
                kcg = 2 * h + mi
                for nt in range(4):
                    b = nbank(*KNOB["pbanks"])
                    for j in range(4):
                        MM(PS[:, b, :], dgm[:, mi, j, :], xmb[:, mi, nt * 512 + j + 1:nt * 512 + j + 513], j == 0, j == 3)
                    ACT(xc[:, mi, nt * 512:(nt + 1) * 512], PS[:, b, :], AF.Silu, bias=MCB[:, kcg:kcg + 1])
            if LEVEL == 2:
                raise _Stop()
            MSET("dve", Gl[:, 0:1], NEG)
            for nt in range(4):
                b = nbank(*KNOB["pbanks"])
                MM(PS[:, b, :], esel[:, h, :], gG[:, nt * 512:(nt + 1) * 512], True, True)
                CP("act", Gl[:, 1 + 4 * nt:5 + 4 * nt], PS[:, b, 127::128])
            TTo("dve", decb, Gl[:, 0:16], Gl[:, 1:17], ALU.subtract)
            ACT(decb, decb, AF.Exp)
            TTo("dve", wkc, gcol[:, :, h], Gl[:, 1:17], ALU.subtract)
            ACT(wkc, wkc, AF.Exp)
            for nt in range(4):
                b = nbank(*KNOB["pbanks"])
                MM(PS[:, b, :], esel[:, h, :], gG[:, nt * 512:(nt + 1) * 512], True, True)
                for cc in range(4):
                    c = 4 * nt + cc
                    ACT(scb[:, c * 128:(c + 1) * 128], PS[:, b, cc * 128:(cc + 1) * 128], AF.Exp, bias=Gl[:, c:c + 1], scale=-1.0)
                STT("dve", dtmp, PS[:, b, :].rearrange("p (c t) -> p c t", t=128), -1.0,
                    maskneg.unsqueeze(1).broadcast_to([128, 4, 128]), ALU.mult, ALU.add)
                for cc in range(4):
                    c = 4 * nt + cc
                    ACT(DT[:, c, :], dtmp[:, cc, :], AF.Exp, bias=gcol[:, c, h:h + 1])
            for j in range(2):
                for nt in range(4):
                    ns = slice(nt * 512, (nt + 1) * 512)
                    b = nbank(*KNOB["pbanks"])
                    for dc in range(2):
                        MM(PS[:, b, :], wk_bf[:, 2 * h + dc, j * 128:(j + 1) * 128], xc[:, dc, ns], dc == 0, dc == 1)
                    CP(KNOB.get("kev", "dve"), kT[:, j, ns], PS[:, b, :])
            for c2 in range(8):
                b = nbank(*KNOB["pbanks"])
                for cc in range(2):
                    c = 2 * c2 + cc
                    for dc in range(2):
                        MM(PS[:, b, cc * 256:(cc + 1) * 256], xc[:, dc, c * 128:(c + 1) * 128], wk_bf[:, 2 * h + dc, :], dc == 0, dc == 1)
                for cc in range(2):
                    c = 2 * c2 + cc
                    TS("dve", kw[:, c, :], PS[:, b, cc * 256:(cc + 1) * 256], wkc[:, c:c + 1], None, ALU.mult)
            for j in range(2):
                for nt in range(4):
                    ns = slice(nt * 512, (nt + 1) * 512)
                    b = nbank(*KNOB["pbanks"])
                    for dc in range(2):
                        MM(PS[:, b, :], wq_bf[:, 2 * h + dc, j * 128:(j + 1) * 128], xc[:, dc, ns], dc == 0, dc == 1)
                    ACT(qT[:, j, ns], PS[:, b, :], AF.Copy, scale=1.0 / 16.0)
                    STT("dve", qsc[:, j, ns], PS[:, b, :], 1.0 / 16.0, scb[:, ns], ALU.mult, ALU.mult)
            for nt in range(4):
                b = nbank(*KNOB["pbanks"])
                MM(PS[:, b, :], esel[:, h, :], gEm[:, nt * 512:(nt + 1) * 512], True, True)
                CP("act", emb[:, nt * 512:(nt + 1) * 512], PS[:, b, :])
            if LEVEL == 3:
                raise _Stop()
            MARKS.append((f"P_h{h}_loop", len(S.ops)))
            MSET("dve", Caug, 0.0)
            MSET("dve", Cb2, 0.0)
            MSET("dve", nb2, 0.0)

            def stageA(c):
                cs_ = slice(c * 128, (c + 1) * 128)
                nbk = 4 + c % 2
                k_ = c % 2
                for dc in range(2):
                    MM(PS[:, 6, dc * 256:(dc + 1) * 256], kw[:, c, dc * 128:(dc + 1) * 128], vt[:, c, 0:256], True, True)
                for dc in range(2):
                    MM(PS[:, 7, 256 + 2 * dc:258 + 2 * dc], kw[:, c, dc * 128:(dc + 1) * 128], vt[:, c, 256:258], True, True)
                for dc in range(2):
                    MM(PS[:, 3, 0:128], kT[:, dc, cs_], qT[:, dc, cs_], dc == 0, dc == 1)
                TTo("dve", sd, PS[:, 3, 0:128], DT[:, c, :], ALU.mult)
                Cbp = Cb2[:, 1 - k_, :, :]
                nbp = nb2[:, 1 - k_, :, :]
                for j in range(2):
                    MM(PS[:, nbk, j * 128:(j + 1) * 128], vt[:, c, j * 128:(j + 1) * 128], sd, True, False)
                    for dc in range(2):
                        MM(PS[:, nbk, j * 128:(j + 1) * 128], Cbp[:, dc, j * 128:(j + 1) * 128], qsc[:, dc, cs_], False, dc == 1)
                MM(PS[:, nbk, 256:384], ones_bf, sd, True, False)
                for dc in range(2):
                    MM(PS[:, nbk, 256:384], nbp[:, dc, :], qsc[:, dc, cs_], False, dc == 1)
                STT("dve", Caug[:, :, 0:256], Caug[:, :, 0:256], decb[:, c:c + 1],
                    PS[:, 6, :].rearrange("p (j e) -> p j e", e=256), ALU.mult, ALU.add)
                STT("dve", Caug[:, :, 256], Caug[:, :, 256], decb[:, c:c + 1], PS[:, 7, 256:260:2], ALU.mult, ALU.add)
                CP("act", Cb2[:, k_, :, :], Caug)
                CP(KNOB.get("nbev", "pool"), nb2[:, k_, :, :], Caug[:, :, 256:257].broadcast_to([128, 2, 128]))

            def stageB(c):
                cs_ = slice(c * 128, (c + 1) * 128)
                nbk = 4 + c % 2
                k_ = c % 2
                ACT(dd2[:, k_, :], PS[:, nbk, 256:384], AF.Abs)
                TTo("dve", dd2[:, k_, :], dd2[:, k_, :], emb[:, cs_], ALU.max)
                ACT(dd2[:, k_, :], dd2[:, k_, :], AF.Ln)
                ACT(dd2[:, k_, :], dd2[:, k_, :], AF.Exp, scale=-1.0)
                TTo("dve", hT2[:, k_, :, :], PS[:, nbk, 0:256].rearrange("p (j t) -> p j t", t=128),
                    dd2[:, k_, :].unsqueeze(1).broadcast_to([128, 2, 128]), ALU.mult)
                ACT(sqh2[:, k_, :, :], hT2[:, k_, :, :], AF.Square)

            def stageC(c):
                cs_ = slice(c * 128, (c + 1) * 128)
                k_ = c % 2
                for j in range(2):
                    MM(PS[:, 7, 0:128], ones_bf, sqh2[:, k_, j, :], j == 0, j == 1)
                ACT(rsh, PS[:, 7, 0:128], AF.Ln, bias=EPS, scale=1.0 / 256.0)
                ACT(rsh, rsh, AF.Exp, scale=-0.5)
                for j in range(2):
                    STT("dve", hn[:, j, cs_], hT2[:, k_, j, :], MGC[:, 2 * h + j:2 * h + j + 1], rsh, ALU.mult, ALU.mult)

            for i_ in range(16 + 2):
                if i_ < 16:
                    stageA(i_)
                if 0 <= i_ - 1 < 16:
                    stageB(i_ - 1)
                if 0 <= i_ - 2 < 16:
                    stageC(i_ - 2)
            DMA("sp", o_pC[h].rearrange("(j p) e -> p j e", p=128), Caug[:, :, 0:256], "o9", is_out=True)
            DMA("sp", o_pn[h].rearrange("(j p) -> p j", p=128), Caug[:, :, 256], "o10", is_out=True, slow=True)
            if LEVEL == 4:
                raise _Stop()
            MARKS.append((f"P_h{h}_gate", len(S.ops)))
            for gi_, c0 in enumerate((2 * D + h * 256, 4 * D + 8 + h * 256)):
                wb = wload(w_in, c0, 256)

                def ev_gate(mi, nt, ps, gi_=gi_):
                    ns = slice(nt * 512, (nt + 1) * 512)
                    sg_ = emb[:, ns]
                    ACT(sg_, ps, AF.Sigmoid)
                    if gi_ == 0:
                        TTo("dve", hn[:, mi, ns], hn[:, mi, ns], sg_, ALU.mult)
                    else:
                        TTo("dve", mrg[:, 2 * h + mi, ns], hn[:, mi, ns], sg_, ALU.mult)
                sweepP(wb, 256, ev_gate, xn, c0)
            if LEVEL == 5:
                raise _Stop()
            MARKS.append((f"P_h{h}_lru", len(S.ops)))
            for mi in range(2):
                n = 2 * h + mi
                MSET("dve", xlb[:, 0:4], 0.0)
                for j in range(4):
                    TS("dve", dgl[:, j, :], ident, lcw[:, j, n:n + 1], None, ALU.mult)
                wb = wload(w_in, n * 128, 128)

                def ev_xl(mi_, nt, ps):
                    CP("act", xlb[:, 4 + nt * 512:4 + (nt + 1) * 512], ps)
                    if nt == 3:
                        CP("dve", xllast[:, 0:3], ps[:, 509:512])
                sweepP(wb, 128, ev_xl, xn, n * 128)
                DMA("sp", o_plconv[:, n * 128:(n + 1) * 128].rearrange("r p -> p r"), xllast[:, 0:3], "o11", is_out=True, slow=True)
                for nt in range(4):
                    ns = slice(nt * 512, (nt + 1) * 512)
                    b = nbank(*KNOB["pbanks"])
                    for j in range(4):
                        MM(PS[:, b, :], dgl[:, j, :], xlb[:, nt * 512 + j + 1:nt * 512 + j + 513], j == 0, j == 3)
                    ACT(xlc[:, ns], PS[:, b, :], AF.Identity, bias=LCB[:, n:n + 1])
                    TS("dve", xlcb[:, ns], PS[:, b, :], LCB[:, n:n + 1], None, ALU.add)
                for nt in range(4):
                    ns = slice(nt * 512, (nt + 1) * 512)
                    b = nbank(*KNOB["pbanks"])
                    MM(PS[:, b, :], wa_bf[:, n, :], xlcb[:, ns], True, True)
                    ACT(rr[:, ns], PS[:, b, :], AF.Sigmoid, bias=LBA[:, n:n + 1])
                    b = nbank(*KNOB["pbanks"])
                    MM(PS[:, b, :], wx_bf[:, n, :], xlcb[:, ns], True, True)
                    ACT(ii[:, ns], PS[:, b, :], AF.Sigmoid, bias=LBX[:, n:n + 1])
                ACT(uu, rr, AF.Exp, scale=C2COL[:, n:n + 1])
                ACT(rr, rr, AF.Exp, scale=CCOL[:, n:n + 1])
                if KNOB.get("sqrt_explog", False):
                    ACT(uu, uu, AF.Ln, bias=1.0, scale=-1.0)
                    ACT(uu, uu, AF.Exp, scale=0.5)
                else:
                    ACT(uu, uu, AF.Sqrt, bias=1.0, scale=-1.0)
                if KNOB.get("ix", False):
                    TTo(KNOB.get("ixeng", "pool"), ii, ii, xlc, ALU.mult)
                    TTo("dve", uu, uu, ii, ALU.mult)
                else:
                    TTo("dve", uu, uu, ii, ALU.mult)
                    TTo("dve", uu, uu, xlc, ALU.mult)
                SCAN(hh_, rr, uu, 0.0, ALU.mult, ALU.add)
                DMA("sp", o_plh[0:1, n * 128:(n + 1) * 128].rearrange("o p -> p o"), hh_[:, T - 1:T], "o12", is_out=True, slow=True)
                wb = wload(w_in, 3 * D + 8 + n * 128, 128)

                def ev_gl(mi_, nt, ps, n=n, mi=mi):
                    ns = slice(nt * 512, (nt + 1) * 512)
                    sg_ = ii[:, ns]
                    ACT(sg_, ps, AF.Sigmoid)
                    TTo("dve", sg_, sg_, hh_[:, ns], ALU.mult)
                    TTo("dve", mrg[:, n, ns], mrg[:, n, ns], sg_, ALU.add)
                sweepP(wb, 128, ev_gl, xn, 3 * D + 8 + n * 128)

        if LEVEL == 6:
            raise _Stop()
        zmap.append(zmap_gates)
        phase_S()
        MARKS.append(("D", len(S.ops)))
        AR.top = PB
        alloc_stage1()
        xres, _ = AR.alloc((8, T), F32)
        xresS, _ = AR.alloc((8, NS), F32)
        sqn, sqoff = AR.alloc((8, 512), BF16)
        rst, _ = AR.alloc((512,), F32)
        sg2, _ = AR.alloc((512,), F32)
        pT, _ = AR.alloc((2, T), BF16, at=sqoff)
        pTS, _ = AR.alloc((2, NS), BF16)
        wple_bf, _ = AR.alloc((2, D), BF16)
        aTS, _ = AR.alloc((12, NS), BF16)
        moff = region(mrg)[3]
        aT = At[:, moff // 4:moff // 4 + 12 * T // 2].bitcast(BF16).rearrange("p (a b) -> p a b", b=T)
        assert moff + 12 * T * 2 <= region(xres)[3], (moff, region(xres))
        scrD = []
        for i_ in range(2):
            o_ = moff + i_ * 10240
            scrD.append((At[:, o_ // 4:o_ // 4 + 2048].bitcast(BF16).rearrange("p (a b) -> p a b", b=512),
                         At[:, (o_ + 8192) // 4:(o_ + 8192) // 4 + 512]))
        load_tiles("raw", xres, xresS, list(range(17)))

        def xsl(bP, bS, m, nt):
            return bP[:, m, nt * 512:(nt + 1) * 512] if nt < 4 else bS[:, m, :]

        def wd_(nt):
            return 512 if nt < 4 else NS

        for blk in range(4):
            wb = wload(w_out, blk * 256, 256)
            for mi in range(2):
                m = blk * 2 + mi
                for nt in range(5):
                    b = nbank()
                    for kc in range(8):
                        MM(PS[:, b, 0:wd_(nt)], wb[:, kc, mi * 128:(mi + 1) * 128], xsl(mrg, mrgS, kc, nt), kc == 0, kc == 7)
                    xr = xsl(xres, xresS, m, nt)
                    TTo("dve", xr, PS[:, b, 0:wd_(nt)], xr, ALU.add)

        def normD(gcols, dstP, dstS, scr=None, after_tile=None):
            for nt in range(5):
                w_ = wd_(nt)
                sq_, rs_ = (sqn, rst) if scr is None else scr[nt % len(scr)]
                for kc in range(8):
                    ACT(sq_[:, kc, 0:w_], xsl(xres, xresS, kc, nt), AF.Square)
                b = nbank()
                for kc in range(8):
                    MM(PS[:, b, 0:w_], ones_bf, sq_[:, kc, 0:w_], kc == 0, kc == 7)
                ACT(rs_[:, 0:w_], PS[:, b, 0:w_], AF.Ln, bias=EPS, scale=1.0 / D)
                ACT(rs_[:, 0:w_], rs_[:, 0:w_], AF.Exp, scale=-0.5)
                for kc in range(8):
                    STT("dve", xsl(dstP, dstS, kc, nt), xsl(xres, xresS, kc, nt), gcols[:, kc:kc + 1], rs_[:, 0:w_], ALU.mult, ALU.mult)
                if after_tile is not None:
                    after_tile(nt)

        MARKS.append(("D_norm2", len(S.ops)))
        normD(G2C, xn, xnS)
        MARKS.append(("D_ffn", len(S.ops)))
        for f0, nf in ((0, 12), (12, 10)):
            for fp in range(nf // 2):
                wg = wload(w_gate, (f0 + 2 * fp) * 128, 256)
                wu = wload(w_up, (f0 + 2 * fp) * 128, 256)
                for fj in range(2):
                    fi = 2 * fp + fj
                    for nt in range(5):
                        w_ = wd_(nt)
                        bg_ = nbank()
                        for kc in range(8):
                            MM(PS[:, bg_, 0:w_], wg[:, kc, fj * 128:(fj + 1) * 128], xsl(xn, xnS, kc, nt), kc == 0, kc == 7)
                        bu_ = nbank()
                        for kc in range(8):
                            MM(PS[:, bu_, 0:w_], wu[:, kc, fj * 128:(fj + 1) * 128], xsl(xn, xnS, kc, nt), kc == 0, kc == 7)
                        ACT(sg2[:, 0:w_], PS[:, bg_, 0:w_], AF.Silu)
                        TTo("dve", xsl(aT, aTS, fi, nt), sg2[:, 0:w_], PS[:, bu_, 0:w_], ALU.mult)
            for m in range(8):
                wdn = wload(w_down[f0 * 128:(f0 + nf) * 128, :], m * 128, 128, kparts=nf)
                for nt in range(5):
                    w_ = wd_(nt)
                    b = nbank()
                    for fi in range(nf):
                        MM(PS[:, b, 0:w_], wdn[:, fi, :], xsl(aT, aTS, fi, nt), fi == 0, fi == nf - 1)
                    xr = xsl(xres, xresS, m, nt)
                    TTo("dve", xr, PS[:, b, 0:w_], xr, ALU.add)
        MARKS.append(("D_norm3", len(S.ops)))
        normD(G3C, xn, xnS, scr=scrD)
        DMA("pool", wple_bf, w_ple.rearrange("(k p) n -> p k n", p=128), "w_ple", cast=True)
        for tt in range(17):
            npart = 128 if tt < 16 else NS
            sl = tt % 2
            pin = st1["xin"][0:npart, sl, 0:PD]
            DMA("sp", pin, pp[tt * 128:(tt + 1) * 128, :] if tt < 16 else psm, f"xin{sl}")
            b = nbank()
            for c in range(2):
                TR(PS[:, b, c * 128:c * 128 + npart], pin[:, c * 128:(c + 1) * 128], ident[0:npart, 0:npart])
            dv = pT[:, :, tt * 128:(tt + 1) * 128] if tt < 16 else pTS
            CP("act", dv, PS[:, b, 0:256].rearrange("p (c t) -> p c t", t=128)[:, :, 0:npart])
        for blk in range(4):
            wb = wload(w_pleg, blk * 256, 256)
            for nt, mi in [(nt, mi) for nt in range(5) for mi in range(2)]:
                if True:
                    m = blk * 2 + mi
                    w_ = wd_(nt)
                    bg_ = nbank()
                    for kc in range(8):
                        MM(PS[:, bg_, 0:w_], wb[:, kc, mi * 128:(mi + 1) * 128], xsl(xn, xnS, kc, nt), kc == 0, kc == 7)
                    bp_ = nbank()
                    for kc in range(2):
                        MM(PS[:, bp_, 0:w_], wple_bf[:, kc, m * 128:(m + 1) * 128], xsl(pT, pTS, kc, nt), kc == 0, kc == 1)
                    ACT(sg2[:, 0:w_], PS[:, bg_, 0:w_], AF.Sigmoid)
                    TTo("dve", sg2[:, 0:w_], sg2[:, 0:w_], PS[:, bp_, 0:w_], ALU.mult)
                    xr = xsl(xres, xresS, m, nt)
                    TTo("dve", xr, xr, sg2[:, 0:w_], ALU.add)
        MARKS.append(("D_final", len(S.ops)))
        yo = st1["xsc"]

        def out_tiles(nt):
            for tt in ([16] if nt == 4 else range(4 * nt, 4 * nt + 4)):
                npart = 128 if tt < 16 else NS
                sl = tt % 2
                for half in range(2):
                    b = nbank()
                    for c in range(4):
                        kc = half * 4 + c
                        src_ = xres[:, kc, tt * 128:(tt + 1) * 128] if tt < 16 else xresS[:, kc, :]
                        TR(PS[0:npart, b, c * 128:(c + 1) * 128], src_, ident)
                    CP("act" if half == 0 else "dve", yo[0:npart, sl, half * 512:(half + 1) * 512], PS[0:npart, b, :])
                DMA("sp", y_p[tt * 128:(tt + 1) * 128, :] if tt < 16 else y_s, yo[0:npart, sl, :], f"yo{sl}", is_out=True)

        normD(GFC, xres, xresS, scr=scrD, after_tile=out_tiles)
    except _Stop:
        pass
    S.emit()
    es.close()
    return nc


OUT_SPECS = [
    ("y_p", (T, D)), ("y_s", (NS, D)), ("o_plconv", (3, D)), ("o_plh", (1, D)), ("o_pmconv", (3, D)),
    ("o_pC", (4, 256, 256)), ("o_pn", (4, 256)), ("o_pm", (1, 4)),
    ("o_slconv", (NS, 3, D)), ("o_slh", (NS, D)), ("o_smconv", (NS, 3, D)),
    ("o_sC", (NS, 4, 256, 256)), ("o_sn", (NS, 4, 256)), ("o_sm", (NS, 4)),
]

_NC_CACHE = []


def kernel(**inputs):
    f = lambda a: np.ascontiguousarray(np.asarray(a, dtype=np.float32))
    I = {k: f(v) for k, v in inputs.items()}
    if not _NC_CACHE:
        _NC_CACHE.append(build())
    nc = _NC_CACHE[0]
    wnames = ["norm_mix_g", "w_in", "b_gates", "lru_conv_w", "lru_conv_b", "lru_w_a", "lru_b_a", "lru_w_x", "lru_b_x",
              "lru_lambda", "mlstm_conv_w", "mlstm_conv_b", "w_q", "w_k", "w_v", "mlstm_norm_g", "w_out", "norm_ffn_g",
              "w_ffn_gate", "w_ffn_up", "w_ffn_down", "norm_ple_g", "w_ple_gate", "w_ple"]
    shared = {k: f(I[k][0]) for k in wnames}
    shared["final_norm_g"] = I["final_norm_g"]
    in_maps = []
    for c in range(8):
        sl = slice(c * NS, (c + 1) * NS)
        m = dict(shared)
        m["xp"] = f(I["x_prompt"][c])
        m["xs"] = f(I["x_sample"][sl, 0])
        m["st_lconv"] = f(I["state_lru_conv"][0, sl])
        m["st_lh"] = f(I["state_lru_h"][0, sl])
        m["st_mconv"] = f(I["state_mlstm_conv"][0, sl])
        m["st_C"] = f(I["state_mlstm_C"][0, sl])
        m["st_n"] = f(I["state_mlstm_n"][0, sl])
        m["st_m"] = f(I["state_mlstm_m"][0, sl])
        m["pp"] = f(I["p_prompt"][0, c])
        m["psm"] = f(I["p_sample"][0, sl, 0])
        in_maps.append(m)
    res = run_bass_kernel_spmd(nc, in_maps, core_ids=list(range(8)))
    R = res.results
    g = lambda name: [np.asarray(R[c][name], dtype=np.float32) for c in range(8)]
    y_prompt = np.stack(g("y_p"), 0)
    y_sample = np.concatenate(g("y_s"), 0)[:, None, :]
    p_lconv = np.stack(g("o_plconv"), 0)[None]
    p_lh = np.concatenate(g("o_plh"), 0)[None]
    p_mconv = np.stack(g("o_pmconv"), 0)[None]
    p_C = np.stack(g("o_pC"), 0)[None]
    p_n = np.stack(g("o_pn"), 0)[None]
    p_m = np.concatenate(g("o_pm"), 0)[None]
    s_lconv = np.concatenate(g("o_slconv"), 0)[None]
    s_lh = np.concatenate(g("o_slh"), 0)[None]
    s_mconv = np.concatenate(g("o_smconv"), 0)[None]
    s_C = np.concatenate(g("o_sC"), 0)[None]
    s_n = np.concatenate(g("o_sn"), 0)[None]
    s_m = np.concatenate(g("o_sm"), 0)[None]
    return (y_prompt, y_sample, p_lconv, p_lh, p_mconv, p_C, p_n, p_m, s_lconv, s_lh, s_mconv, s_C, s_n, s_m)
```

```python
from contextlib import ExitStack
import numpy as np
import concourse.bass as bass
import concourse.mybir as mybir
from concourse.bass_utils import run_bass_kernel_spmd

F32 = mybir.dt.float32
BF16 = mybir.dt.bfloat16
AF = mybir.ActivationFunctionType
ALU = mybir.AluOpType
AX = mybir.AxisListType

T = 2048
NS = 16
TT = T + NS
D = 1024
NIN = 5128
DFF = 2816
PD = 256
EPS = 1e-6
NEG = -1.0e30
ENGS = ("pe", "act", "dve", "pool", "sp")
BANK = 3000
SCHEDULE = True
MARKS = []
KNOB = {"pbanks": (0, 8), "prio": "cp", "lat": 400.0, "asq": "act", "ix": True, "ixeng": "dve", "nbev": "dve", "tset_aware": True, "tpen": 2500.0, "sqrt_explog": True}


def _esz(dt):
    return 2 if dt == BF16 else 4


def region(ap):
    steps = ap.ap
    esz = _esz(ap.dtype)
    name = ap.tensor.name
    if str(ap.space) == "DRAM" or "DRAM" in str(ap.space):
        ext = sum((c - 1) * abs(s) for s, c in steps) + 1
        return (name, 0, 1, ap.offset * esz, (ap.offset + ext) * esz)
    pstep, pcount = steps[0]
    if pstep == 0:
        p0, f0 = 0, ap.offset
        pcount = 128
    else:
        p0, f0 = ap.offset // pstep, ap.offset % pstep
    ext = sum((c - 1) * abs(s) for s, c in steps[1:]) + 1
    if name == "PS":
        return (name, 0, 128, (f0 * esz) // 2048 * 2048, ((f0 + ext) * esz + 2047) // 2048 * 2048)
    return (name, p0, p0 + pcount, f0 * esz, (f0 + ext) * esz)


class Op:
    __slots__ = ("eng", "fn", "deps", "seq", "needed", "ms", "stream", "scount", "waits", "idx", "cost", "xfer", "tset")


class Sched:
    def __init__(self, nc):
        self.nc = nc
        self.ops = []
        self.eng_ops = {e: [] for e in ENGS}
        self.acc = {}
        self.streams = {}
        self.out_dmas = []
        self.last_stream = {}

    def add(self, eng, fn, reads, writes, stream=None, is_out=False, cost=300.0, xfer=0.0):
        op = Op()
        op.idx = len(self.ops)
        op.cost = cost
        op.xfer = xfer
        op.tset = None
        op.eng = eng
        op.fn = fn
        op.needed = False
        op.stream = stream
        op.seq = len(self.eng_ops[eng])
        deps = set()
        key = eng if stream is None else ("dma", len(self.ops))
        if stream is not None:
            prev = self.last_stream.get(stream)
            if prev is not None:
                deps.add(prev)
            self.last_stream[stream] = op
        writes = list(writes) + [ap for ap in reads if ap.tensor.name == "PS"]
        reads = [ap for ap in reads if ap.tensor.name != "PS"]
        for ap in reads:
            name, p0, p1, b0, b1 = region(ap)
            recs = self.acc.setdefault(name, [])
            for r in recs:
                if r[4] and r[0] < p1 and p0 < r[1] and r[2] < b1 and b0 < r[3]:
                    deps.add(r[5])
            for r in recs:
                if (not r[4]) and r[6] == key and r[0] == p0 and r[1] == p1 and r[2] == b0 and r[3] == b1:
                    r[5].append(op)
                    break
            else:
                recs.append([p0, p1, b0, b1, False, [op], key])
        for ap in writes:
            name, p0, p1, b0, b1 = region(ap)
            recs = self.acc.setdefault(name, [])
            keep = []
            for r in recs:
                if r[0] < p1 and p0 < r[1] and r[2] < b1 and b0 < r[3]:
                    if r[4]:
                        if r[5] is not op:
                            deps.add(r[5])
                    else:
                        deps.update(r[5])
                    if p0 <= r[0] and r[1] <= p1 and b0 <= r[2] and r[3] <= b1:
                        continue
                keep.append(r)
            keep.append([p0, p1, b0, b1, True, op, key])
            self.acc[name] = keep
        deps.discard(op)
        op.deps = deps
        if stream is not None:
            self.streams[stream] = self.streams.get(stream, 0) + 16
            op.scount = self.streams[stream]
            if is_out:
                self.out_dmas.append(op)
        self.ops.append(op)
        self.eng_ops[eng].append(op)
        return op

    def schedule(self, window=1000):
        ops = self.ops
        n = len(ops)
        succs = [[] for _ in range(n)]
        npred = [0] * n
        for op in ops:
            npred[op.idx] = len(op.deps)
            for d in op.deps:
                succs[d.idx].append(op)
        rt = [0.0] * n
        fin = [0.0] * n
        lat = KNOB.get("lat", 0.0)
        cp = [0.0] * n
        if KNOB.get("prio", "idx") == "cp":
            for op in reversed(ops):
                m_ = 0.0
                for s in succs[op.idx]:
                    if cp[s.idx] > m_:
                        m_ = cp[s.idx]
                cp[op.idx] = m_ + op.cost + (op.xfer if op.stream is not None else 0.0)
        use_cp = KNOB.get("prio", "idx") == "cp"
        self.start_t = [0.0] * n
        self.fin_t = fin
        done = [False] * n
        avail = {e: [] for e in ENGS}
        for op in ops:
            if npred[op.idx] == 0:
                avail[op.eng].append(op)
        efree = {e: 0.0 for e in ENGS}
        dma_free = 0.0
        cur_set = None
        aware = KNOB.get("tset_aware", True)
        nsw = 0
        new_eng = {e: [] for e in ENGS}
        new_ops = []
        oldest = 0
        cnt = 0
        while cnt < n:
            while oldest < n and done[oldest]:
                oldest += 1
            lim = oldest + window
            best = None
            for e in ENGS:
                fe = efree[e]
                for op in avail[e]:
                    if op.idx > lim:
                        continue
                    r = rt[op.idx]
                    pk = -cp[op.idx] if use_cp else op.idx
                    pen = KNOB.get("tpen", ACT_SWITCH_NS) if (aware and op.tset is not None and op.tset != cur_set) else 0.0
                    key = (fe + pen, 0, pk) if r <= fe else (r + pen, 1, pk)
                    if best is None or key < best[0]:
                        best = (key, op)
            if best is None:
                cands = [op for e in ENGS for op in avail[e]]
                op = min(cands, key=lambda o: o.idx)
                best = ((max(rt[op.idx], efree[op.eng]), 0, op.idx), op)
            op = best[1]
            st = max(rt[op.idx], efree[op.eng])
            avail[op.eng].remove(op)
            if op.stream is not None:
                efree[op.eng] = st + op.cost
                ts = max(st + op.cost, dma_free)
                dma_free = ts + op.xfer * 0.6
                f = ts + 2000.0 + op.xfer
            else:
                f = st + op.cost
                if op.tset is not None and op.tset != cur_set:
                    f += ACT_SWITCH_NS
                    cur_set = op.tset
                    nsw += 1
                efree[op.eng] = f
            fin[op.idx] = f
            self.start_t[op.idx] = st
            done[op.idx] = True
            cnt += 1
            op.seq = len(new_eng[op.eng])
            new_eng[op.eng].append(op)
            new_ops.append(op)
            for s in succs[op.idx]:
                fl = f + (lat if s.eng != op.eng else 0.0)
                if fl > rt[s.idx]:
                    rt[s.idx] = fl
                npred[s.idx] -= 1
                if npred[s.idx] == 0:
                    avail[s.eng].append(s)
        self.ops = new_ops
        self.eng_ops = new_eng
        self.est_ns = max(fin) if n else 0.0
        self.n_switch = nsw

    def emit(self):
        nc = self.nc
        if SCHEDULE:
            self.schedule()
        fin = Op()
        fin.eng = "sp"
        fin.fn = None
        fin.needed = False
        fin.stream = None
        fin.seq = len(self.eng_ops["sp"])
        fin.deps = set(self.out_dmas)
        self.ops.append(fin)
        self.eng_ops["sp"].append(fin)
        wm = {e: {} for e in ENGS}
        for op in self.ops:
            e = op.eng
            best = {}
            waits = []
            for d in op.deps:
                if d.stream is not None:
                    k = ("s", d.stream)
                    if wm[e].get(k, 0) >= d.scount:
                        continue
                    if k not in best or best[k].scount < d.scount:
                        best[k] = d
                else:
                    if d.eng == e and e == "pe":
                        continue
                    if wm[e].get(d.eng, -1) >= d.seq:
                        continue
                    if d.eng not in best or best[d.eng].seq < d.seq:
                        best[d.eng] = d
            for k, d in best.items():
                if d.stream is not None:
                    wm[e][k] = d.scount
                else:
                    wm[e][k] = d.seq
                    d.needed = True
                waits.append(d)
            op.waits = waits
        nbanks = {}
        for e in ENGS:
            c = 0
            for op in self.eng_ops[e]:
                if op.needed:
                    op.ms = c
                    c += 1
            nbanks[e] = c // BANK + 1
        with ExitStack() as es:
            esem = {e: [es.enter_context(nc.semaphore(f"s_{e}{i}")) for i in range(nbanks[e])] for e in ENGS}
            ssem = {s: es.enter_context(nc.semaphore(f"d_{s}")) for s in self.streams}
            block = es.enter_context(nc.Block())

            def make(engname):
                def body(eng):
                    for op in self.eng_ops[engname]:
                        for d in op.waits:
                            if d.stream is not None:
                                eng.wait_ge(ssem[d.stream], d.scount)
                            else:
                                eng.wait_ge(esem[d.eng][d.ms // BANK], d.ms % BANK + 1)
                        if op.fn is None:
                            continue
                        ins = op.fn(eng)
                        if op.stream is not None:
                            ins.then_inc(ssem[op.stream], 16)
                        elif op.needed:
                            ins.then_inc(esem[engname][op.ms // BANK], 1)
                return body

            block.tensor(make("pe"))
            block.scalar(make("act"))
            block.vector(make("dve"))
            block.gpsimd(make("pool"))
            block.sync(make("sp"))


class Arena:
    def __init__(self, t):
        self.t = t
        self.top = 0

    def alloc(self, shape, dtype=F32, parts=128, at=None):
        n = int(np.prod(shape))
        nb = (n * _esz(dtype) + 63) // 64 * 64
        if at is None:
            at = self.top
            self.top += nb
            assert self.top <= self.t.shape[1] * 4, ("arena overflow", self.top)
        v = self.t[0:parts, at // 4:(at + nb) // 4]
        if dtype != F32:
            v = v.bitcast(dtype)
        v = v[:, 0:n]
        if len(shape) == 2:
            v = v.rearrange("p (a b) -> p a b", b=shape[1])
        elif len(shape) == 3:
            v = v.rearrange("p (a b c) -> p a b c", b=shape[1], c=shape[2])
        return v, at


class _Stop(Exception):
    pass


LEVEL = 99


ACT_TSET = {AF.Exp: "E", AF.Ln: "E", AF.Square: "E", AF.Abs: "E", AF.Sigmoid: "S", AF.Silu: "U", AF.Sqrt: "Q"}
ACT_SWITCH_NS = 1300.0


def build():
    nc = bass.Bass("TRN2", target_bir_lowering=False)

    def din(name, shape):
        return nc.dram_tensor(name, list(shape), F32, kind="ExternalInput").ap()

    def dout(name, shape):
        return nc.dram_tensor(name, list(shape), F32, kind="ExternalOutput").ap()

    xp = din("xp", (T, D)); xs = din("xs", (NS, D))
    st_lconv = din("st_lconv", (NS, 3, D)); st_lh = din("st_lh", (NS, D))
    st_mconv = din("st_mconv", (NS, 3, D)); st_C = din("st_C", (NS, 4, 256, 256))
    st_n = din("st_n", (NS, 4, 256)); st_m = din("st_m", (NS, 4))
    pp = din("pp", (T, PD)); psm = din("psm", (NS, PD))
    g_mix = din("norm_mix_g", (D,)); w_in = din("w_in", (D, NIN)); b_gates = din("b_gates", (8,))
    lru_cw = din("lru_conv_w", (4, D)); lru_cb = din("lru_conv_b", (D,))
    lru_wa = din("lru_w_a", (8, 128, 128)); lru_ba = din("lru_b_a", (D,))
    lru_wx = din("lru_w_x", (8, 128, 128)); lru_bx = din("lru_b_x", (D,))
    lru_lam = din("lru_lambda", (D,))
    m_cw = din("mlstm_conv_w", (4, D)); m_cb = din("mlstm_conv_b", (D,))
    w_q = din("w_q", (4, 256, 256)); w_k = din("w_k", (4, 256, 256)); w_v = din("w_v", (4, 256, 256))
    m_g = din("mlstm_norm_g", (4, 256))
    w_out = din("w_out", (D, D)); g_ffn = din("norm_ffn_g", (D,))
    w_gate = din("w_ffn_gate", (D, DFF)); w_up = din("w_ffn_up", (D, DFF)); w_down = din("w_ffn_down", (DFF, D))
    g_ple = din("norm_ple_g", (D,)); w_pleg = din("w_ple_gate", (D, D)); w_ple = din("w_ple", (PD, D))
    g_fin = din("final_norm_g", (D,))

    y_p = dout("y_p", (T, D)); y_s = dout("y_s", (NS, D))
    o_plconv = dout("o_plconv", (3, D)); o_plh = dout("o_plh", (1, D)); o_pmconv = dout("o_pmconv", (3, D))
    o_pC = dout("o_pC", (4, 256, 256)); o_pn = dout("o_pn", (4, 256)); o_pm = dout("o_pm", (1, 4))
    o_slconv = dout("o_slconv", (NS, 3, D)); o_slh = dout("o_slh", (NS, D)); o_smconv = dout("o_smconv", (NS, 3, D))
    o_sC = dout("o_sC", (NS, 4, 256, 256)); o_sn = dout("o_sn", (NS, 4, 256)); o_sm = dout("o_sm", (NS, 4))

    es = ExitStack()
    ARENA_BYTES = 207 * 1024
    At = es.enter_context(nc.sbuf_tensor("A", [128, ARENA_BYTES // 4], F32))
    PS = es.enter_context(nc.psum_tensor("PS", [128, 8, 512], F32))
    S = Sched(nc)
    AR = Arena(At)

    def aps(*xs_):
        return [x for x in xs_ if x is not None and not isinstance(x, (int, float))]

    def nfree(ap):
        return int(np.prod(ap.shape[1:]))

    def MM(out, lhsT, rhs, start, stop):
        rd = [lhsT, rhs] + ([] if start else [out])
        nf_ = nfree(rhs)
        c = (60.0 + 0.30 * nf_ * 4) if rhs.dtype == F32 else max(30.0, 30.0 + 0.42 * nf_, 110.0 if nf_ >= 64 else 0.0)
        return S.add("pe", lambda e: e.matmul(out, lhsT=lhsT, rhs=rhs, start=start, stop=stop), rd, [out], cost=c)

    def TR(out, in_, ident):
        return S.add("pe", lambda e: e.transpose(out, in_, ident), [in_, ident], [out], cost=200.0)

    def ACT(out, in_, func, bias=None, scale=None, accum_out=None):
        kw = {}
        if bias is not None:
            kw["bias"] = bias
        if scale is not None:
            kw["scale"] = scale
        if accum_out is not None:
            kw["accum_out"] = accum_out
        op_ = S.add("act", lambda e: e.activation(out, in_, func, **kw),
                    aps(in_, bias, scale), aps(out, accum_out), cost=220.0 + 0.75 * nfree(out))
        op_.tset = ACT_TSET.get(func)
        return op_

    def ENG(name):
        return {"dve": "dve", "pool": "pool"}[name]

    def TTo(eng, out, in0, in1, op):
        return S.add(eng, lambda e: e.tensor_tensor(out, in0, in1, op), [in0, in1], [out], cost=((100.0 + 1.0 * nfree(out)) if eng != "pool" else (500.0 + 2.0 * nfree(out))))

    def STT(eng, out, in0, scalar, in1, op0, op1):
        return S.add(eng, lambda e: e.scalar_tensor_tensor(out, in0, scalar, in1, op0, op1),
                     aps(in0, scalar, in1), [out], cost=((100.0 + 1.0 * nfree(out)) if eng != "pool" else (500.0 + 2.0 * nfree(out))))

    def TS(eng, out, in0, s1, s2, op0, op1=None):
        if op1 is None:
            return S.add(eng, lambda e: e.tensor_scalar(out, in0, s1, None, op0), aps(in0, s1), [out], cost=((100.0 + 1.0 * nfree(out)) if eng != "pool" else (500.0 + 2.0 * nfree(out))))
        return S.add(eng, lambda e: e.tensor_scalar(out, in0, s1, s2, op0, op1), aps(in0, s1, s2), [out], cost=((100.0 + 1.0 * nfree(out)) if eng != "pool" else (500.0 + 2.0 * nfree(out))))

    def CP(eng, out, in_):
        if eng == "act":
            return S.add("act", lambda e: e.copy(out, in_), [in_], [out], cost=220.0 + 0.75 * nfree(out))
        return S.add(eng, lambda e: e.tensor_copy(out, in_), [in_], [out], cost=((100.0 + 1.0 * nfree(out)) if eng != "pool" else (500.0 + 2.0 * nfree(out))))

    def MSET(eng, ap, val):
        return S.add(eng, lambda e: e.memset(ap, val), [], [ap], cost=((100.0 + 1.0 * nfree(ap)) if eng != "pool" else (500.0 + 2.0 * nfree(ap))))

    def SCAN(out, d0, d1, init, op0, op1):
        return S.add("dve", lambda e: e.tensor_tensor_scan(out, d0, d1, init, op0, op1), aps(d0, d1, init), [out],
                     cost=100.0 + 2.1 * nfree(out))

    def RECIP(out, in_):
        return S.add("dve", lambda e: e.reciprocal(out, in_), [in_], [out], cost=100.0 + 8.0 * nfree(out))

    def DMA(q, out, in_, stream, is_out=False, slow=False, cast=False):
        kw = {}
        if slow:
            kw["allow_slow_non_contiguous"] = True
        if cast:
            kw["max_dma_last_dim"] = 4096
        nbytes = int(np.prod(in_.shape)) * 4
        per_b = 0.008 if cast else (0.05 if slow else 0.004)
        return S.add(q, lambda e: e.dma_start(out=out, in_=in_, **kw), [in_], [out], stream=stream, is_out=is_out,
                     cost=(1000.0 if q == "pool" else 100.0), xfer=nbytes * per_b)

    def psb(bank, parts=128, n=512):
        return PS[0:parts, bank, 0:n]

    ident, _ = AR.alloc((128,), F32)
    maskneg, _ = AR.alloc((128,), F32)
    ones_bf, _ = AR.alloc((128,), BF16)
    esel, _ = AR.alloc((4, 128), F32, parts=4)
    cols, _ = AR.alloc((16, 8), F32)
    G1C, G2C, G3C, GFC, LCB, LBA, LBX, CCOL, MCB, MGC = [cols[:, i, :] for i in range(10)]
    LAMC = cols[:, 10, :]
    C2COL = cols[:, 11, :]
    lcw, _ = AR.alloc((4, 8), F32)
    mcw, _ = AR.alloc((4, 8), F32)
    bgc, _ = AR.alloc((2,), F32, parts=4)
    wa_bf, _ = AR.alloc((8, 128), BF16)
    wx_bf, _ = AR.alloc((8, 128), BF16)
    wq_bf, _ = AR.alloc((8, 256), BF16)
    wk_bf, _ = AR.alloc((8, 256), BF16)
    wv_bf, _ = AR.alloc((8, 256), BF16)
    xnS, _ = AR.alloc((8, NS), BF16)
    mrgS, _ = AR.alloc((8, NS), BF16)
    zT, zT_off = AR.alloc((41, NS), F32)
    zmap = []

    MSET("pool", ident, 1.0)
    S.add("pool", lambda e: e.affine_select(ident, ident, [[1, 128]], ALU.is_equal, 0.0, base=0, channel_multiplier=-1),
          [ident], [ident])
    idrep, _ = AR.alloc((NS, NS), F32)
    MSET("pool", idrep, 1.0)
    S.add("pool", lambda e: e.affine_select(idrep.rearrange("p a b -> p (a b)"), idrep.rearrange("p a b -> p (a b)"),
                                            [[1, NS], [-1, NS]], ALU.is_equal, 0.0, base=0, channel_multiplier=0),
          [idrep], [idrep])
    MSET("pool", maskneg, 0.0)
    S.add("pool", lambda e: e.affine_select(maskneg, maskneg, [[1, 128]], ALU.is_ge, -1.0e4, base=0, channel_multiplier=-1),
          [maskneg], [maskneg])
    MSET("dve", ones_bf, 1.0)
    for h in range(4):
        CP("dve", esel[:, h, :], ident[0:4, h:h + 1].broadcast_to([4, 128]))

    stg, _ = AR.alloc((128,), F32, parts=88, at=zT_off)
    stg2, _ = AR.alloc((128,), F32, parts=64, at=zT_off + 512)
    MSET("dve", stg, 0.0)
    for slot_, src_ in ((0, g_mix), (1, g_ffn), (2, g_ple), (3, g_fin), (4, lru_cb), (5, lru_ba), (6, lru_bx), (10, lru_lam), (8, m_cb)):
        DMA("sp", stg[8 * slot_:8 * slot_ + 8, :], src_.rearrange("(c p) -> c p", p=128), f"c{slot_}")
    DMA("sp", stg[72:80, :], m_g.rearrange("h (j p) -> (h j) p", p=128), "c9")
    DMA("sp", stg2[0:32, :], lru_cw.rearrange("j (c p) -> (j c) p", p=128), "c11")
    DMA("sp", stg2[32:64, :], m_cw.rearrange("j (c p) -> (j c) p", p=128), "c13")
    TR(PS[:, 0, 0:88], stg, ident[0:88, 0:88])
    CP("dve", cols[:, 0:11, :], PS[:, 0, 0:88].rearrange("p (s c) -> p s c", c=8))
    TR(PS[:, 1, 0:64], stg2, ident[0:64, 0:64])
    CP("dve", lcw, PS[:, 1, 0:32].rearrange("p (j c) -> p j c", c=8))
    CP("dve", mcw, PS[:, 1, 32:64].rearrange("p (j c) -> p j c", c=8))
    DMA("sp", bgc, b_gates.rearrange("(g h) -> h g", h=4), "c12", slow=True)
    DMA("pool", wa_bf, lru_wa.rearrange("n c d -> c n d"), "w_a", cast=True)
    DMA("pool", wx_bf, lru_wx.rearrange("n c d -> c n d"), "w_x", cast=True)
    DMA("pool", wq_bf, w_q.rearrange("h (j p) e -> p (h j) e", p=128), "w_q", cast=True)
    DMA("pool", wk_bf, w_k.rearrange("h (j p) e -> p (h j) e", p=128), "w_k", cast=True)
    DMA("pool", wv_bf, w_v.rearrange("h (j p) e -> p (h j) e", p=128), "w_v", cast=True)
    ACT(CCOL, LAMC, AF.Exp, scale=-1.0)
    ACT(CCOL, CCOL, AF.Ln, bias=1.0)
    TS("dve", CCOL, CCOL, -8.0, None, ALU.mult)
    TS("dve", C2COL, CCOL, 2.0, None, ALU.mult)

    WS_BYTES = 3 * 5632
    wreg, wreg_off = AR.alloc((WS_BYTES // 2,), BF16)
    wctr = [0]
    wcfg = {"n": 3}

    def wload(src2d, c0, w, kparts=8):
        n = wcfg["n"]
        sz = WS_BYTES // 2 // n
        assert kparts * w <= sz, (kparts, w, sz)
        s = wctr[0] % n
        wctr[0] += 1
        dst = wreg[:, s * sz:s * sz + kparts * w].rearrange("p (a b) -> p a b", b=w)
        DMA("pool", dst, src2d[:, c0:c0 + w].rearrange("(k p) n -> p k n", p=128), f"ws{n}_{s}", cast=True)
        return dst

    bctr = [0]

    def nbank(lo=0, hi=8):
        b = lo + bctr[0] % (hi - lo)
        bctr[0] += 1
        return b

    st1 = {}

    def alloc_stage1(n=2):
        st1["n"] = n
        st1["xin"], _ = AR.alloc((n, D), F32)
        st1["xsc"], _ = AR.alloc((n, D), F32)
        st1["nstat"], _ = AR.alloc((n, 4), F32)
        st1["sq"], _ = AR.alloc((D,), BF16)

    mk0 = AR.top
    alloc_stage1()

    def load_tiles(mode, dst, dstS, tiles):
        for tt in tiles:
            npart = 128 if tt < 16 else NS
            src_ = xp[tt * 128:(tt + 1) * 128, :] if tt < 16 else xs
            sl = tt % st1["n"]
            xin, xsc, nstat, sq_scr = st1["xin"], st1["xsc"], st1["nstat"], st1["sq"]
            xi = xin[0:npart, sl, :]
            DMA("sp", xi, src_, f"xin{sl}")
            tin = xi
            if mode == "norm":
                ss = nstat[0:npart, sl, 0:1]
                rs = nstat[0:npart, sl, 1:2]
                ACT(sq_scr[0:npart, :], xi, AF.Square, accum_out=ss)
                ACT(rs, ss, AF.Ln, bias=EPS, scale=1.0 / D)
                ACT(rs, rs, AF.Exp, scale=-0.5)
                tin = xsc[0:npart, sl, :]
                TS("dve", tin, xi, rs, None, ALU.mult)
            for half in range(2):
                b = nbank()
                for c in range(4):
                    kc = half * 4 + c
                    TR(PS[:, b, c * 128:c * 128 + npart], tin[:, kc * 128:(kc + 1) * 128], ident[0:npart, 0:npart])
                pv = PS[:, b, :].rearrange("p (c t) -> p c t", t=128)[:, :, 0:npart]
                if tt < 16:
                    dv = dst[:, half * 4:half * 4 + 4, tt * 128:tt * 128 + npart]
                else:
                    dv = dstS[:, half * 4:half * 4 + 4, :]
                if mode == "norm":
                    TTo("dve", dv, pv, G1C[:, half * 4:half * 4 + 4].unsqueeze(2).broadcast_to([128, 4, npart]), ALU.mult)
                else:
                    CP("act", dv, pv)

    load_tiles("norm", None, xnS, [16])

    def phase_S():
        AR.top = PB
        MARKS.append(("S", len(S.ops)))
        zs, _ = AR.alloc((NIN,), F32, parts=NS)
        for zi_, (zc_, c0_, w_) in enumerate(zmap):
            b = nbank()
            if w_ == 128:
                TR(PS[0:NS, b, 0:128], zT[:, zc_, :], ident)
            else:
                TR(PS[0:NS, b, 0:w_], zT[0:w_, zc_, :], ident[0:w_, 0:w_])
            CP("act" if zi_ % 2 == 0 else "dve", zs[:, c0_:c0_ + w_], PS[0:NS, b, 0:w_])

        bcn = [0]

        MARKS.append(("S_chain", len(S.ops)))
        def bcload(src_flat, n, t=None):
            if t is None:
                t, _ = AR.alloc((n,), F32, parts=NS)
            bcn[0] += 1
            DMA("sp", t, src_flat.partition_broadcast(NS), f"b{bcn[0]}")
            return t

        lam_b = bcload(lru_lam, D)
        bg_b = bcload(b_gates, 8)
        cwt, _ = AR.alloc((3, D), F32, parts=NS)
        rot = [0]

        def prow(src_flat):
            rot[0] += 1
            return bcload(src_flat, D, t=cwt[:, rot[0] % 3, :])

        cs_l, _ = AR.alloc((3, D), F32, parts=NS)
        cs_m = cs_l
        h0, _ = AR.alloc((D,), F32, parts=NS)
        n0, _ = AR.alloc((D,), F32, parts=NS)
        m0, _ = AR.alloc((4,), F32, parts=NS)
        DMA("sp", cs_l, st_lconv, "b20")
        DMA("sp", h0, st_lh, "b22")
        DMA("sp", n0, st_n.rearrange("b h e -> b (h e)"), "b23")
        DMA("sp", m0, st_m, "b24")
        ACT(lam_b, lam_b, AF.Exp, scale=-1.0)
        ACT(lam_b, lam_b, AF.Ln, bias=1.0)
        TS("dve", lam_b, lam_b, -8.0, None, ALU.mult)
        cc_b = lam_b

        ta, _ = AR.alloc((D,), F32, parts=NS)
        tb, _ = AR.alloc((D,), F32, parts=NS)
        tc_, _ = AR.alloc((D,), F32, parts=NS)
        td, _ = AR.alloc((D,), F32, parts=NS)
        te, _ = AR.alloc((D,), F32, parts=NS)
        tT, _ = AR.alloc((8, NS), BF16)
        tT2, _ = AR.alloc((8, NS), BF16)
        qTf, _ = AR.alloc((8, NS), F32)
        mS, _ = AR.alloc((D,), F32, parts=NS)
        sm, _ = AR.alloc((16, 4), F32, parts=NS)

        xl_s = zs[:, 0:D]
        xm_s = zs[:, D:2 * D]
        o_s = zs[:, 2 * D:3 * D]
        ig_s = zs[:, 3 * D:3 * D + 4]
        fg_s = zs[:, 3 * D + 4:3 * D + 8]
        gl_s = zs[:, 3 * D + 8:4 * D + 8]
        gm_s = zs[:, 4 * D + 8:5 * D + 8]

        def convS(out, xnew, cs, wsrc, bsrc):
            b_b = prow(bsrc)
            for j in range(4):
                wt = prow(wsrc[j, :])
                xj = cs[:, j, :] if j < 3 else xnew
                if j == 0:
                    TTo("dve", out, xj, wt, ALU.mult)
                    TTo("dve", out, out, b_b, ALU.add)
                else:
                    TTo("dve", ta, xj, wt, ALU.mult)
                    TTo("dve", out, out, ta, ALU.add)

        def transS(dstT, src_):
            b = nbank()
            for kc in range(8):
                TR(PS[:, b, kc * NS:(kc + 1) * NS], src_[:, kc * 128:(kc + 1) * 128], ident[0:NS, 0:NS])
            CP("dve", dstT, PS[:, b, 0:8 * NS].rearrange("p (c t) -> p c t", t=NS))

        def wideS(fn_mm):
            b0_, b1_ = nbank(), nbank()
            fn_mm(lambda col, w: PS[0:NS, b0_ if col < 512 else b1_, (col % 512):(col % 512) + w])
            return [PS[0:NS, b0_, :], PS[0:NS, b1_, :]]

        DMA("sp", o_slconv[:, 0:2, :], cs_l[:, 1:3, :], "o0", is_out=True)
        DMA("sp", o_slconv[:, 2, :], xl_s, "o1", is_out=True)
        convS(tb, xl_s, cs_l, lru_cw, lru_cb)
        transS(tT, tb)

        def mm_gate(wbf):
            def f(dst):
                for n in range(8):
                    MM(dst(n * 128, 128), tT[:, n, :], wbf[:, n, :], True, True)
            return f

        pr = wideS(mm_gate(wa_bf))
        pi = wideS(mm_gate(wx_bf))
        lba_b = prow(lru_ba)
        lbx_b = prow(lru_bx)
        for hh in range(2):
            sl_ = slice(hh * 512, (hh + 1) * 512)
            TTo("dve", tc_[:, sl_], pr[hh], lba_b[:, sl_], ALU.add)
            TTo("dve", td[:, sl_], pi[hh], lbx_b[:, sl_], ALU.add)
        ACT(tc_, tc_, AF.Sigmoid)
        ACT(td, td, AF.Sigmoid)
        TTo("dve", tc_, tc_, cc_b, ALU.mult)
        ACT(tc_, tc_, AF.Exp)
        TTo("dve", te, tc_, tc_, ALU.mult)
        ACT(te, te, AF.Sqrt, bias=1.0, scale=-1.0)
        TTo("dve", te, te, td, ALU.mult)
        TTo("dve", te, te, tb, ALU.mult)
        TTo("dve", tc_, tc_, h0, ALU.mult)
        TTo("dve", tc_, tc_, te, ALU.add)
        DMA("sp", o_slh, tc_, "o2", is_out=True)
        ACT(td, gl_s, AF.Sigmoid)
        TTo("dve", mS, td, tc_, ALU.mult)

        DMA("sp", cs_m, st_mconv, "b21")
        DMA("sp", o_smconv[:, 0:2, :], cs_m[:, 1:3, :], "o3", is_out=True)
        DMA("sp", o_smconv[:, 2, :], xm_s, "o4", is_out=True)
        convS(tb, xm_s, cs_m, m_cw, m_cb)
        ACT(tb, tb, AF.Silu)
        transS(tT, tb)
        transS(tT2, xm_s)
        qS, _ = AR.alloc((D,), F32, parts=NS)
        kS, _ = AR.alloc((D,), F32, parts=NS)
        vS, _ = AR.alloc((D,), F32, parts=NS)

        def mm_qkv(wbf, xT):
            def f(dst):
                for h in range(4):
                    for j in range(2):
                        MM(dst(h * 256, 256), xT[:, 2 * h + j, :], wbf[:, 2 * h + j, :], j == 0, j == 1)
            return f

        pq = wideS(mm_qkv(wq_bf, tT))
        for hh in range(2):
            ACT(qS[:, hh * 512:(hh + 1) * 512], pq[hh], AF.Copy, scale=1.0 / 16.0)
        pk = wideS(mm_qkv(wk_bf, tT))
        for hh in range(2):
            CP("dve", kS[:, hh * 512:(hh + 1) * 512], pk[hh])
        pv_ = wideS(mm_qkv(wv_bf, tT2))
        for hh in range(2):
            CP("act", vS[:, hh * 512:(hh + 1) * 512], pv_[hh])
        SM = lambda k: sm[:, k, :]
        TTo("dve", SM(0), ig_s, bg_b[:, 0:4], ALU.add)
        TTo("dve", SM(1), fg_s, bg_b[:, 4:8], ALU.add)
        ACT(SM(1), SM(1), AF.Exp, scale=-1.0)
        ACT(SM(1), SM(1), AF.Ln, bias=1.0)
        TTo("dve", SM(2), m0, SM(1), ALU.subtract)
        TTo("dve", SM(3), SM(2), SM(0), ALU.max)
        DMA("sp", o_sm, SM(3), "o5", is_out=True)
        TTo("dve", SM(4), SM(0), SM(3), ALU.subtract)
        ACT(SM(4), SM(4), AF.Exp)
        TTo("dve", SM(5), SM(2), SM(3), ALU.subtract)
        ACT(SM(5), SM(5), AF.Exp)
        ACT(SM(6), SM(3), AF.Exp, scale=-1.0)
        TTo("dve", ta, qS, kS, ALU.mult)
        S.add("dve", lambda e: e.tensor_reduce(SM(7), ta.rearrange("p (h e) -> p h e", e=256), AX.X, ALU.add),
              [ta], [SM(7)])
        TTo("dve", ta, qS, n0, ALU.mult)
        S.add("dve", lambda e: e.tensor_reduce(SM(8), ta.rearrange("p (h e) -> p h e", e=256), AX.X, ALU.add),
              [ta], [SM(8)])
        TTo("dve", SM(9), SM(7), SM(4), ALU.mult)
        TTo("dve", SM(10), SM(5), SM(8), ALU.mult)
        TTo("dve", SM(10), SM(10), SM(9), ALU.add)
        STT("dve", SM(14), SM(10), -1.0, SM(10), ALU.mult, ALU.max)
        TTo("dve", SM(10), SM(14), SM(6), ALU.max)
        RECIP(SM(11), SM(10))
        for h in range(4):
            hs = slice(h * 256, (h + 1) * 256)
            TS("dve", ta[:, hs], kS[:, hs], sm[:, 4, h:h + 1], None, ALU.mult)
            STT("dve", tb[:, hs], n0[:, hs], sm[:, 5, h:h + 1], ta[:, hs], ALU.mult, ALU.add)
        DMA("sp", o_sn.rearrange("b h e -> b (h e)"), tb, "o6", is_out=True)
        kwS = ta
        b = nbank()
        for kc in range(8):
            TR(PS[:, b, kc * NS:(kc + 1) * NS], qS[:, (kc // 2) * 256 + kc % 2:(kc // 2 + 1) * 256:2], ident[0:NS, 0:NS])
        CP("dve", qTf, PS[:, b, 0:8 * NS].rearrange("p (c t) -> p c t", t=NS))
        scd, _ = AR.alloc((NS, 4), F32, parts=NS)
        scbc, _ = AR.alloc((NS, 4), F32)
        ones16, _ = AR.alloc((128,), F32, parts=NS)
        MSET("dve", ones16, 1.0)
        TTo("dve", scd, sm[:, 5, :].unsqueeze(1).broadcast_to([NS, NS, 4]),
            ident[0:NS, 0:NS].unsqueeze(2).broadcast_to([NS, NS, 4]), ALU.mult)
        b = nbank()
        MM(PS[:, b, 0:64], ones16, scd.rearrange("p a b -> p (a b)"), True, True)
        CP("dve", scbc.rearrange("p a b -> p (a b)"), PS[:, b, 0:64])
        vbf, _ = AR.alloc((D,), BF16, parts=NS)
        CP("act", vbf, vS)
        qmk, _ = AR.alloc((2, 8, NS), F32)
        kmk, _ = AR.alloc((2, D), BF16, parts=NS)
        C0t, _ = AR.alloc((2, 8, 256), F32, at=mk0)
        Cnt, _ = AR.alloc((2, 8, 256), F32, at=mk0 + 16384)
        numS, _ = AR.alloc((D,), F32, parts=NS, at=wreg_off + 12288)
        for h in range(4):
            pass
        def c0_load(bsm):
            DMA("sp", C0t[:, bsm % 2, :, :].rearrange("p (h r) e -> p h (r e)", r=2),
                st_C[bsm].rearrange("h (p r) e -> p h (r e)", r=2), f"c0_{bsm % 2}")

        MARKS.append(("S_loop", len(S.ops)))
        c0_load(0)
        for bsm in range(NS):
            s3 = bsm % 2
            s2 = bsm % 2
            if bsm + 1 < NS:
                c0_load(bsm + 1)
            TTo("dve", qmk[:, s2, :, :], qTf, idrep[:, bsm:bsm + 1, :].broadcast_to([128, 8, NS]), ALU.mult)
            TS("dve", kmk[:, s2, :], kwS, ident[0:NS, bsm:bsm + 1], None, ALU.mult)
            for h in range(4):
                for j in range(2):
                    MM(PS[0:NS, 4 + h, 0:256], qmk[:, s2, 2 * h + j, :], C0t[:, s3, 2 * h + j, :],
                       bsm == 0 and j == 0, bsm == NS - 1 and j == 1)
            for h in range(4):
                bb = nbank(0, 4)
                for j in range(2):
                    MM(PS[:, bb, j * 256:(j + 1) * 256], kmk[:, s2, h * 256 + j:(h + 1) * 256:2], vbf[:, h * 256:(h + 1) * 256],
                       True, True)
                STT("dve", Cnt[:, s2, 2 * h:2 * h + 2, :], C0t[:, s3, 2 * h:2 * h + 2, :], scbc[:, bsm, h:h + 1],
                    PS[:, bb, :].rearrange("p (j e) -> p j e", e=256), ALU.mult, ALU.add)
            DMA(KNOB.get("cst_q", "act"), o_sC[bsm].rearrange("h (p r) e -> p h (r e)", r=2),
                Cnt[:, s2, :, :].rearrange("p (h r) e -> p h (r e)", r=2), f"oc{s2}", is_out=True)
        for h in range(4):
            CP("act", numS[:, h * 256:(h + 1) * 256], PS[0:NS, 4 + h, 0:256])
        for h in range(4):
            hs = slice(h * 256, (h + 1) * 256)
            TS("dve", numS[:, hs], numS[:, hs], sm[:, 5, h:h + 1], None, ALU.mult)
            STT("dve", numS[:, hs], vS[:, hs], sm[:, 9, h:h + 1], numS[:, hs], ALU.mult, ALU.add)
            TS("dve", numS[:, hs], numS[:, hs], sm[:, 11, h:h + 1], None, ALU.mult)
            ACT(tb[:, hs], numS[:, hs], AF.Square, accum_out=sm[:, 12, h:h + 1])
        ACT(SM(13), SM(12), AF.Ln, bias=EPS, scale=1.0 / 256.0)
        ACT(SM(13), SM(13), AF.Exp, scale=-0.5)
        mg_b = prow(m_g.rearrange("h e -> (h e)"))
        for h in range(4):
            hs = slice(h * 256, (h + 1) * 256)
            STT("dve", numS[:, hs], numS[:, hs], sm[:, 13, h:h + 1], mg_b[:, hs], ALU.mult, ALU.mult)
        ACT(td, o_s, AF.Sigmoid)
        TTo("dve", numS, numS, td, ALU.mult)
        ACT(td, gm_s, AF.Sigmoid)
        TTo("dve", numS, numS, td, ALU.mult)
        TTo("dve", mS, mS, numS, ALU.add)
        transS(mrgS, mS)

    try:
        MARKS.append(("P", len(S.ops)))
        AR.top = mk0
        xn, _ = AR.alloc((8, T), BF16)
        mrg, _ = AR.alloc((8, T), BF16)
        PB = AR.top
        alloc_stage1(KNOB.get("st1_slots", 4))
        load_tiles("norm", xn, None, list(range(16)))
        AR.top = PB
        gX2, _ = AR.alloc((T,), F32, parts=4)
        gX3, _ = AR.alloc((T,), F32, parts=4)
        gcol, _ = AR.alloc((16, 4), F32)
        nbf, _ = AR.alloc((2,), F32, parts=4)
        bufA, _ = AR.alloc((2, T + 4), F32)
        xc, _ = AR.alloc((2, T), BF16)
        qTb, _ = AR.alloc((2, T + 4), BF16)
        qT = qTb[:, :, 0:T]
        qsc, _ = AR.alloc((2, T), BF16)
        kT, _ = AR.alloc((2, T), BF16)
        kw, kwoff = AR.alloc((16, 256), BF16)
        gX1, _ = AR.alloc((T,), F32, parts=4, at=kwoff)
        vt, _ = AR.alloc((16, 258), BF16)
        DT, _ = AR.alloc((16, 128), BF16)
        LB0 = bufA
        Caug, _ = AR.alloc((2, 258), F32)
        Cb2, _ = AR.alloc((2, 2, 258), BF16)
        nb2, _ = AR.alloc((2, 2, 128), BF16)
        Gl, _ = AR.alloc((17,), F32)
        decb, _ = AR.alloc((16,), F32)
        wkc, _ = AR.alloc((16,), F32)
        dd2, _ = AR.alloc((2, 128), F32)
        hT2, _ = AR.alloc((2, 2, 128), F32)
        sqh2, _ = AR.alloc((2, 2, 128), BF16)
        rsh, _ = AR.alloc((128,), F32)
        sd, _ = AR.alloc((128,), BF16)
        dtmp, _ = AR.alloc((4, 128), F32)
        sg = dtmp.rearrange("p a b -> p (a b)")
        gm1, _ = AR.alloc((1,), F32, parts=4)
        xmp = bufA
        hn = bufA[:, :, 0:T]
        scb = bufA[:, 0, 0:T]
        xmb = qTb
        xmlast, _ = AR.alloc((2, 4), F32)
        xllast, _ = AR.alloc((4,), F32)
        dgm, _ = AR.alloc((2, 4, 128), BF16)
        dgl, _ = AR.alloc((4, 128), BF16)
        xcoff = region(xc)[3]
        emb = At[:, xcoff // 4:xcoff // 4 + T]
        ktoff = region(kT)[3]
        ctmp = At[:, ktoff // 4:ktoff // 4 + T]
        def _f32(off, n_):
            return At[:, off // 4:off // 4 + n_]
        oA, oQ, oK, oW, oV = (region(b_)[3] for b_ in (bufA, qsc, kT, kw, vt))
        xlc = _f32(oA, T)
        rr = _f32(oA + T * 4, T)
        ii = _f32(oQ, T)
        uu = _f32(oK, T)
        hh_ = _f32(oW, T)
        xlb = At[:, oV // 4:oV // 4 + (T + 4) // 2].bitcast(BF16)
        xlcb = At[:, (oV + (T + 4) * 2) // 4:(oV + (T + 4) * 2) // 4 + T // 2].bitcast(BF16)
        assert (T + 4) * 2 + T * 2 <= 16 * 258 * 2 and 2 * T * 4 <= 2 * (T + 4) * 4

        def sweepP(wb, w, evac, rhs, c0, nk=8):
            for mi in range(w // 128):
                b = nbank(*KNOB["pbanks"])
                for kc in range(nk):
                    MM(PS[:, b, 0:NS], wb[:, kc, mi * 128:(mi + 1) * 128], xnS[:, kc, :], kc == 0, kc == nk - 1)
                CP("act", zT[:, len(zmap), :], PS[:, b, 0:NS])
                zmap.append((len(zmap), c0 + mi * 128, 128))
                for nt in range(4):
                    b = nbank(*KNOB["pbanks"])
                    for kc in range(nk):
                        MM(PS[:, b, :], wb[:, kc, mi * 128:(mi + 1) * 128], rhs[:, kc, nt * 512:(nt + 1) * 512], kc == 0, kc == nk - 1)
                    evac(mi, nt, PS[:, b, :])

        TS("dve", nbf, bgc, -1.0, None, ALU.mult)
        wb = wload(w_in, 3 * D, 8)
        b = nbank(*KNOB["pbanks"])
        for kc in range(8):
            MM(PS[0:8, b, 0:NS], wb[:, kc, 0:8], xnS[:, kc, :], kc == 0, kc == 7)
        CP("act", zT[0:8, 40, :], PS[0:8, b, 0:NS])
        zmap_gates = (40, 3 * D, 8)
        for nt in range(4):
            ns = slice(nt * 512, (nt + 1) * 512)
            b = nbank(*KNOB["pbanks"])
            for kc in range(8):
                MM(PS[0:4, b, :], wb[:, kc, 0:4], xn[:, kc, ns], kc == 0, kc == 7)
            ACT(gX1[:, ns], PS[0:4, b, :], AF.Identity, bias=bgc[:, 0:1])
            b = nbank(*KNOB["pbanks"])
            for kc in range(8):
                MM(PS[0:4, b, :], wb[:, kc, 4:8], xn[:, kc, ns], kc == 0, kc == 7)
            ACT(gX2[:, ns], PS[0:4, b, :], AF.Exp, bias=nbf[:, 1:2], scale=-1.0)
        ACT(gX2, gX2, AF.Ln, bias=1.0)
        SCAN(gX3, gX2, gX2, 0.0, ALU.add, ALU.max)
        TTo("dve", gX1, gX1, gX3, ALU.add)
        SCAN(gX2, gX1, gX1, NEG, ALU.max, ALU.max)
        TTo("dve", gX3, gX3, gX2, ALU.subtract)
        TS("dve", gm1, gX3[:, T - 1:T], -1.0, None, ALU.mult)
        DMA("sp", o_pm.rearrange("o h -> h o"), gm1, "o7", is_out=True, slow=True)
        ACT(gX3, gX3, AF.Exp)
        b = nbank(*KNOB["pbanks"])
        for c in range(16):
            TR(PS[:, b, c * 4:(c + 1) * 4], gX1[:, c * 128:(c + 1) * 128], ident[0:4, 0:4])
        CP("dve", gcol, PS[:, b, 0:64].rearrange("p (c h) -> p c h", h=4))
        if LEVEL == 1:
            raise _Stop()
        gG, gEm = gX2, gX3

        for h in range(4):
            MARKS.append((f"P_h{h}_prep", len(S.ops)))
            MSET("dve", xmb[:, :, 0:4], 0.0)
            for mi in range(2):
                for j in range(4):
                    TS("dve", dgm[:, mi, j, :], ident, mcw[:, j, 2 * h + mi:2 * h + mi + 1], None, ALU.mult)
            wb = wload(w_in, D + h * 256, 256)

            def ev_xm(mi, nt, ps):
                CP("act", xmb[:, mi, 4 + nt * 512:4 + (nt + 1) * 512], ps)
                if nt == 3:
                    CP("dve", xmlast[:, mi, 0:3], ps[:, 509:512])
            sweepP(wb, 256, ev_xm, xn, D + h * 256)
            for mi in range(2):
                DMA("sp", o_pmconv[:, h * 256 + mi * 128:h * 256 + (mi + 1) * 128].rearrange("r p -> p r"), xmlast[:, mi, 0:3],
                    "o8", is_out=True, slow=True)
            MSET("dve", vt[:, :, 256:258], 1.0)
            for c2 in range(8):
                b = nbank(*KNOB["pbanks"])
                for cc in range(2):
                    c = 2 * c2 + cc
                    for dc in range(2):
                        MM(PS[:, b, cc * 256:(cc + 1) * 256], xmb[:, dc, 4 + c * 128:4 + (c + 1) * 128], wv_bf[:, 2 * h + dc, :], dc == 0, dc == 1)
                CP("act", vt[:, 2 * c2:2 * c2 + 2, 0:256], PS[:, b, :].rearrange("p (c e) -> p c e", e=256))
            for mi in range(2):
                kcg = 2 * h + mi
                for nt in range(4):
                    b = nbank(*KNOB["pbanks"])
                    for j in range(4):
                        MM(PS[:, b, :], dgm[:, mi, j, :], xmb[:, mi, nt * 512 + j + 1:nt * 512 + j + 513], j == 0, j == 3)
                    ACT(xc[:, mi, nt * 512:(nt + 1) * 512], PS[:, b, :], AF.Silu, bias=MCB[:, kcg:kcg + 1])
            if LEVEL == 2:
                raise _Stop()
            MSET("dve", Gl[:, 0:1], NEG)
            for nt in range(4):
                b = nbank(*KNOB["pbanks"])
                MM(PS[:, b, :], esel[:, h, :], gG[:, nt * 512:(nt + 1) * 512], True, True)
                CP("act", Gl[:, 1 + 4 * nt:5 + 4 * nt], PS[:, b, 127::128])
                for cc in range(4):
                    c = 4 * nt + cc
                    ACT(scb[:, c * 128:(c + 1) * 128], PS[:, b, cc * 128:(cc + 1) * 128], AF.Exp, bias=Gl[:, c:c + 1], scale=-1.0)
                STT("dve", dtmp, PS[:, b, :].rearrange("p (c t) -> p c t", t=128), -1.0,
                    maskneg.unsqueeze(1).broadcast_to([128, 4, 128]), ALU.mult, ALU.add)
                for cc in range(4):
                    c = 4 * nt + cc
                    ACT(DT[:, c, :], dtmp[:, cc, :], AF.Exp, bias=gcol[:, c, h:h + 1])
            TTo("dve", decb, Gl[:, 0:16], Gl[:, 1:17], ALU.subtract)
            ACT(decb, decb, AF.Exp)
            TTo("dve", wkc, gcol[:, :, h], Gl[:, 1:17], ALU.subtract)
            ACT(wkc, wkc, AF.Exp)
            for j in range(2):
                for nt in range(4):
                    ns = slice(nt * 512, (nt + 1) * 512)
                    b = nbank(*KNOB["pbanks"])
                    for dc in range(2):
                        MM(PS[:, b, :], wk_bf[:, 2 * h + dc, j * 128:(j + 1) * 128], xc[:, dc, ns], dc == 0, dc == 1)
                    CP(KNOB.get("kev", "dve"), kT[:, j, ns], PS[:, b, :])
            for c2 in range(8):
                b = nbank(*KNOB["pbanks"])
                for cc in range(2):
                    c = 2 * c2 + cc
                    for dc in range(2):
                        MM(PS[:, b, cc * 256:(cc + 1) * 256], xc[:, dc, c * 128:(c + 1) * 128], wk_bf[:, 2 * h + dc, :], dc == 0, dc == 1)
                for cc in range(2):
                    c = 2 * c2 + cc
                    TS("dve", kw[:, c, :], PS[:, b, cc * 256:(cc + 1) * 256], wkc[:, c:c + 1], None, ALU.mult)
            for j in range(2):
                for nt in range(4):
                    ns = slice(nt * 512, (nt + 1) * 512)
                    b = nbank(*KNOB["pbanks"])
                    for dc in range(2):
                        MM(PS[:, b, :], wq_bf[:, 2 * h + dc, j * 128:(j + 1) * 128], xc[:, dc, ns], dc == 0, dc == 1)
                    ACT(qT[:, j, ns], PS[:, b, :], AF.Copy, scale=1.0 / 16.0)
                    STT("dve", qsc[:, j, ns], PS[:, b, :], 1.0 / 16.0, scb[:, ns], ALU.mult, ALU.mult)
            for nt in range(4):
                b = nbank(*KNOB["pbanks"])
                MM(PS[:, b, :], esel[:, h, :], gEm[:, nt * 512:(nt + 1) * 512], True, True)
                CP("act", emb[:, nt * 512:(nt + 1) * 512], PS[:, b, :])
            if LEVEL == 3:
                raise _Stop()
            MARKS.append((f"P_h{h}_loop", len(S.ops)))
            MSET("dve", Caug, 0.0)
            MSET("dve", Cb2, 0.0)
            MSET("dve", nb2, 0.0)

            def stageA(c):
                cs_ = slice(c * 128, (c + 1) * 128)
                nbk = 4 + c % 2
                k_ = c % 2
                for dc in range(2):
                    MM(PS[:, 6, dc * 256:(dc + 1) * 256], kw[:, c, dc * 128:(dc + 1) * 128], vt[:, c, 0:256], True, True)
                for dc in range(2):
                    MM(PS[:, 7, 256 + 2 * dc:258 + 2 * dc], kw[:, c, dc * 128:(dc + 1) * 128], vt[:, c, 256:258], True, True)
                for dc in range(2):
                    MM(PS[:, 3, 0:128], kT[:, dc, cs_], qT[:, dc, cs_], dc == 0, dc == 1)
                TTo("dve", sd, PS[:, 3, 0:128], DT[:, c, :], ALU.mult)
                Cbp = Cb2[:, 1 - k_, :, :]
                nbp = nb2[:, 1 - k_, :, :]
                for j in range(2):
                    MM(PS[:, nbk, j * 128:(j + 1) * 128], vt[:, c, j * 128:(j + 1) * 128], sd, True, False)
                    for dc in range(2):
                        MM(PS[:, nbk, j * 128:(j + 1) * 128], Cbp[:, dc, j * 128:(j + 1) * 128], qsc[:, dc, cs_], False, dc == 1)
                MM(PS[:, nbk, 256:384], ones_bf, sd, True, False)
                for dc in range(2):
                    MM(PS[:, nbk, 256:384], nbp[:, dc, :], qsc[:, dc, cs_], False, dc == 1)
                STT("dve", Caug[:, :, 0:256], Caug[:, :, 0:256], decb[:, c:c + 1],
                    PS[:, 6, :].rearrange("p (j e) -> p j e", e=256), ALU.mult, ALU.add)
                STT("dve", Caug[:, :, 256], Caug[:, :, 256], decb[:, c:c + 1], PS[:, 7, 256:260:2], ALU.mult, ALU.add)
                CP("act", Cb2[:, k_, :, :], Caug)
                CP(KNOB.get("nbev", "pool"), nb2[:, k_, :, :], Caug[:, :, 256:257].broadcast_to([128, 2, 128]))

            def stageB(c):
                cs_ = slice(c * 128, (c + 1) * 128)
                nbk = 4 + c % 2
                k_ = c % 2
                ACT(dd2[:, k_, :], PS[:, nbk, 256:384], AF.Abs)
                TTo("dve", dd2[:, k_, :], dd2[:, k_, :], emb[:, cs_], ALU.max)
                ACT(dd2[:, k_, :], dd2[:, k_, :], AF.Ln)
                ACT(dd2[:, k_, :], dd2[:, k_, :], AF.Exp, scale=-1.0)
                TTo("dve", hT2[:, k_, :, :], PS[:, nbk, 0:256].rearrange("p (j t) -> p j t", t=128),
                    dd2[:, k_, :].unsqueeze(1).broadcast_to([128, 2, 128]), ALU.mult)
                ACT(sqh2[:, k_, :, :], hT2[:, k_, :, :], AF.Square)

            def stageC(c):
                cs_ = slice(c * 128, (c + 1) * 128)
                k_ = c % 2
                for j in range(2):
                    MM(PS[:, 7, 0:128], ones_bf, sqh2[:, k_, j, :], j == 0, j == 1)
                ACT(rsh, PS[:, 7, 0:128], AF.Ln, bias=EPS, scale=1.0 / 256.0)
                ACT(rsh, rsh, AF.Exp, scale=-0.5)
                for j in range(2):
                    STT("dve", hn[:, j, cs_], hT2[:, k_, j, :], MGC[:, 2 * h + j:2 * h + j + 1], rsh, ALU.mult, ALU.mult)

            for i_ in range(16 + 2):
                if i_ < 16:
                    stageA(i_)
                if 0 <= i_ - 1 < 16:
                    stageB(i_ - 1)
                if 0 <= i_ - 2 < 16:
                    stageC(i_ - 2)
            DMA("sp", o_pC[h].rearrange("(j p) e -> p j e", p=128), Caug[:, :, 0:256], "o9", is_out=True)
            DMA("sp", o_pn[h].rearrange("(j p) -> p j", p=128), Caug[:, :, 256], "o10", is_out=True, slow=True)
            if LEVEL == 4:
                raise _Stop()
            MARKS.append((f"P_h{h}_gate", len(S.ops)))
            for gi_, c0 in enumerate((2 * D + h * 256, 4 * D + 8 + h * 256)):
                wb = wload(w_in, c0, 256)

                def ev_gate(mi, nt, ps, gi_=gi_):
                    ns = slice(nt * 512, (nt + 1) * 512)
                    sg_ = emb[:, ns]
                    ACT(sg_, ps, AF.Sigmoid)
                    if gi_ == 0:
                        TTo("dve", hn[:, mi, ns], hn[:, mi, ns], sg_, ALU.mult)
                    else:
                        TTo("dve", mrg[:, 2 * h + mi, ns], hn[:, mi, ns], sg_, ALU.mult)
                sweepP(wb, 256, ev_gate, xn, c0)
            if LEVEL == 5:
                raise _Stop()
            MARKS.append((f"P_h{h}_lru", len(S.ops)))
            for mi in range(2):
                n = 2 * h + mi
                MSET("dve", xlb[:, 0:4], 0.0)
                for j in range(4):
                    TS("dve", dgl[:, j, :], ident, lcw[:, j, n:n + 1], None, ALU.mult)
                wb = wload(w_in, n * 128, 128)

                def ev_xl(mi_, nt, ps):
                    CP("act", xlb[:, 4 + nt * 512:4 + (nt + 1) * 512], ps)
                    if nt == 3:
                        CP("dve", xllast[:, 0:3], ps[:, 509:512])
                sweepP(wb, 128, ev_xl, xn, n * 128)
                DMA("sp", o_plconv[:, n * 128:(n + 1) * 128].rearrange("r p -> p r"), xllast[:, 0:3], "o11", is_out=True, slow=True)
                for nt in range(4):
                    ns = slice(nt * 512, (nt + 1) * 512)
                    b = nbank(*KNOB["pbanks"])
                    for j in range(4):
                        MM(PS[:, b, :], dgl[:, j, :], xlb[:, nt * 512 + j + 1:nt * 512 + j + 513], j == 0, j == 3)
                    ACT(xlc[:, ns], PS[:, b, :], AF.Identity, bias=LCB[:, n:n + 1])
                    TS("dve", xlcb[:, ns], PS[:, b, :], LCB[:, n:n + 1], None, ALU.add)
                for nt in range(4):
                    ns = slice(nt * 512, (nt + 1) * 512)
                    b = nbank(*KNOB["pbanks"])
                    MM(PS[:, b, :], wa_bf[:, n, :], xlcb[:, ns], True, True)
                    ACT(rr[:, ns], PS[:, b, :], AF.Sigmoid, bias=LBA[:, n:n + 1])
                    b = nbank(*KNOB["pbanks"])
                    MM(PS[:, b, :], wx_bf[:, n, :], xlcb[:, ns], True, True)
                    ACT(ii[:, ns], PS[:, b, :], AF.Sigmoid, bias=LBX[:, n:n + 1])
                ACT(uu, rr, AF.Exp, scale=C2COL[:, n:n + 1])
                ACT(rr, rr, AF.Exp, scale=CCOL[:, n:n + 1])
                if KNOB.get("sqrt_explog", False):
                    ACT(uu, uu, AF.Ln, bias=1.0, scale=-1.0)
                    ACT(uu, uu, AF.Exp, scale=0.5)
                else:
                    ACT(uu, uu, AF.Sqrt, bias=1.0, scale=-1.0)
                if KNOB.get("ix", False):
                    TTo(KNOB.get("ixeng", "pool"), ii, ii, xlc, ALU.mult)
                    TTo("dve", uu, uu, ii, ALU.mult)
                else:
                    TTo("dve", uu, uu, ii, ALU.mult)
                    TTo("dve", uu, uu, xlc, ALU.mult)
                SCAN(hh_, rr, uu, 0.0, ALU.mult, ALU.add)
                DMA("sp", o_plh[0:1, n * 128:(n + 1) * 128].rearrange("o p -> p o"), hh_[:, T - 1:T], "o12", is_out=True, slow=True)
                wb = wload(w_in, 3 * D + 8 + n * 128, 128)

                def ev_gl(mi_, nt, ps, n=n, mi=mi):
                    ns = slice(nt * 512, (nt + 1) * 512)
                    sg_ = ii[:, ns]
                    ACT(sg_, ps, AF.Sigmoid)
                    TTo("dve", sg_, sg_, hh_[:, ns], ALU.mult)
                    TTo("dve", mrg[:, n, ns], mrg[:, n, ns], sg_, ALU.add)
                sweepP(wb, 128, ev_gl, xn, 3 * D + 8 + n * 128)

        if LEVEL == 6:
            raise _Stop()
        zmap.append(zmap_gates)
        phase_S()
        MARKS.append(("D", len(S.ops)))
        AR.top = PB
        alloc_stage1()
        xres, _ = AR.alloc((8, T), F32)
        xresS, _ = AR.alloc((8, NS), F32)
        sqn, sqoff = AR.alloc((8, 512), BF16)
        rst, _ = AR.alloc((512,), F32)
        sg2, _ = AR.alloc((512,), F32)
        pT, _ = AR.alloc((2, T), BF16, at=sqoff)
        pTS, _ = AR.alloc((2, NS), BF16)
        wple_bf, _ = AR.alloc((2, D), BF16)
        aTS, _ = AR.alloc((12, NS), BF16)
        moff = region(mrg)[3]
        aT = At[:, moff // 4:moff // 4 + 12 * T // 2].bitcast(BF16).rearrange("p (a b) -> p a b", b=T)
        assert moff + 12 * T * 2 <= region(xres)[3], (moff, region(xres))
        scrD = []
        for i_ in range(2):
            o_ = moff + i_ * 10240
            scrD.append((At[:, o_ // 4:o_ // 4 + 2048].bitcast(BF16).rearrange("p (a b) -> p a b", b=512),
                         At[:, (o_ + 8192) // 4:(o_ + 8192) // 4 + 512]))
        load_tiles("raw", xres, xresS, list(range(17)))

        def xsl(bP, bS, m, nt):
            return bP[:, m, nt * 512:(nt + 1) * 512] if nt < 4 else bS[:, m, :]

        def wd_(nt):
            return 512 if nt < 4 else NS

        for blk in range(4):
            wb = wload(w_out, blk * 256, 256)
            for mi in range(2):
                m = blk * 2 + mi
                for nt in range(5):
                    b = nbank()
                    for kc in range(8):
                        MM(PS[:, b, 0:wd_(nt)], wb[:, kc, mi * 128:(mi + 1) * 128], xsl(mrg, mrgS, kc, nt), kc == 0, kc == 7)
                    xr = xsl(xres, xresS, m, nt)
                    TTo("dve", xr, PS[:, b, 0:wd_(nt)], xr, ALU.add)

        def normD(gcols, dstP, dstS, scr=None, after_tile=None):
            for nt in range(5):
                w_ = wd_(nt)
                sq_, rs_ = (sqn, rst) if scr is None else scr[nt % len(scr)]
                for kc in range(8):
                    ACT(sq_[:, kc, 0:w_], xsl(xres, xresS, kc, nt), AF.Square)
                b = nbank()
                for kc in range(8):
                    MM(PS[:, b, 0:w_], ones_bf, sq_[:, kc, 0:w_], kc == 0, kc == 7)
                ACT(rs_[:, 0:w_], PS[:, b, 0:w_], AF.Ln, bias=EPS, scale=1.0 / D)
                ACT(rs_[:, 0:w_], rs_[:, 0:w_], AF.Exp, scale=-0.5)
                for kc in range(8):
                    STT("dve", xsl(dstP, dstS, kc, nt), xsl(xres, xresS, kc, nt), gcols[:, kc:kc + 1], rs_[:, 0:w_], ALU.mult, ALU.mult)
                if after_tile is not None:
                    after_tile(nt)

        MARKS.append(("D_norm2", len(S.ops)))
        normD(G2C, xn, xnS)
        MARKS.append(("D_ffn", len(S.ops)))
        for f0, nf in ((0, 12), (12, 10)):
            for fp in range(nf // 2):
                wg = wload(w_gate, (f0 + 2 * fp) * 128, 256)
                wu = wload(w_up, (f0 + 2 * fp) * 128, 256)
                for fj in range(2):
                    fi = 2 * fp + fj
                    for nt in range(5):
                        w_ = wd_(nt)
                        bg_ = nbank()
                        for kc in range(8):
                            MM(PS[:, bg_, 0:w_], wg[:, kc, fj * 128:(fj + 1) * 128], xsl(xn, xnS, kc, nt), kc == 0, kc == 7)
                        bu_ = nbank()
                        for kc in range(8):
                            MM(PS[:, bu_, 0:w_], wu[:, kc, fj * 128:(fj + 1) * 128], xsl(xn, xnS, kc, nt), kc == 0, kc == 7)
                        ACT(sg2[:, 0:w_], PS[:, bg_, 0:w_], AF.Silu)
                        TTo("dve", xsl(aT, aTS, fi, nt), sg2[:, 0:w_], PS[:, bu_, 0:w_], ALU.mult)
            for m in range(8):
                wdn = wload(w_down[f0 * 128:(f0 + nf) * 128, :], m * 128, 128, kparts=nf)
                for nt in range(5):
                    w_ = wd_(nt)
                    b = nbank()
                    for fi in range(nf):
                        MM(PS[:, b, 0:w_], wdn[:, fi, :], xsl(aT, aTS, fi, nt), fi == 0, fi == nf - 1)
                    xr = xsl(xres, xresS, m, nt)
                    TTo("dve", xr, PS[:, b, 0:w_], xr, ALU.add)
        MARKS.append(("D_norm3", len(S.ops)))
        normD(G3C, xn, xnS, scr=scrD)
        DMA("pool", wple_bf, w_ple.rearrange("(k p) n -> p k n", p=128), "w_ple", cast=True)
        for tt in range(17):
            npart = 128 if tt < 16 else NS
            sl = tt % 2
            pin = st1["xin"][0:npart, sl, 0:PD]
            DMA("sp", pin, pp[tt * 128:(tt + 1) * 128, :] if tt < 16 else psm, f"xin{sl}")
            b = nbank()
            for c in range(2):
                TR(PS[:, b, c * 128:c * 128 + npart], pin[:, c * 128:(c + 1) * 128], ident[0:npart, 0:npart])
            dv = pT[:, :, tt * 128:(tt + 1) * 128] if tt < 16 else pTS
            CP("act", dv, PS[:, b, 0:256].rearrange("p (c t) -> p c t", t=128)[:, :, 0:npart])
        for blk in range(4):
            wb = wload(w_pleg, blk * 256, 256)
            for nt, mi in [(nt, mi) for nt in range(5) for mi in range(2)]:
                if True:
                    m = blk * 2 + mi
                    w_ = wd_(nt)
                    bg_ = nbank()
                    for kc in range(8):
                        MM(PS[:, bg_, 0:w_], wb[:, kc, mi * 128:(mi + 1) * 128], xsl(xn, xnS, kc, nt), kc == 0, kc == 7)
                    bp_ = nbank()
                    for kc in range(2):
                        MM(PS[:, bp_, 0:w_], wple_bf[:, kc, m * 128:(m + 1) * 128], xsl(pT, pTS, kc, nt), kc == 0, kc == 1)
                    ACT(sg2[:, 0:w_], PS[:, bg_, 0:w_], AF.Sigmoid)
                    TTo("dve", sg2[:, 0:w_], sg2[:, 0:w_], PS[:, bp_, 0:w_], ALU.mult)
                    xr = xsl(xres, xresS, m, nt)
                    TTo("dve", xr, xr, sg2[:, 0:w_], ALU.add)
        MARKS.append(("D_final", len(S.ops)))
        yo = st1["xsc"]

        def out_tiles(nt):
            for tt in ([16] if nt == 4 else range(4 * nt, 4 * nt + 4)):
                npart = 128 if tt < 16 else NS
                sl = tt % 2
                for half in range(2):
                    b = nbank()
                    for c in range(4):
                        kc = half * 4 + c
                        src_ = xres[:, kc, tt * 128:(tt + 1) * 128] if tt < 16 else xresS[:, kc, :]
                        TR(PS[0:npart, b, c * 128:(c + 1) * 128], src_, ident)
                    CP("act" if half == 0 else "dve", yo[0:npart, sl, half * 512:(half + 1) * 512], PS[0:npart, b, :])
                DMA("sp", y_p[tt * 128:(tt + 1) * 128, :] if tt < 16 else y_s, yo[0:npart, sl, :], f"yo{sl}", is_out=True)

        normD(GFC, xres, xresS, scr=scrD, after_tile=out_tiles)
    except _Stop:
        pass
    S.emit()
    es.close()
    return nc


OUT_SPECS = [
    ("y_p", (T, D)), ("y_s", (NS, D)), ("o_plconv", (3, D)), ("o_plh", (1, D)), ("o_pmconv", (3, D)),
    ("o_pC", (4, 256, 256)), ("o_pn", (4, 256)), ("o_pm", (1, 4)),
    ("o_slconv", (NS, 3, D)), ("o_slh", (NS, D)), ("o_smconv", (NS, 3, D)),
    ("o_sC", (NS, 4, 256, 256)), ("o_sn", (NS, 4, 256)), ("o_sm", (NS, 4)),
]

_NC_CACHE = []


def kernel(**inputs):
    f = lambda a: np.ascontiguousarray(np.asarray(a, dtype=np.float32))
    I = {k: f(v) for k, v in inputs.items()}
    if not _NC_CACHE:
        _NC_CACHE.append(build())
    nc = _NC_CACHE[0]
    wnames = ["norm_mix_g", "w_in", "b_gates", "lru_conv_w", "lru_conv_b", "lru_w_a", "lru_b_a", "lru_w_x", "lru_b_x",
              "lru_lambda", "mlstm_conv_w", "mlstm_conv_b", "w_q", "w_k", "w_v", "mlstm_norm_g", "w_out", "norm_ffn_g",
              "w_ffn_gate", "w_ffn_up", "w_ffn_down", "norm_ple_g", "w_ple_gate", "w_ple"]
    shared = {k: f(I[k][0]) for k in wnames}
    shared["final_norm_g"] = I["final_norm_g"]
    in_maps = []
    for c in range(8):
        sl = slice(c * NS, (c + 1) * NS)
        m = dict(shared)
        m["xp"] = f(I["x_prompt"][c])
        m["xs"] = f(I["x_sample"][sl, 0])
        m["st_lconv"] = f(I["state_lru_conv"][0, sl])
        m["st_lh"] = f(I["state_lru_h"][0, sl])
        m["st_mconv"] = f(I["state_mlstm_conv"][0, sl])
        m["st_C"] = f(I["state_mlstm_C"][0, sl])
        m["st_n"] = f(I["state_mlstm_n"][0, sl])
        m["st_m"] = f(I["state_mlstm_m"][0, sl])
        m["pp"] = f(I["p_prompt"][0, c])
        m["psm"] = f(I["p_sample"][0, sl, 0])
        in_maps.append(m)
    res = run_bass_kernel_spmd(nc, in_maps, core_ids=list(range(8)))
    R = res.results
    g = lambda name: [np.asarray(R[c][name], dtype=np.float32) for c in range(8)]
    y_prompt = np.stack(g("y_p"), 0)
    y_sample = np.concatenate(g("y_s"), 0)[:, None, :]
    p_lconv = np.stack(g("o_plconv"), 0)[None]
    p_lh = np.concatenate(g("o_plh"), 0)[None]
    p_mconv = np.stack(g("o_pmconv"), 0)[None]
    p_C = np.stack(g("o_pC"), 0)[None]
    p_n = np.stack(g("o_pn"), 0)[None]
    p_m = np.concatenate(g("o_pm"), 0)[None]
    s_lconv = np.concatenate(g("o_slconv"), 0)[None]
    s_lh = np.concatenate(g("o_slh"), 0)[None]
    s_mconv = np.concatenate(g("o_smconv"), 0)[None]
    s_C = np.concatenate(g("o_sC"), 0)[None]
    s_n = np.concatenate(g("o_sn"), 0)[None]
    s_m = np.concatenate(g("o_sm"), 0)[None]
    return (y_prompt, y_sample, p_lconv, p_lh, p_mconv, p_C, p_n, p_m, s_lconv, s_lh, s_mconv, s_C, s_n, s_m)
```

```python
from contextlib import ExitStack
import numpy as np
import concourse.bass as bass
import concourse.mybir as mybir
from concourse.bass_utils import run_bass_kernel_spmd

F32 = mybir.dt.float32
BF16 = mybir.dt.bfloat16
AF = mybir.ActivationFunctionType
ALU = mybir.AluOpType
AX = mybir.AxisListType

T = 2048
NS = 16
TT = T + NS
D = 1024
NIN = 5128
DFF = 2816
PD = 256
EPS = 1e-6
NEG = -1.0e30
ENGS = ("pe", "act", "dve", "pool", "sp")
BANK = 3000
SCHEDULE = True
MARKS = []
KNOB = {"pbanks": (0, 8), "prio": "cp", "lat": 400.0, "asq": "act", "ix": True, "ixeng": "dve", "nbev": "dve", "tset_aware": True, "tpen": 2500.0, "sqrt_explog": True}


def _esz(dt):
    return 2 if dt == BF16 else 4


def region(ap):
    steps = ap.ap
    esz = _esz(ap.dtype)
    name = ap.tensor.name
    if str(ap.space) == "DRAM" or "DRAM" in str(ap.space):
        ext = sum((c - 1) * abs(s) for s, c in steps) + 1
        return (name, 0, 1, ap.offset * esz, (ap.offset + ext) * esz)
    pstep, pcount = steps[0]
    if pstep == 0:
        p0, f0 = 0, ap.offset
        pcount = 128
    else:
        p0, f0 = ap.offset // pstep, ap.offset % pstep
    ext = sum((c - 1) * abs(s) for s, c in steps[1:]) + 1
    if name == "PS":
        return (name, 0, 128, (f0 * esz) // 2048 * 2048, ((f0 + ext) * esz + 2047) // 2048 * 2048)
    return (name, p0, p0 + pcount, f0 * esz, (f0 + ext) * esz)


class Op:
    __slots__ = ("eng", "fn", "deps", "seq", "needed", "ms", "stream", "scount", "waits", "idx", "cost", "xfer", "tset")


class Sched:
    def __init__(self, nc):
        self.nc = nc
        self.ops = []
        self.eng_ops = {e: [] for e in ENGS}
        self.acc = {}
        self.streams = {}
        self.out_dmas = []
        self.last_stream = {}

    def add(self, eng, fn, reads, writes, stream=None, is_out=False, cost=300.0, xfer=0.0):
        op = Op()
        op.idx = len(self.ops)
        op.cost = cost
        op.xfer = xfer
        op.tset = None
        op.eng = eng
        op.fn = fn
        op.needed = False
        op.stream = stream
        op.seq = len(self.eng_ops[eng])
        deps = set()
        key = eng if stream is None else ("dma", len(self.ops))
        if stream is not None:
            prev = self.last_stream.get(stream)
            if prev is not None:
                deps.add(prev)
            self.last_stream[stream] = op
        writes = list(writes) + [ap for ap in reads if ap.tensor.name == "PS"]
        reads = [ap for ap in reads if ap.tensor.name != "PS"]
        for ap in reads:
            name, p0, p1, b0, b1 = region(ap)
            recs = self.acc.setdefault(name, [])
            for r in recs:
                if r[4] and r[0] < p1 and p0 < r[1] and r[2] < b1 and b0 < r[3]:
                    deps.add(r[5])
            for r in recs:
                if (not r[4]) and r[6] == key and r[0] == p0 and r[1] == p1 and r[2] == b0 and r[3] == b1:
                    r[5].append(op)
                    break
            else:
                recs.append([p0, p1, b0, b1, False, [op], key])
        for ap in writes:
            name, p0, p1, b0, b1 = region(ap)
            recs = self.acc.setdefault(name, [])
            keep = []
            for r in recs:
                if r[0] < p1 and p0 < r[1] and r[2] < b1 and b0 < r[3]:
                    if r[4]:
                        if r[5] is not op:
                            deps.add(r[5])
                    else:
                        deps.update(r[5])
                    if p0 <= r[0] and r[1] <= p1 and b0 <= r[2] and r[3] <= b1:
                        continue
                keep.append(r)
            keep.append([p0, p1, b0, b1, True, op, key])
            self.acc[name] = keep
        deps.discard(op)
        op.deps = deps
        if stream is not None:
            self.streams[stream] = self.streams.get(stream, 0) + 16
            op.scount = self.streams[stream]
            if is_out:
                self.out_dmas.append(op)
        self.ops.append(op)
        self.eng_ops[eng].append(op)
        return op

    def schedule(self, window=1000):
        ops = self.ops
        n = len(ops)
        succs = [[] for _ in range(n)]
        npred = [0] * n
        for op in ops:
            npred[op.idx] = len(op.deps)
            for d in op.deps:
                succs[d.idx].append(op)
        rt = [0.0] * n
        fin = [0.0] * n
        lat = KNOB.get("lat", 0.0)
        cp = [0.0] * n
        if KNOB.get("prio", "idx") == "cp":
            for op in reversed(ops):
                m_ = 0.0
                for s in succs[op.idx]:
                    if cp[s.idx] > m_:
                        m_ = cp[s.idx]
                cp[op.idx] = m_ + op.cost + (op.xfer if op.stream is not None else 0.0)
        use_cp = KNOB.get("prio", "idx") == "cp"
        self.start_t = [0.0] * n
        self.fin_t = fin
        done = [False] * n
        avail = {e: [] for e in ENGS}
        for op in ops:
            if npred[op.idx] == 0:
                avail[op.eng].append(op)
        efree = {e: 0.0 for e in ENGS}
        dma_free = 0.0
        cur_set = None
        aware = KNOB.get("tset_aware", True)
        nsw = 0
        new_eng = {e: [] for e in ENGS}
        new_ops = []
        oldest = 0
        cnt = 0
        while cnt < n:
            while oldest < n and done[oldest]:
                oldest += 1
            lim = oldest + window
            best = None
            for e in ENGS:
                fe = efree[e]
                for op in avail[e]:
                    if op.idx > lim:
                        continue
                    r = rt[op.idx]
                    pk = -cp[op.idx] if use_cp else op.idx
                    pen = KNOB.get("tpen", ACT_SWITCH_NS) if (aware and op.tset is not None and op.tset != cur_set) else 0.0
                    key = (fe + pen, 0, pk) if r <= fe else (r + pen, 1, pk)
                    if best is None or key < best[0]:
                        best = (key, op)
            if best is None:
                cands = [op for e in ENGS for op in avail[e]]
                op = min(cands, key=lambda o: o.idx)
                best = ((max(rt[op.idx], efree[op.eng]), 0, op.idx), op)
            op = best[1]
            st = max(rt[op.idx], efree[op.eng])
            avail[op.eng].remove(op)
            if op.stream is not None:
                efree[op.eng] = st + op.cost
                ts = max(st + op.cost, dma_free)
                dma_free = ts + op.xfer * 0.6
                f = ts + 2000.0 + op.xfer
            else:
                f = st + op.cost
                if op.tset is not None and op.tset != cur_set:
                    f += ACT_SWITCH_NS
                    cur_set = op.tset
                    nsw += 1
                efree[op.eng] = f
            fin[op.idx] = f
            self.start_t[op.idx] = st
            done[op.idx] = True
            cnt += 1
            op.seq = len(new_eng[op.eng])
            new_eng[op.eng].append(op)
            new_ops.append(op)
            for s in succs[op.idx]:
                fl = f + (lat if s.eng != op.eng else 0.0)
                if fl > rt[s.idx]:
                    rt[s.idx] = fl
                npred[s.idx] -= 1
                if npred[s.idx] == 0:
                    avail[s.eng].append(s)
        self.ops = new_ops
        self.eng_ops = new_eng
        self.est_ns = max(fin) if n else 0.0
        self.n_switch = nsw

    def emit(self):
        nc = self.nc
        if SCHEDULE:
            self.schedule()
        fin = Op()
        fin.eng = "sp"
        fin.fn = None
        fin.needed = False
        fin.stream = None
        fin.seq = len(self.eng_ops["sp"])
        fin.deps = set(self.out_dmas)
        self.ops.append(fin)
        self.eng_ops["sp"].append(fin)
        wm = {e: {} for e in ENGS}
        for op in self.ops:
            e = op.eng
            best = {}
            waits = []
            for d in op.deps:
                if d.stream is not None:
                    k = ("s", d.stream)
                    if wm[e].get(k, 0) >= d.scount:
                        continue
                    if k not in best or best[k].scount < d.scount:
                        best[k] = d
                else:
                    if d.eng == e and e == "pe":
                        continue
                    if wm[e].get(d.eng, -1) >= d.seq:
                        continue
                    if d.eng not in best or best[d.eng].seq < d.seq:
                        best[d.eng] = d
            for k, d in best.items():
                if d.stream is not None:
                    wm[e][k] = d.scount
                else:
                    wm[e][k] = d.seq
                    d.needed = True
                waits.append(d)
            op.waits = waits
        nbanks = {}
        for e in ENGS:
            c = 0
            for op in self.eng_ops[e]:
                if op.needed:
                    op.ms = c
                    c += 1
            nbanks[e] = c // BANK + 1
        with ExitStack() as es:
            esem = {e: [es.enter_context(nc.semaphore(f"s_{e}{i}")) for i in range(nbanks[e])] for e in ENGS}
            ssem = {s: es.enter_context(nc.semaphore(f"d_{s}")) for s in self.streams}
            block = es.enter_context(nc.Block())

            def make(engname):
                def body(eng):
                    for op in self.eng_ops[engname]:
                        for d in op.waits:
                            if d.stream is not None:
                                eng.wait_ge(ssem[d.stream], d.scount)
                            else:
                                eng.wait_ge(esem[d.eng][d.ms // BANK], d.ms % BANK + 1)
                        if op.fn is None:
                            continue
                        ins = op.fn(eng)
                        if op.stream is not None:
                            ins.then_inc(ssem[op.stream], 16)
                        elif op.needed:
                            ins.then_inc(esem[engname][op.ms // BANK], 1)
                return body

            block.tensor(make("pe"))
            block.scalar(make("act"))
            block.vector(make("dve"))
            block.gpsimd(make("pool"))
            block.sync(make("sp"))


class Arena:
    def __init__(self, t):
        self.t = t
        self.top = 0

    def alloc(self, shape, dtype=F32, parts=128, at=None):
        n = int(np.prod(shape))
        nb = (n * _esz(dtype) + 63) // 64 * 64
        if at is None:
            at = self.top
            self.top += nb
            assert self.top <= self.t.shape[1] * 4, ("arena overflow", self.top)
        v = self.t[0:parts, at // 4:(at + nb) // 4]
        if dtype != F32:
            v = v.bitcast(dtype)
        v = v[:, 0:n]
        if len(shape) == 2:
            v = v.rearrange("p (a b) -> p a b", b=shape[1])
        elif len(shape) == 3:
            v = v.rearrange("p (a b c) -> p a b c", b=shape[1], c=shape[2])
        return v, at


class _Stop(Exception):
    pass


LEVEL = 99


ACT_TSET = {AF.Exp: "E", AF.Ln: "E", AF.Square: "E", AF.Abs: "E", AF.Sigmoid: "S", AF.Silu: "U", AF.Sqrt: "Q"}
ACT_SWITCH_NS = 1300.0


def build():
    nc = bass.Bass("TRN2", target_bir_lowering=False)

    def din(name, shape):
        return nc.dram_tensor(name, list(shape), F32, kind="ExternalInput").ap()

    def dout(name, shape):
        return nc.dram_tensor(name, list(shape), F32, kind="ExternalOutput").ap()

    xp = din("xp", (T, D)); xs = din("xs", (NS, D))
    st_lconv = din("st_lconv", (NS, 3, D)); st_lh = din("st_lh", (NS, D))
    st_mconv = din("st_mconv", (NS, 3, D)); st_C = din("st_C", (NS, 4, 256, 256))
    st_n = din("st_n", (NS, 4, 256)); st_m = din("st_m", (NS, 4))
    pp = din("pp", (T, PD)); psm = din("psm", (NS, PD))
    g_mix = din("norm_mix_g", (D,)); w_in = din("w_in", (D, NIN)); b_gates = din("b_gates", (8,))
    lru_cw = din("lru_conv_w", (4, D)); lru_cb = din("lru_conv_b", (D,))
    lru_wa = din("lru_w_a", (8, 128, 128)); lru_ba = din("lru_b_a", (D,))
    lru_wx = din("lru_w_x", (8, 128, 128)); lru_bx = din("lru_b_x", (D,))
    lru_lam = din("lru_lambda", (D,))
    m_cw = din("mlstm_conv_w", (4, D)); m_cb = din("mlstm_conv_b", (D,))
    w_q = din("w_q", (4, 256, 256)); w_k = din("w_k", (4, 256, 256)); w_v = din("w_v", (4, 256, 256))
    m_g = din("mlstm_norm_g", (4, 256))
    w_out = din("w_out", (D, D)); g_ffn = din("norm_ffn_g", (D,))
    w_gate = din("w_ffn_gate", (D, DFF)); w_up = din("w_ffn_up", (D, DFF)); w_down = din("w_ffn_down", (DFF, D))
    g_ple = din("norm_ple_g", (D,)); w_pleg = din("w_ple_gate", (D, D)); w_ple = din("w_ple", (PD, D))
    g_fin = din("final_norm_g", (D,))

    y_p = dout("y_p", (T, D)); y_s = dout("y_s", (NS, D))
    o_plconv = dout("o_plconv", (3, D)); o_plh = dout("o_plh", (1, D)); o_pmconv = dout("o_pmconv", (3, D))
    o_pC = dout("o_pC", (4, 256, 256)); o_pn = dout("o_pn", (4, 256)); o_pm = dout("o_pm", (1, 4))
    o_slconv = dout("o_slconv", (NS, 3, D)); o_slh = dout("o_slh", (NS, D)); o_smconv = dout("o_smconv", (NS, 3, D))
    o_sC = dout("o_sC", (NS, 4, 256, 256)); o_sn = dout("o_sn", (NS, 4, 256)); o_sm = dout("o_sm", (NS, 4))

    es = ExitStack()
    ARENA_BYTES = 207 * 1024
    At = es.enter_context(nc.sbuf_tensor("A", [128, ARENA_BYTES // 4], F32))
    PS = es.enter_context(nc.psum_tensor("PS", [128, 8, 512], F32))
    S = Sched(nc)
    AR = Arena(At)

    def aps(*xs_):
        return [x for x in xs_ if x is not None and not isinstance(x, (int, float))]

    def nfree(ap):
        return int(np.prod(ap.shape[1:]))

    def MM(out, lhsT, rhs, start, stop):
        rd = [lhsT, rhs] + ([] if start else [out])
        nf_ = nfree(rhs)
        c = (60.0 + 0.30 * nf_ * 4) if rhs.dtype == F32 else max(30.0, 30.0 + 0.42 * nf_, 110.0 if nf_ >= 64 else 0.0)
        return S.add("pe", lambda e: e.matmul(out, lhsT=lhsT, rhs=rhs, start=start, stop=stop), rd, [out], cost=c)

    def TR(out, in_, ident):
        return S.add("pe", lambda e: e.transpose(out, in_, ident), [in_, ident], [out], cost=200.0)

    def ACT(out, in_, func, bias=None, scale=None, accum_out=None):
        kw = {}
        if bias is not None:
            kw["bias"] = bias
        if scale is not None:
            kw["scale"] = scale
        if accum_out is not None:
            kw["accum_out"] = accum_out
        op_ = S.add("act", lambda e: e.activation(out, in_, func, **kw),
                    aps(in_, bias, scale), aps(out, accum_out), cost=220.0 + 0.75 * nfree(out))
        op_.tset = ACT_TSET.get(func)
        return op_

    def ENG(name):
        return {"dve": "dve", "pool": "pool"}[name]

    def TTo(eng, out, in0, in1, op):
        return S.add(eng, lambda e: e.tensor_tensor(out, in0, in1, op), [in0, in1], [out], cost=((100.0 + 1.0 * nfree(out)) if eng != "pool" else (500.0 + 2.0 * nfree(out))))

    def STT(eng, out, in0, scalar, in1, op0, op1):
        return S.add(eng, lambda e: e.scalar_tensor_tensor(out, in0, scalar, in1, op0, op1),
                     aps(in0, scalar, in1), [out], cost=((100.0 + 1.0 * nfree(out)) if eng != "pool" else (500.0 + 2.0 * nfree(out))))

    def TS(eng, out, in0, s1, s2, op0, op1=None):
        if op1 is None:
            return S.add(eng, lambda e: e.tensor_scalar(out, in0, s1, None, op0), aps(in0, s1), [out], cost=((100.0 + 1.0 * nfree(out)) if eng != "pool" else (500.0 + 2.0 * nfree(out))))
        return S.add(eng, lambda e: e.tensor_scalar(out, in0, s1, s2, op0, op1), aps(in0, s1, s2), [out], cost=((100.0 + 1.0 * nfree(out)) if eng != "pool" else (500.0 + 2.0 * nfree(out))))

    def CP(eng, out, in_):
        if eng == "act":
            return S.add("act", lambda e: e.copy(out, in_), [in_], [out], cost=220.0 + 0.75 * nfree(out))
        return S.add(eng, lambda e: e.tensor_copy(out, in_), [in_], [out], cost=((100.0 + 1.0 * nfree(out)) if eng != "pool" else (500.0 + 2.0 * nfree(out))))

    def MSET(eng, ap, val):
        return S.add(eng, lambda e: e.memset(ap, val), [], [ap], cost=((100.0 + 1.0 * nfree(ap)) if eng != "pool" else (500.0 + 2.0 * nfree(ap))))

    def SCAN(out, d0, d1, init, op0, op1):
        return S.add("dve", lambda e: e.tensor_tensor_scan(out, d0, d1, init, op0, op1), aps(d0, d1, init), [out],
                     cost=100.0 + 2.1 * nfree(out))

    def RECIP(out, in_):
        return S.add("dve", lambda e: e.reciprocal(out, in_), [in_], [out], cost=100.0 + 8.0 * nfree(out))

    def DMA(q, out, in_, stream, is_out=False, slow=False, cast=False):
        kw = {}
        if slow:
            kw["allow_slow_non_contiguous"] = True
        if cast:
            kw["max_dma_last_dim"] = 4096
        nbytes = int(np.prod(in_.shape)) * 4
        per_b = 0.008 if cast else (0.05 if slow else 0.004)
        return S.add(q, lambda e: e.dma_start(out=out, in_=in_, **kw), [in_], [out], stream=stream, is_out=is_out,
                     cost=(1000.0 if q == "pool" else 100.0), xfer=nbytes * per_b)

    def psb(bank, parts=128, n=512):
        return PS[0:parts, bank, 0:n]

    ident, _ = AR.alloc((128,), F32)
    maskneg, _ = AR.alloc((128,), F32)
    ones_bf, _ = AR.alloc((128,), BF16)
    esel, _ = AR.alloc((4, 128), F32, parts=4)
    cols, _ = AR.alloc((16, 8), F32)
    G1C, G2C, G3C, GFC, LCB, LBA, LBX, CCOL, MCB, MGC = [cols[:, i, :] for i in range(10)]
    LAMC = cols[:, 10, :]
    C2COL = cols[:, 11, :]
    lcw, _ = AR.alloc((4, 8), F32)
    mcw, _ = AR.alloc((4, 8), F32)
    bgc, _ = AR.alloc((2,), F32, parts=4)
    wa_bf, _ = AR.alloc((8, 128), BF16)
    wx_bf, _ = AR.alloc((8, 128), BF16)
    wq_bf, _ = AR.alloc((8, 256), BF16)
    wk_bf, _ = AR.alloc((8, 256), BF16)
    wv_bf, _ = AR.alloc((8, 256), BF16)
    xnS, _ = AR.alloc((8, NS), BF16)
    mrgS, _ = AR.alloc((8, NS), BF16)
    zT, zT_off = AR.alloc((41, NS), F32)
    zmap = []

    MSET("pool", ident, 1.0)
    S.add("pool", lambda e: e.affine_select(ident, ident, [[1, 128]], ALU.is_equal, 0.0, base=0, channel_multiplier=-1),
          [ident], [ident])
    idrep, _ = AR.alloc((NS, NS), F32)
    MSET("pool", idrep, 1.0)
    S.add("pool", lambda e: e.affine_select(idrep.rearrange("p a b -> p (a b)"), idrep.rearrange("p a b -> p (a b)"),
                                            [[1, NS], [-1, NS]], ALU.is_equal, 0.0, base=0, channel_multiplier=0),
          [idrep], [idrep])
    MSET("pool", maskneg, 0.0)
    S.add("pool", lambda e: e.affine_select(maskneg, maskneg, [[1, 128]], ALU.is_ge, -1.0e4, base=0, channel_multiplier=-1),
          [maskneg], [maskneg])
    MSET("dve", ones_bf, 1.0)
    for h in range(4):
        CP("dve", esel[:, h, :], ident[0:4, h:h + 1].broadcast_to([4, 128]))

    stg, _ = AR.alloc((128,), F32, parts=88, at=zT_off)
    stg2, _ = AR.alloc((128,), F32, parts=64, at=zT_off + 512)
    MSET("dve", stg, 0.0)
    for slot_, src_ in ((0, g_mix), (1, g_ffn), (2, g_ple), (3, g_fin), (4, lru_cb), (5, lru_ba), (6, lru_bx), (10, lru_lam), (8, m_cb)):
        DMA("sp", stg[8 * slot_:8 * slot_ + 8, :], src_.rearrange("(c p) -> c p", p=128), f"c{slot_}")
    DMA("sp", stg[72:80, :], m_g.rearrange("h (j p) -> (h j) p", p=128), "c9")
    DMA("sp", stg2[0:32, :], lru_cw.rearrange("j (c p) -> (j c) p", p=128), "c11")
    DMA("sp", stg2[32:64, :], m_cw.rearrange("j (c p) -> (j c) p", p=128), "c13")
    TR(PS[:, 0, 0:88], stg, ident[0:88, 0:88])
    CP("dve", cols[:, 0:11, :], PS[:, 0, 0:88].rearrange("p (s c) -> p s c", c=8))
    TR(PS[:, 1, 0:64], stg2, ident[0:64, 0:64])
    CP("dve", lcw, PS[:, 1, 0:32].rearrange("p (j c) -> p j c", c=8))
    CP("dve", mcw, PS[:, 1, 32:64].rearrange("p (j c) -> p j c", c=8))
    DMA("sp", bgc, b_gates.rearrange("(g h) -> h g", h=4), "c12", slow=True)
    DMA("pool", wa_bf, lru_wa.rearrange("n c d -> c n d"), "w_a", cast=True)
    DMA("pool", wx_bf, lru_wx.rearrange("n c d -> c n d"), "w_x", cast=True)
    DMA("pool", wq_bf, w_q.rearrange("h (j p) e -> p (h j) e", p=128), "w_q", cast=True)
    DMA("pool", wk_bf, w_k.rearrange("h (j p) e -> p (h j) e", p=128), "w_k", cast=True)
    DMA("pool", wv_bf, w_v.rearrange("h (j p) e -> p (h j) e", p=128), "w_v", cast=True)
    ACT(CCOL, LAMC, AF.Exp, scale=-1.0)
    ACT(CCOL, CCOL, AF.Ln, bias=1.0)
    TS("dve", CCOL, CCOL, -8.0, None, ALU.mult)
    TS("dve", C2COL, CCOL, 2.0, None, ALU.mult)

    WS_BYTES = 3 * 5632
    wreg, wreg_off = AR.alloc((WS_BYTES // 2,), BF16)
    wctr = [0]
    wcfg = {"n": 3}

    def wload(src2d, c0, w, kparts=8):
        n = wcfg["n"]
        sz = WS_BYTES // 2 // n
        assert kparts * w <= sz, (kparts, w, sz)
        s = wctr[0] % n
        wctr[0] += 1
        dst = wreg[:, s * sz:s * sz + kparts * w].rearrange("p (a b) -> p a b", b=w)
        DMA("pool", dst, src2d[:, c0:c0 + w].rearrange("(k p) n -> p k n", p=128), f"ws{n}_{s}", cast=True)
        return dst

    bctr = [0]

    def nbank(lo=0, hi=8):
        b = lo + bctr[0] % (hi - lo)
        bctr[0] += 1
        return b

    st1 = {}

    def alloc_stage1(n=2):
        st1["n"] = n
        st1["xin"], _ = AR.alloc((n, D), F32)
        st1["xsc"], _ = AR.alloc((n, D), F32)
        st1["nstat"], _ = AR.alloc((n, 4), F32)
        st1["sq"], _ = AR.alloc((D,), BF16)

    mk0 = AR.top
    alloc_stage1()

    def load_tiles(mode, dst, dstS, tiles):
        for tt in tiles:
            npart = 128 if tt < 16 else NS
            src_ = xp[tt * 128:(tt + 1) * 128, :] if tt < 16 else xs
            sl = tt % st1["n"]
            xin, xsc, nstat, sq_scr = st1["xin"], st1["xsc"], st1["nstat"], st1["sq"]
            xi = xin[0:npart, sl, :]
            DMA("sp", xi, src_, f"xin{sl}")
            tin = xi
            if mode == "norm":
                ss = nstat[0:npart, sl, 0:1]
                rs = nstat[0:npart, sl, 1:2]
                ACT(sq_scr[0:npart, :], xi, AF.Square, accum_out=ss)
                ACT(rs, ss, AF.Ln, bias=EPS, scale=1.0 / D)
                ACT(rs, rs, AF.Exp, scale=-0.5)
                tin = xsc[0:npart, sl, :]
                TS("dve", tin, xi, rs, None, ALU.mult)
            for half in range(2):
                b = nbank()
                for c in range(4):
                    kc = half * 4 + c
                    TR(PS[:, b, c * 128:c * 128 + npart], tin[:, kc * 128:(kc + 1) * 128], ident[0:npart, 0:npart])
                pv = PS[:, b, :].rearrange("p (c t) -> p c t", t=128)[:, :, 0:npart]
                if tt < 16:
                    dv = dst[:, half * 4:half * 4 + 4, tt * 128:tt * 128 + npart]
                else:
                    dv = dstS[:, half * 4:half * 4 + 4, :]
                if mode == "norm":
                    TTo("dve", dv, pv, G1C[:, half * 4:half * 4 + 4].unsqueeze(2).broadcast_to([128, 4, npart]), ALU.mult)
                else:
                    CP("act", dv, pv)

    load_tiles("norm", None, xnS, [16])

    def phase_S():
        AR.top = PB
        MARKS.append(("S", len(S.ops)))
        zs, _ = AR.alloc((NIN,), F32, parts=NS)
        for zi_, (zc_, c0_, w_) in enumerate(zmap):
            b = nbank()
            if w_ == 128:
                TR(PS[0:NS, b, 0:128], zT[:, zc_, :], ident)
            else:
                TR(PS[0:NS, b, 0:w_], zT[0:w_, zc_, :], ident[0:w_, 0:w_])
            CP("act" if zi_ % 2 == 0 else "dve", zs[:, c0_:c0_ + w_], PS[0:NS, b, 0:w_])

        bcn = [0]

        MARKS.append(("S_chain", len(S.ops)))
        def bcload(src_flat, n, t=None):
            if t is None:
                t, _ = AR.alloc((n,), F32, parts=NS)
            bcn[0] += 1
            DMA("sp", t, src_flat.partition_broadcast(NS), f"b{bcn[0]}")
            return t

        lam_b = bcload(lru_lam, D)
        bg_b = bcload(b_gates, 8)
        cwt, _ = AR.alloc((3, D), F32, parts=NS)
        rot = [0]

        def prow(src_flat):
            rot[0] += 1
            return bcload(src_flat, D, t=cwt[:, rot[0] % 3, :])

        cs_l, _ = AR.alloc((3, D), F32, parts=NS)
        cs_m = cs_l
        h0, _ = AR.alloc((D,), F32, parts=NS)
        n0, _ = AR.alloc((D,), F32, parts=NS)
        m0, _ = AR.alloc((4,), F32, parts=NS)
        DMA("sp", cs_l, st_lconv, "b20")
        DMA("sp", h0, st_lh, "b22")
        DMA("sp", n0, st_n.rearrange("b h e -> b (h e)"), "b23")
        DMA("sp", m0, st_m, "b24")
        ACT(lam_b, lam_b, AF.Exp, scale=-1.0)
        ACT(lam_b, lam_b, AF.Ln, bias=1.0)
        TS("dve", lam_b, lam_b, -8.0, None, ALU.mult)
        cc_b = lam_b

        ta, _ = AR.alloc((D,), F32, parts=NS)
        tb, _ = AR.alloc((D,), F32, parts=NS)
        tc_, _ = AR.alloc((D,), F32, parts=NS)
        td, _ = AR.alloc((D,), F32, parts=NS)
        te, _ = AR.alloc((D,), F32, parts=NS)
        tT, _ = AR.alloc((8, NS), BF16)
        tT2, _ = AR.alloc((8, NS), BF16)
        qTf, _ = AR.alloc((8, NS), F32)
        mS, _ = AR.alloc((D,), F32, parts=NS)
        sm, _ = AR.alloc((16, 4), F32, parts=NS)

        xl_s = zs[:, 0:D]
        xm_s = zs[:, D:2 * D]
        o_s = zs[:, 2 * D:3 * D]
        ig_s = zs[:, 3 * D:3 * D + 4]
        fg_s = zs[:, 3 * D + 4:3 * D + 8]
        gl_s = zs[:, 3 * D + 8:4 * D + 8]
        gm_s = zs[:, 4 * D + 8:5 * D + 8]

        def convS(out, xnew, cs, wsrc, bsrc):
            b_b = prow(bsrc)
            for j in range(4):
                wt = prow(wsrc[j, :])
                xj = cs[:, j, :] if j < 3 else xnew
                if j == 0:
                    TTo("dve", out, xj, wt, ALU.mult)
                    TTo("dve", out, out, b_b, ALU.add)
                else:
                    TTo("dve", ta, xj, wt, ALU.mult)
                    TTo("dve", out, out, ta, ALU.add)

        def transS(dstT, src_):
            b = nbank()
            for kc in range(8):
                TR(PS[:, b, kc * NS:(kc + 1) * NS], src_[:, kc * 128:(kc + 1) * 128], ident[0:NS, 0:NS])
            CP("dve", dstT, PS[:, b, 0:8 * NS].rearrange("p (c t) -> p c t", t=NS))

        def wideS(fn_mm):
            b0_, b1_ = nbank(), nbank()
            fn_mm(lambda col, w: PS[0:NS, b0_ if col < 512 else b1_, (col % 512):(col % 512) + w])
            return [PS[0:NS, b0_, :], PS[0:NS, b1_, :]]

        DMA("sp", o_slconv[:, 0:2, :], cs_l[:, 1:3, :], "o0", is_out=True)
        DMA("sp", o_slconv[:, 2, :], xl_s, "o1", is_out=True)
        convS(tb, xl_s, cs_l, lru_cw, lru_cb)
        transS(tT, tb)

        def mm_gate(wbf):
            def f(dst):
                for n in range(8):
                    MM(dst(n * 128, 128), tT[:, n, :], wbf[:, n, :], True, True)
            return f

        pr = wideS(mm_gate(wa_bf))
        pi = wideS(mm_gate(wx_bf))
        lba_b = prow(lru_ba)
        lbx_b = prow(lru_bx)
        for hh in range(2):
            sl_ = slice(hh * 512, (hh + 1) * 512)
            TTo("dve", tc_[:, sl_], pr[hh], lba_b[:, sl_], ALU.add)
            TTo("dve", td[:, sl_], pi[hh], lbx_b[:, sl_], ALU.add)
        ACT(tc_, tc_, AF.Sigmoid)
        ACT(td, td, AF.Sigmoid)
        TTo("dve", tc_, tc_, cc_b, ALU.mult)
        ACT(tc_, tc_, AF.Exp)
        TTo("dve", te, tc_, tc_, ALU.mult)
        ACT(te, te, AF.Sqrt, bias=1.0, scale=-1.0)
        TTo("dve", te, te, td, ALU.mult)
        TTo("dve", te, te, tb, ALU.mult)
        TTo("dve", tc_, tc_, h0, ALU.mult)
        TTo("dve", tc_, tc_, te, ALU.add)
        DMA("sp", o_slh, tc_, "o2", is_out=True)
        ACT(td, gl_s, AF.Sigmoid)
        TTo("dve", mS, td, tc_, ALU.mult)

        DMA("sp", cs_m, st_mconv, "b21")
        DMA("sp", o_smconv[:, 0:2, :], cs_m[:, 1:3, :], "o3", is_out=True)
        DMA("sp", o_smconv[:, 2, :], xm_s, "o4", is_out=True)
        convS(tb, xm_s, cs_m, m_cw, m_cb)
        ACT(tb, tb, AF.Silu)
        transS(tT, tb)
        transS(tT2, xm_s)
        qS, _ = AR.alloc((D,), F32, parts=NS)
        kS, _ = AR.alloc((D,), F32, parts=NS)
        vS, _ = AR.alloc((D,), F32, parts=NS)

        def mm_qkv(wbf, xT):
            def f(dst):
                for h in range(4):
                    for j in range(2):
                        MM(dst(h * 256, 256), xT[:, 2 * h + j, :], wbf[:, 2 * h + j, :], j == 0, j == 1)
            return f

        pq = wideS(mm_qkv(wq_bf, tT))
        for hh in range(2):
            ACT(qS[:, hh * 512:(hh + 1) * 512], pq[hh], AF.Copy, scale=1.0 / 16.0)
        pk = wideS(mm_qkv(wk_bf, tT))
        for hh in range(2):
            CP("dve", kS[:, hh * 512:(hh + 1) * 512], pk[hh])
        pv_ = wideS(mm_qkv(wv_bf, tT2))
        for hh in range(2):
            CP("act", vS[:, hh * 512:(hh + 1) * 512], pv_[hh])
        SM = lambda k: sm[:, k, :]
        TTo("dve", SM(0), ig_s, bg_b[:, 0:4], ALU.add)
        TTo("dve", SM(1), fg_s, bg_b[:, 4:8], ALU.add)
        ACT(SM(1), SM(1), AF.Exp, scale=-1.0)
        ACT(SM(1), SM(1), AF.Ln, bias=1.0)
        TTo("dve", SM(2), m0, SM(1), ALU.subtract)
        TTo("dve", SM(3), SM(2), SM(0), ALU.max)
        DMA("sp", o_sm, SM(3), "o5", is_out=True)
        TTo("dve", SM(4), SM(0), SM(3), ALU.subtract)
        ACT(SM(4), SM(4), AF.Exp)
        TTo("dve", SM(5), SM(2), SM(3), ALU.subtract)
        ACT(SM(5), SM(5), AF.Exp)
        ACT(SM(6), SM(3), AF.Exp, scale=-1.0)
        TTo("dve", ta, qS, kS, ALU.mult)
        S.add("dve", lambda e: e.tensor_reduce(SM(7), ta.rearrange("p (h e) -> p h e", e=256), AX.X, ALU.add),
              [ta], [SM(7)])
        TTo("dve", ta, qS, n0, ALU.mult)
        S.add("dve", lambda e: e.tensor_reduce(SM(8), ta.rearrange("p (h e) -> p h e", e=256), AX.X, ALU.add),
              [ta], [SM(8)])
        TTo("dve", SM(9), SM(7), SM(4), ALU.mult)
        TTo("dve", SM(10), SM(5), SM(8), ALU.mult)
        TTo("dve", SM(10), SM(10), SM(9), ALU.add)
        STT("dve", SM(14), SM(10), -1.0, SM(10), ALU.mult, ALU.max)
        TTo("dve", SM(10), SM(14), SM(6), ALU.max)
        RECIP(SM(11), SM(10))
        for h in range(4):
            hs = slice(h * 256, (h + 1) * 256)
            TS("dve", ta[:, hs], kS[:, hs], sm[:, 4, h:h + 1], None, ALU.mult)
            STT("dve", tb[:, hs], n0[:, hs], sm[:, 5, h:h + 1], ta[:, hs], ALU.mult, ALU.add)
        DMA("sp", o_sn.rearrange("b h e -> b (h e)"), tb, "o6", is_out=True)
        kwS = ta
        b = nbank()
        for kc in range(8):
            TR(PS[:, b, kc * NS:(kc + 1) * NS], qS[:, (kc // 2) * 256 + kc % 2:(kc // 2 + 1) * 256:2], ident[0:NS, 0:NS])
        CP("dve", qTf, PS[:, b, 0:8 * NS].rearrange("p (c t) -> p c t", t=NS))
        scd, _ = AR.alloc((NS, 4), F32, parts=NS)
        scbc, _ = AR.alloc((NS, 4), F32)
        ones16, _ = AR.alloc((128,), F32, parts=NS)
        MSET("dve", ones16, 1.0)
        TTo("dve", scd, sm[:, 5, :].unsqueeze(1).broadcast_to([NS, NS, 4]),
            ident[0:NS, 0:NS].unsqueeze(2).broadcast_to([NS, NS, 4]), ALU.mult)
        b = nbank()
        MM(PS[:, b, 0:64], ones16, scd.rearrange("p a b -> p (a b)"), True, True)
        CP("dve", scbc.rearrange("p a b -> p (a b)"), PS[:, b, 0:64])
        vbf, _ = AR.alloc((D,), BF16, parts=NS)
        CP("act", vbf, vS)
        qmk, _ = AR.alloc((2, 8, NS), F32)
        kmk, _ = AR.alloc((2, D), BF16, parts=NS)
        C0t, _ = AR.alloc((2, 8, 256), F32, at=mk0)
        Cnt, _ = AR.alloc((2, 8, 256), F32, at=mk0 + 16384)
        numS, _ = AR.alloc((D,), F32, parts=NS, at=wreg_off + 12288)
        for h in range(4):
            pass
        def c0_load(bsm):
            DMA("sp", C0t[:, bsm % 2, :, :].rearrange("p (h r) e -> p h (r e)", r=2),
                st_C[bsm].rearrange("h (p r) e -> p h (r e)", r=2), f"c0_{bsm % 2}")

        MARKS.append(("S_loop", len(S.ops)))
        c0_load(0)
        for bsm in range(NS):
            s3 = bsm % 2
            s2 = bsm % 2
            if bsm + 1 < NS:
                c0_load(bsm + 1)
            TTo("dve", qmk[:, s2, :, :], qTf, idrep[:, bsm:bsm + 1, :].broadcast_to([128, 8, NS]), ALU.mult)
            TS("dve", kmk[:, s2, :], kwS, ident[0:NS, bsm:bsm + 1], None, ALU.mult)
            for h in range(4):
                for j in range(2):
                    MM(PS[0:NS, 4 + h, 0:256], qmk[:, s2, 2 * h + j, :], C0t[:, s3, 2 * h + j, :],
                       bsm == 0 and j == 0, bsm == NS - 1 and j == 1)
            for h in range(4):
                bb = nbank(0, 4)
                for j in range(2):
                    MM(PS[:, bb, j * 256:(j + 1) * 256], kmk[:, s2, h * 256 + j:(h + 1) * 256:2], vbf[:, h * 256:(h + 1) * 256],
                       True, True)
                STT("dve", Cnt[:, s2, 2 * h:2 * h + 2, :], C0t[:, s3, 2 * h:2 * h + 2, :], scbc[:, bsm, h:h + 1],
                    PS[:, bb, :].rearrange("p (j e) -> p j e", e=256), ALU.mult, ALU.add)
            DMA(KNOB.get("cst_q", "act"), o_sC[bsm].rearrange("h (p r) e -> p h (r e)", r=2),
                Cnt[:, s2, :, :].rearrange("p (h r) e -> p h (r e)", r=2), f"oc{s2}", is_out=True)
        for h in range(4):
            CP("act", numS[:, h * 256:(h + 1) * 256], PS[0:NS, 4 + h, 0:256])
        for h in range(4):
            hs = slice(h * 256, (h + 1) * 256)
            TS("dve", numS[:, hs], numS[:, hs], sm[:, 5, h:h + 1], None, ALU.mult)
            STT("dve", numS[:, hs], vS[:, hs], sm[:, 9, h:h + 1], numS[:, hs], ALU.mult, ALU.add)
            TS("dve", numS[:, hs], numS[:, hs], sm[:, 11, h:h + 1], None, ALU.mult)
            ACT(tb[:, hs], numS[:, hs], AF.Square, accum_out=sm[:, 12, h:h + 1])
        ACT(SM(13), SM(12), AF.Ln, bias=EPS, scale=1.0 / 256.0)
        ACT(SM(13), SM(13), AF.Exp, scale=-0.5)
        mg_b = prow(m_g.rearrange("h e -> (h e)"))
        for h in range(4):
            hs = slice(h * 256, (h + 1) * 256)
            STT("dve", numS[:, hs], numS[:, hs], sm[:, 13, h:h + 1], mg_b[:, hs], ALU.mult, ALU.mult)
        ACT(td, o_s, AF.Sigmoid)
        TTo("dve", numS, numS, td, ALU.mult)
        ACT(td, gm_s, AF.Sigmoid)
        TTo("dve", numS, numS, td, ALU.mult)
        TTo("dve", mS, mS, numS, ALU.add)
        transS(mrgS, mS)

    try:
        MARKS.append(("P", len(S.ops)))
        AR.top = mk0
        xn, _ = AR.alloc((8, T), BF16)
        mrg, _ = AR.alloc((8, T), BF16)
        PB = AR.top
        alloc_stage1(KNOB.get("st1_slots", 4))
        load_tiles("norm", xn, None, list(range(16)))
        AR.top = PB
        gX2, _ = AR.alloc((T,), F32, parts=4)
        gX3, _ = AR.alloc((T,), F32, parts=4)
        gcol, _ = AR.alloc((16, 4), F32)
        nbf, _ = AR.alloc((2,), F32, parts=4)
        bufA, _ = AR.alloc((2, T + 4), F32)
        xc, _ = AR.alloc((2, T), BF16)
        qTb, _ = AR.alloc((2, T + 4), BF16)
        qT = qTb[:, :, 0:T]
        qsc, _ = AR.alloc((2, T), BF16)
        kT, _ = AR.alloc((2, T), BF16)
        kw, kwoff = AR.alloc((16, 256), BF16)
        gX1, _ = AR.alloc((T,), F32, parts=4, at=kwoff)
        vt, _ = AR.alloc((16, 258), BF16)
        DT, _ = AR.alloc((16, 128), BF16)
        LB0 = bufA
        Caug, _ = AR.alloc((2, 258), F32)
        Cb2, _ = AR.alloc((2, 2, 258), BF16)
        nb2, _ = AR.alloc((2, 2, 128), BF16)
        Gl, _ = AR.alloc((17,), F32)
        decb, _ = AR.alloc((16,), F32)
        wkc, _ = AR.alloc((16,), F32)
        dd2, _ = AR.alloc((2, 128), F32)
        hT2, _ = AR.alloc((2, 2, 128), F32)
        sqh2, _ = AR.alloc((2, 2, 128), BF16)
        rsh, _ = AR.alloc((128,), F32)
        sd, _ = AR.alloc((128,), BF16)
        dtmp, _ = AR.alloc((4, 128), F32)
        sg = dtmp.rearrange("p a b -> p (a b)")
        gm1, _ = AR.alloc((1,), F32, parts=4)
        xmp = bufA
        hn = bufA[:, :, 0:T]
        scb = bufA[:, 0, 0:T]
        xmb = qTb
        xmlast, _ = AR.alloc((2, 4), F32)
        xllast, _ = AR.alloc((4,), F32)
        dgm, _ = AR.alloc((2, 4, 128), BF16)
        dgl, _ = AR.alloc((4, 128), BF16)
        xcoff = region(xc)[3]
        emb = At[:, xcoff // 4:xcoff // 4 + T]
        ktoff = region(kT)[3]
        ctmp = At[:, ktoff // 4:ktoff // 4 + T]
        def _f32(off, n_):
            return At[:, off // 4:off // 4 + n_]
        oA, oQ, oK, oW, oV = (region(b_)[3] for b_ in (bufA, qsc, kT, kw, vt))
        xlc = _f32(oA, T)
        rr = _f32(oA + T * 4, T)
        ii = _f32(oQ, T)
        uu = _f32(oK, T)
        hh_ = _f32(oW, T)
        xlb = At[:, oV // 4:oV // 4 + (T + 4) // 2].bitcast(BF16)
        xlcb = At[:, (oV + (T + 4) * 2) // 4:(oV + (T + 4) * 2) // 4 + T // 2].bitcast(BF16)
        assert (T + 4) * 2 + T * 2 <= 16 * 258 * 2 and 2 * T * 4 <= 2 * (T + 4) * 4

        def sweepP(wb, w, evac, rhs, c0, nk=8):
            for mi in range(w // 128):
                b = nbank(*KNOB["pbanks"])
                for kc in range(nk):
                    MM(PS[:, b, 0:NS], wb[:, kc, mi * 128:(mi + 1) * 128], xnS[:, kc, :], kc == 0, kc == nk - 1)
                CP("act", zT[:, len(zmap), :], PS[:, b, 0:NS])
                zmap.append((len(zmap), c0 + mi * 128, 128))
                for nt in range(4):
                    b = nbank(*KNOB["pbanks"])
                    for kc in range(nk):
                        MM(PS[:, b, :], wb[:, kc, mi * 128:(mi + 1) * 128], rhs[:, kc, nt * 512:(nt + 1) * 512], kc == 0, kc == nk - 1)
                    evac(mi, nt, PS[:, b, :])

        TS("dve", nbf, bgc, -1.0, None, ALU.mult)
        wb = wload(w_in, 3 * D, 8)
        b = nbank(*KNOB["pbanks"])
        for kc in range(8):
            MM(PS[0:8, b, 0:NS], wb[:, kc, 0:8], xnS[:, kc, :], kc == 0, kc == 7)
        CP("act", zT[0:8, 40, :], PS[0:8, b, 0:NS])
        zmap_gates = (40, 3 * D, 8)
        for nt in range(4):
            ns = slice(nt * 512, (nt + 1) * 512)
            b = nbank(*KNOB["pbanks"])
            for kc in range(8):
                MM(PS[0:4, b, :], wb[:, kc, 0:4], xn[:, kc, ns], kc == 0, kc == 7)
            ACT(gX1[:, ns], PS[0:4, b, :], AF.Identity, bias=bgc[:, 0:1])
            b = nbank(*KNOB["pbanks"])
            for kc in range(8):
                MM(PS[0:4, b, :], wb[:, kc, 4:8], xn[:, kc, ns], kc == 0, kc == 7)
            ACT(gX2[:, ns], PS[0:4, b, :], AF.Exp, bias=nbf[:, 1:2], scale=-1.0)
        ACT(gX2, gX2, AF.Ln, bias=1.0)
        SCAN(gX3, gX2, gX2, 0.0, ALU.add, ALU.max)
        TTo("dve", gX1, gX1, gX3, ALU.add)
        SCAN(gX2, gX1, gX1, NEG, ALU.max, ALU.max)
        TTo("dve", gX3, gX3, gX2, ALU.subtract)
        TS("dve", gm1, gX3[:, T - 1:T], -1.0, None, ALU.mult)
        DMA("sp", o_pm.rearrange("o h -> h o"), gm1, "o7", is_out=True, slow=True)
        ACT(gX3, gX3, AF.Exp)
        b = nbank(*KNOB["pbanks"])
        for c in range(16):
            TR(PS[:, b, c * 4:(c + 1) * 4], gX1[:, c * 128:(c + 1) * 128], ident[0:4, 0:4])
        CP("dve", gcol, PS[:, b, 0:64].rearrange("p (c h) -> p c h", h=4))
        if LEVEL == 1:
            raise _Stop()
        gG, gEm = gX2, gX3

        for h in range(4):
            MARKS.append((f"P_h{h}_prep", len(S.ops)))
            MSET("dve", xmb[:, :, 0:4], 0.0)
            for mi in range(2):
                for j in range(4):
                    TS("dve", dgm[:, mi, j, :], ident, mcw[:, j, 2 * h + mi:2 * h + mi + 1], None, ALU.mult)
            wb = wload(w_in, D + h * 256, 256)

            def ev_xm(mi, nt, ps):
                CP("act", xmb[:, mi, 4 + nt * 512:4 + (nt + 1) * 512], ps)
                if nt == 3:
                    CP("dve", xmlast[:, mi, 0:3], ps[:, 509:512])
            sweepP(wb, 256, ev_xm, xn, D + h * 256)
            for mi in range(2):
                DMA("sp", o_pmconv[:, h * 256 + mi * 128:h * 256 + (mi + 1) * 128].rearrange("r p -> p r"), xmlast[:, mi, 0:3],
                    "o8", is_out=True, slow=True)
            MSET("dve", vt[:, :, 256:258], 1.0)
            for c2 in range(8):
                b = nbank(*KNOB["pbanks"])
                for cc in range(2):
                    c = 2 * c2 + cc
                    for dc in range(2):
                        MM(PS[:, b, cc * 256:(cc + 1) * 256], xmb[:, dc, 4 + c * 128:4 + (c + 1) * 128], wv_bf[:, 2 * h + dc, :], dc == 0, dc == 1)
                CP("act", vt[:, 2 * c2:2 * c2 + 2, 0:256], PS[:, b, :].rearrange("p (c e) -> p c e", e=256))
            for mi in range(2):
                kcg = 2 * h + mi
                for nt in range(4):
                    b = nbank(*KNOB["pbanks"])
                    for j in range(4):
                        MM(PS[:, b, :], dgm[:, mi, j, :], xmb[:, mi, nt * 512 + j + 1:nt * 512 + j + 513], j == 0, j == 3)
                    ACT(xc[:, mi, nt * 512:(nt + 1) * 512], PS[:, b, :], AF.Silu, bias=MCB[:, kcg:kcg + 1])
            if LEVEL == 2:
                raise _Stop()
            MSET("dve", Gl[:, 0:1], NEG)
            for nt in range(4):
                b = nbank(*KNOB["pbanks"])
                MM(PS[:, b, :], esel[:, h, :], gG[:, nt * 512:(nt + 1) * 512], True, True)
                CP("act", Gl[:, 1 + 4 * nt:5 + 4 * nt], PS[:, b, 127::128])
                for cc in range(4):
                    c = 4 * nt + cc
                    ACT(scb[:, c * 128:(c + 1) * 128], PS[:, b, cc * 128:(cc + 1) * 128], AF.Exp, bias=Gl[:, c:c + 1], scale=-1.0)
                STT("dve", dtmp, PS[:, b, :].rearrange("p (c t) -> p c t", t=128), -1.0,
                    maskneg.unsqueeze(1).broadcast_to([128, 4, 128]), ALU.mult, ALU.add)
                for cc in range(4):
                    c = 4 * nt + cc
                    ACT(DT[:, c, :], dtmp[:, cc, :], AF.Exp, bias=gcol[:, c, h:h + 1])
            TTo("dve", decb, Gl[:, 0:16], Gl[:, 1:17], ALU.subtract)
            ACT(decb, decb, AF.Exp)
            TTo("dve", wkc, gcol[:, :, h], Gl[:, 1:17], ALU.subtract)
            ACT(wkc, wkc, AF.Exp)
            for j in range(2):
                for nt in range(4):
                    ns = slice(nt * 512, (nt + 1) * 512)
                    b = nbank(*KNOB["pbanks"])
                    for dc in range(2):
                        MM(PS[:, b, :], wk_bf[:, 2 * h + dc, j * 128:(j + 1) * 128], xc[:, dc, ns], dc == 0, dc == 1)
                    CP(KNOB.get("kev", "dve"), kT[:, j, ns], PS[:, b, :])
            for c2 in range(8):
                b = nbank(*KNOB["pbanks"])
                for cc in range(2):
                    c = 2 * c2 + cc
                    for dc in range(2):
                        MM(PS[:, b, cc * 256:(cc + 1) * 256], xc[:, dc, c * 128:(c + 1) * 128], wk_bf[:, 2 * h + dc, :], dc == 0, dc == 1)
                for cc in range(2):
                    c = 2 * c2 + cc
                    TS("dve", kw[:, c, :], PS[:, b, cc * 256:(cc + 1) * 256], wkc[:, c:c + 1], None, ALU.mult)
            for j in range(2):
                for nt in range(4):
                    ns = slice(nt * 512, (nt + 1) * 512)
                    b = nbank(*KNOB["pbanks"])
                    for dc in range(2):
                        MM(PS[:, b, :], wq_bf[:, 2 * h + dc, j * 128:(j + 1) * 128], xc[:, dc, ns], dc == 0, dc == 1)
                    ACT(qT[:, j, ns], PS[:, b, :], AF.Copy, scale=1.0 / 16.0)
                    STT("dve", qsc[:, j, ns], PS[:, b, :], 1.0 / 16.0, scb[:, ns], ALU.mult, ALU.mult)
            for nt in range(4):
                b = nbank(*KNOB["pbanks"])
                MM(PS[:, b, :], esel[:, h, :], gEm[:, nt * 512:(nt + 1) * 512], True, True)
                CP("act", emb[:, nt * 512:(nt + 1) * 512], PS[:, b, :])
            if LEVEL == 3:
                raise _Stop()
            MARKS.append((f"P_h{h}_loop", len(S.ops)))
            MSET("dve", Caug, 0.0)
            MSET("dve", Cb2, 0.0)
            MSET("dve", nb2, 0.0)

            def stageA(c):
                cs_ = slice(c * 128, (c + 1) * 128)
                nbk = 4 + c % 2
                k_ = c % 2
                MM(PS[:, 6, 0:257], kw[:, c, 0:128], vt[:, c, 0:257], True, True)
                MM(PS[:, 7, 128:385], kw[:, c, 128:256], vt[:, c, 0:257], True, True)
                for dc in range(2):
                    MM(PS[:, 3, 0:128], kT[:, dc, cs_], qT[:, dc, cs_], dc == 0, dc == 1)
                TTo("dve", sd, PS[:, 3, 0:128], DT[:, c, :], ALU.mult)
                Cbp = Cb2[:, 1 - k_, :, :]
                nbp = nb2[:, 1 - k_, :, :]
                for j in range(2):
                    MM(PS[:, nbk, j * 128:(j + 1) * 128], vt[:, c, j * 128:(j + 1) * 128], sd, True, False)
                    for dc in range(2):
                        MM(PS[:, nbk, j * 128:(j + 1) * 128], Cbp[:, dc, j * 128:(j + 1) * 128], qsc[:, dc, cs_], False, dc == 1)
                MM(PS[:, nbk, 256:384], ones_bf, sd, True, False)
                for dc in range(2):
                    MM(PS[:, nbk, 256:384], nbp[:, dc, :], qsc[:, dc, cs_], False, dc == 1)
                STT("dve", Caug[:, 0, 0:257], Caug[:, 0, 0:257], decb[:, c:c + 1], PS[:, 6, 0:257], ALU.mult, ALU.add)
                STT("dve", Caug[:, 1, 0:257], Caug[:, 1, 0:257], decb[:, c:c + 1], PS[:, 7, 128:385], ALU.mult, ALU.add)
                CP("act", Cb2[:, k_, :, :], Caug)
                CP(KNOB.get("nbev", "pool"), nb2[:, k_, :, :], Caug[:, :, 256:257].broadcast_to([128, 2, 128]))

            def stageB(c):
                cs_ = slice(c * 128, (c + 1) * 128)
                nbk = 4 + c % 2
                k_ = c % 2
                ACT(dd2[:, k_, :], PS[:, nbk, 256:384], AF.Abs)
                TTo("dve", dd2[:, k_, :], dd2[:, k_, :], emb[:, cs_], ALU.max)
                ACT(dd2[:, k_, :], dd2[:, k_, :], AF.Ln)
                ACT(dd2[:, k_, :], dd2[:, k_, :], AF.Exp, scale=-1.0)
                TTo("dve", hT2[:, k_, :, :], PS[:, nbk, 0:256].rearrange("p (j t) -> p j t", t=128),
                    dd2[:, k_, :].unsqueeze(1).broadcast_to([128, 2, 128]), ALU.mult)
                ACT(sqh2[:, k_, :, :], hT2[:, k_, :, :], AF.Square)

            def stageC(c):
                cs_ = slice(c * 128, (c + 1) * 128)
                k_ = c % 2
                for j in range(2):
                    MM(PS[:, 7, 0:128], ones_bf, sqh2[:, k_, j, :], j == 0, j == 1)
                ACT(rsh, PS[:, 7, 0:128], AF.Ln, bias=EPS, scale=1.0 / 256.0)
                ACT(rsh, rsh, AF.Exp, scale=-0.5)
                for j in range(2):
                    STT("dve", hn[:, j, cs_], hT2[:, k_, j, :], MGC[:, 2 * h + j:2 * h + j + 1], rsh, ALU.mult, ALU.mult)

            for i_ in range(16 + 2):
                if i_ < 16:
                    stageA(i_)
                if 0 <= i_ - 1 < 16:
                    stageB(i_ - 1)
                if 0 <= i_ - 2 < 16:
                    stageC(i_ - 2)
            DMA("sp", o_pC[h].rearrange("(j p) e -> p j e", p=128), Caug[:, :, 0:256], "o9", is_out=True)
            DMA("sp", o_pn[h].rearrange("(j p) -> p j", p=128), Caug[:, :, 256], "o10", is_out=True, slow=True)
            if LEVEL == 4:
                raise _Stop()
            MARKS.append((f"P_h{h}_gate", len(S.ops)))
            for gi_, c0 in enumerate((2 * D + h * 256, 4 * D + 8 + h * 256)):
                wb = wload(w_in, c0, 256)

                def ev_gate(mi, nt, ps, gi_=gi_):
                    ns = slice(nt * 512, (nt + 1) * 512)
                    sg_ = emb[:, ns]
                    ACT(sg_, ps, AF.Sigmoid)
                    if gi_ == 0:
                        TTo("dve", hn[:, mi, ns], hn[:, mi, ns], sg_, ALU.mult)
                    else:
                        TTo("dve", mrg[:, 2 * h + mi, ns], hn[:, mi, ns], sg_, ALU.mult)
                sweepP(wb, 256, ev_gate, xn, c0)
            if LEVEL == 5:
                raise _Stop()
            MARKS.append((f"P_h{h}_lru", len(S.ops)))
            for mi in range(2):
                n = 2 * h + mi
                MSET("dve", xlb[:, 0:4], 0.0)
                for j in range(4):
                    TS("dve", dgl[:, j, :], ident, lcw[:, j, n:n + 1], None, ALU.mult)
                wb = wload(w_in, n * 128, 128)

                def ev_xl(mi_, nt, ps):
                    CP("act", xlb[:, 4 + nt * 512:4 + (nt + 1) * 512], ps)
                    if nt == 3:
                        CP("dve", xllast[:, 0:3], ps[:, 509:512])
                sweepP(wb, 128, ev_xl, xn, n * 128)
                DMA("sp", o_plconv[:, n * 128:(n + 1) * 128].rearrange("r p -> p r"), xllast[:, 0:3], "o11", is_out=True, slow=True)
                for nt in range(4):
                    ns = slice(nt * 512, (nt + 1) * 512)
                    b = nbank(*KNOB["pbanks"])
                    for j in range(4):
                        MM(PS[:, b, :], dgl[:, j, :], xlb[:, nt * 512 + j + 1:nt * 512 + j + 513], j == 0, j == 3)
                    ACT(xlc[:, ns], PS[:, b, :], AF.Identity, bias=LCB[:, n:n + 1])
                    TS("dve", xlcb[:, ns], PS[:, b, :], LCB[:, n:n + 1], None, ALU.add)
                for nt in range(4):
                    ns = slice(nt * 512, (nt + 1) * 512)
                    b = nbank(*KNOB["pbanks"])
                    MM(PS[:, b, :], wa_bf[:, n, :], xlcb[:, ns], True, True)
                    ACT(rr[:, ns], PS[:, b, :], AF.Sigmoid, bias=LBA[:, n:n + 1])
                    b = nbank(*KNOB["pbanks"])
                    MM(PS[:, b, :], wx_bf[:, n, :], xlcb[:, ns], True, True)
                    ACT(ii[:, ns], PS[:, b, :], AF.Sigmoid, bias=LBX[:, n:n + 1])
                ACT(uu, rr, AF.Exp, scale=C2COL[:, n:n + 1])
                ACT(rr, rr, AF.Exp, scale=CCOL[:, n:n + 1])
                if KNOB.get("sqrt_explog", False):
                    ACT(uu, uu, AF.Ln, bias=1.0, scale=-1.0)
                    ACT(uu, uu, AF.Exp, scale=0.5)
                else:
                    ACT(uu, uu, AF.Sqrt, bias=1.0, scale=-1.0)
                if KNOB.get("ix", False):
                    TTo(KNOB.get("ixeng", "pool"), ii, ii, xlc, ALU.mult)
                    TTo("dve", uu, uu, ii, ALU.mult)
                else:
                    TTo("dve", uu, uu, ii, ALU.mult)
                    TTo("dve", uu, uu, xlc, ALU.mult)
                SCAN(hh_, rr, uu, 0.0, ALU.mult, ALU.add)
                DMA("sp", o_plh[0:1, n * 128:(n + 1) * 128].rearrange("o p -> p o"), hh_[:, T - 1:T], "o12", is_out=True, slow=True)
                wb = wload(w_in, 3 * D + 8 + n * 128, 128)

                def ev_gl(mi_, nt, ps, n=n, mi=mi):
                    ns = slice(nt * 512, (nt + 1) * 512)
                    sg_ = ii[:, ns]
                    ACT(sg_, ps, AF.Sigmoid)
                    TTo("dve", sg_, sg_, hh_[:, ns], ALU.mult)
                    TTo("dve", mrg[:, n, ns], mrg[:, n, ns], sg_, ALU.add)
                sweepP(wb, 128, ev_gl, xn, 3 * D + 8 + n * 128)

        if LEVEL == 6:
            raise _Stop()
        zmap.append(zmap_gates)
        phase_S()
        MARKS.append(("D", len(S.ops)))
        AR.top = PB
        alloc_stage1()
        xres, _ = AR.alloc((8, T), F32)
        xresS, _ = AR.alloc((8, NS), F32)
        sqn, sqoff = AR.alloc((8, 512), BF16)
        rst, _ = AR.alloc((512,), F32)
        sg2, _ = AR.alloc((512,), F32)
        pT, _ = AR.alloc((2, T), BF16, at=sqoff)
        pTS, _ = AR.alloc((2, NS), BF16)
        wple_bf, _ = AR.alloc((2, D), BF16)
        aTS, _ = AR.alloc((12, NS), BF16)
        moff = region(mrg)[3]
        aT = At[:, moff // 4:moff // 4 + 12 * T // 2].bitcast(BF16).rearrange("p (a b) -> p a b", b=T)
        assert moff + 12 * T * 2 <= region(xres)[3], (moff, region(xres))
        scrD = []
        for i_ in range(2):
            o_ = moff + i_ * 10240
            scrD.append((At[:, o_ // 4:o_ // 4 + 2048].bitcast(BF16).rearrange("p (a b) -> p a b", b=512),
                         At[:, (o_ + 8192) // 4:(o_ + 8192) // 4 + 512]))
        load_tiles("raw", xres, xresS, list(range(17)))

        def xsl(bP, bS, m, nt):
            return bP[:, m, nt * 512:(nt + 1) * 512] if nt < 4 else bS[:, m, :]

        def wd_(nt):
            return 512 if nt < 4 else NS

        for blk in range(4):
            wb = wload(w_out, blk * 256, 256)
            for mi in range(2):
                m = blk * 2 + mi
                for nt in range(5):
                    b = nbank()
                    for kc in range(8):
                        MM(PS[:, b, 0:wd_(nt)], wb[:, kc, mi * 128:(mi + 1) * 128], xsl(mrg, mrgS, kc, nt), kc == 0, kc == 7)
                    xr = xsl(xres, xresS, m, nt)
                    TTo("dve", xr, PS[:, b, 0:wd_(nt)], xr, ALU.add)

        def normD(gcols, dstP, dstS, scr=None, after_tile=None):
            for nt in range(5):
                w_ = wd_(nt)
                sq_, rs_ = (sqn, rst) if scr is None else scr[nt % len(scr)]
                for kc in range(8):
                    ACT(sq_[:, kc, 0:w_], xsl(xres, xresS, kc, nt), AF.Square)
                b = nbank()
                for kc in range(8):
                    MM(PS[:, b, 0:w_], ones_bf, sq_[:, kc, 0:w_], kc == 0, kc == 7)
                ACT(rs_[:, 0:w_], PS[:, b, 0:w_], AF.Ln, bias=EPS, scale=1.0 / D)
                ACT(rs_[:, 0:w_], rs_[:, 0:w_], AF.Exp, scale=-0.5)
                for kc in range(8):
                    STT("dve", xsl(dstP, dstS, kc, nt), xsl(xres, xresS, kc, nt), gcols[:, kc:kc + 1], rs_[:, 0:w_], ALU.mult, ALU.mult)
                if after_tile is not None:
                    after_tile(nt)

        MARKS.append(("D_norm2", len(S.ops)))
        normD(G2C, xn, xnS)
        MARKS.append(("D_ffn", len(S.ops)))
        for f0, nf in ((0, 12), (12, 10)):
            for fp in range(nf // 2):
                wg = wload(w_gate, (f0 + 2 * fp) * 128, 256)
                wu = wload(w_up, (f0 + 2 * fp) * 128, 256)
                for fj in range(2):
                    fi = 2 * fp + fj
                    for nt in range(5):
                        w_ = wd_(nt)
                        bg_ = nbank()
                        for kc in range(8):
                            MM(PS[:, bg_, 0:w_], wg[:, kc, fj * 128:(fj + 1) * 128], xsl(xn, xnS, kc, nt), kc == 0, kc == 7)
                        bu_ = nbank()
                        for kc in range(8):
                            MM(PS[:, bu_, 0:w_], wu[:, kc, fj * 128:(fj + 1) * 128], xsl(xn, xnS, kc, nt), kc == 0, kc == 7)
                        ACT(sg2[:, 0:w_], PS[:, bg_, 0:w_], AF.Silu)
                        TTo("dve", xsl(aT, aTS, fi, nt), sg2[:, 0:w_], PS[:, bu_, 0:w_], ALU.mult)
            for m in range(8):
                wdn = wload(w_down[f0 * 128:(f0 + nf) * 128, :], m * 128, 128, kparts=nf)
                for nt in range(5):
                    w_ = wd_(nt)
                    b = nbank()
                    for fi in range(nf):
                        MM(PS[:, b, 0:w_], wdn[:, fi, :], xsl(aT, aTS, fi, nt), fi == 0, fi == nf - 1)
                    xr = xsl(xres, xresS, m, nt)
                    TTo("dve", xr, PS[:, b, 0:w_], xr, ALU.add)
        MARKS.append(("D_norm3", len(S.ops)))
        normD(G3C, xn, xnS, scr=scrD)
        DMA("pool", wple_bf, w_ple.rearrange("(k p) n -> p k n", p=128), "w_ple", cast=True)
        for tt in range(17):
            npart = 128 if tt < 16 else NS
            sl = tt % 2
            pin = st1["xin"][0:npart, sl, 0:PD]
            DMA("sp", pin, pp[tt * 128:(tt + 1) * 128, :] if tt < 16 else psm, f"xin{sl}")
            b = nbank()
            for c in range(2):
                TR(PS[:, b, c * 128:c * 128 + npart], pin[:, c * 128:(c + 1) * 128], ident[0:npart, 0:npart])
            dv = pT[:, :, tt * 128:(tt + 1) * 128] if tt < 16 else pTS
            CP("act", dv, PS[:, b, 0:256].rearrange("p (c t) -> p c t", t=128)[:, :, 0:npart])
        for blk in range(4):
            wb = wload(w_pleg, blk * 256, 256)
            for nt, mi in [(nt, mi) for nt in range(5) for mi in range(2)]:
                if True:
                    m = blk * 2 + mi
                    w_ = wd_(nt)
                    bg_ = nbank()
                    for kc in range(8):
                        MM(PS[:, bg_, 0:w_], wb[:, kc, mi * 128:(mi + 1) * 128], xsl(xn, xnS, kc, nt), kc == 0, kc == 7)
                    bp_ = nbank()
                    for kc in range(2):
                        MM(PS[:, bp_, 0:w_], wple_bf[:, kc, m * 128:(m + 1) * 128], xsl(pT, pTS, kc, nt), kc == 0, kc == 1)
                    ACT(sg2[:, 0:w_], PS[:, bg_, 0:w_], AF.Sigmoid)
                    TTo("dve", sg2[:, 0:w_], sg2[:, 0:w_], PS[:, bp_, 0:w_], ALU.mult)
                    xr = xsl(xres, xresS, m, nt)
                    TTo("dve", xr, xr, sg2[:, 0:w_], ALU.add)
        MARKS.append(("D_final", len(S.ops)))
        yo = st1["xsc"]

        def out_tiles(nt):
            for tt in ([16] if nt == 4 else range(4 * nt, 4 * nt + 4)):
                npart = 128 if tt < 16 else NS
                sl = tt % 2
                for half in range(2):
                    b = nbank()
                    for c in range(4):
                        kc = half * 4 + c
                        src_ = xres[:, kc, tt * 128:(tt + 1) * 128] if tt < 16 else xresS[:, kc, :]
                        TR(PS[0:npart, b, c * 128:(c + 1) * 128], src_, ident)
                    CP("act" if half == 0 else "dve", yo[0:npart, sl, half * 512:(half + 1) * 512], PS[0:npart, b, :])
                DMA("sp", y_p[tt * 128:(tt + 1) * 128, :] if tt < 16 else y_s, yo[0:npart, sl, :], f"yo{sl}", is_out=True)

        normD(GFC, xres, xresS, scr=scrD, after_tile=out_tiles)
    except _Stop:
        pass
    S.emit()
    es.close()
    return nc


OUT_SPECS = [
    ("y_p", (T, D)), ("y_s", (NS, D)), ("o_plconv", (3, D)), ("o_plh", (1, D)), ("o_pmconv", (3, D)),
    ("o_pC", (4, 256, 256)), ("o_pn", (4, 256)), ("o_pm", (1, 4)),
    ("o_slconv", (NS, 3, D)), ("o_slh", (NS, D)), ("o_smconv", (NS, 3, D)),
    ("o_sC", (NS, 4, 256, 256)), ("o_sn", (NS, 4, 256)), ("o_sm", (NS, 4)),
]

_NC_CACHE = []


def kernel(**inputs):
    f = lambda a: np.ascontiguousarray(np.asarray(a, dtype=np.float32))
    I = {k: f(v) for k, v in inputs.items()}
    if not _NC_CACHE:
        _NC_CACHE.append(build())
    nc = _NC_CACHE[0]
    wnames = ["norm_mix_g", "w_in", "b_gates", "lru_conv_w", "lru_conv_b", "lru_w_a", "lru_b_a", "lru_w_x", "lru_b_x",
              "lru_lambda", "mlstm_conv_w", "mlstm_conv_b", "w_q", "w_k", "w_v", "mlstm_norm_g", "w_out", "norm_ffn_g",
              "w_ffn_gate", "w_ffn_up", "w_ffn_down", "norm_ple_g", "w_ple_gate", "w_ple"]
    shared = {k: f(I[k][0]) for k in wnames}
    shared["final_norm_g"] = I["final_norm_g"]
    in_maps = []
    for c in range(8):
        sl = slice(c * NS, (c + 1) * NS)
        m = dict(shared)
        m["xp"] = f(I["x_prompt"][c])
        m["xs"] = f(I["x_sample"][sl, 0])
        m["st_lconv"] = f(I["state_lru_conv"][0, sl])
        m["st_lh"] = f(I["state_lru_h"][0, sl])
        m["st_mconv"] = f(I["state_mlstm_conv"][0, sl])
        m["st_C"] = f(I["state_mlstm_C"][0, sl])
        m["st_n"] = f(I["state_mlstm_n"][0, sl])
        m["st_m"] = f(I["state_mlstm_m"][0, sl])
        m["pp"] = f(I["p_prompt"][0, c])
        m["psm"] = f(I["p_sample"][0, sl, 0])
        in_maps.append(m)
    res = run_bass_kernel_spmd(nc, in_maps, core_ids=list(range(8)))
    R = res.results
    g = lambda name: [np.asarray(R[c][name], dtype=np.float32) for c in range(8)]
    y_prompt = np.stack(g("y_p"), 0)
    y_sample = np.concatenate(g("y_s"), 0)[:, None, :]
    p_lconv = np.stack(g("o_plconv"), 0)[None]
    p_lh = np.concatenate(g("o_plh"), 0)[None]
    p_mconv = np.stack(g("o_pmconv"), 0)[None]
    p_C = np.stack(g("o_pC"), 0)[None]
    p_n = np.stack(g("o_pn"), 0)[None]
    p_m = np.concatenate(g("o_pm"), 0)[None]
    s_lconv = np.concatenate(g("o_slconv"), 0)[None]
    s_lh = np.concatenate(g("o_slh"), 0)[None]
    s_mconv = np.concatenate(g("o_smconv"), 0)[None]
    s_C = np.concatenate(g("o_sC"), 0)[None]
    s_n = np.concatenate(g("o_sn"), 0)[None]
    s_m = np.concatenate(g("o_sm"), 0)[None]
    return (y_prompt, y_sample, p_lconv, p_lh, p_mconv, p_C, p_n, p_m, s_lconv, s_lh, s_mconv, s_C, s_n, s_m)
```
